# Optimizing a Trainium2 kernel written in Bass

```python
import jax, jax.numpy as jnp
from jax import lax
import numpy as np

D_MODEL = 1024
BATCH = 8
SEQ = 2048
DEPTH = 2
DEC_BATCH = 128
DEC_SEQ = 1
PAST_LEN = 16384
PAGE_SIZE = 128

N_MEM = 256
D_A = D_MODEL
D_CONF = D_MODEL
D_POOL = D_MODEL
N_POOL_GROUPS = 4
POOL_GROUP = D_POOL // N_POOL_GROUPS
POOL_WINDOWS = (2, 4, 8, 16)
POOL_BUF = max(POOL_WINDOWS) - 1
N_MEM_HEADS = 4
MEM_HEAD_DIM = D_MODEL // N_MEM_HEADS
D_MEM = N_MEM_HEADS * MEM_HEAD_DIM
CONV_A_WIDTH = 3
CONV_B_WIDTH = 31
N_BRANCHES = 4
D_FF = 4 * D_MODEL
PROJ_SIZES = (D_A, D_A, D_A, 2 * D_CONF, D_POOL, D_MEM, N_BRANCHES * D_MODEL)
D_PROJ = D_A * 3 + 2 * D_CONF + D_POOL + D_MEM + N_BRANCHES * D_MODEL
EPS = 1e-6

kernel_name = "hybrid_gated_conv_pool_memattn_decoder_step"


def _split_points():
    pts, acc = [], 0
    for s in PROJ_SIZES[:-1]:
        acc += s
        pts.append(acc)
    return pts


def _rmsnorm(x, g):
    xf = x.astype(jnp.float32)
    y = xf * lax.rsqrt(jnp.mean(xf * xf, axis=-1, keepdims=True) + EPS)
    return (y * g.astype(jnp.float32)).astype(x.dtype)


def _layernorm(x, g, b):
    xf = x.astype(jnp.float32)
    mu = jnp.mean(xf, axis=-1, keepdims=True)
    xc = xf - mu
    var = jnp.mean(xc * xc, axis=-1, keepdims=True)
    y = xc * lax.rsqrt(var + EPS) * g.astype(jnp.float32) + b.astype(jnp.float32)
    return y.astype(x.dtype)


def _causal_dwconv(ext, w):
    return lax.conv_general_dilated(
        ext, w[:, None, :].astype(ext.dtype), window_strides=(1,), padding="VALID",
        dimension_numbers=("NWC", "WIO", "NWC"), feature_group_count=ext.shape[-1])


def _multiscale_pool(ext, start_pos):
    b, n, c = ext.shape
    L = n - POOL_BUF
    cs = jnp.cumsum(ext.astype(jnp.float32), axis=1)
    cs0 = jnp.concatenate([jnp.zeros((b, 1, c), jnp.float32), cs], axis=1)
    pos = start_pos + jnp.arange(L)
    outs = []
    for g, w in enumerate(POOL_WINDOWS):
        lo, hi = g * POOL_GROUP, (g + 1) * POOL_GROUP
        s = cs0[:, POOL_BUF + 1:, lo:hi] - cs0[:, POOL_BUF + 1 - w:POOL_BUF + 1 - w + L, lo:hi]
        cnt = jnp.minimum(pos + 1, w).astype(jnp.float32)[None, :, None]
        outs.append(s / cnt)
    mean = jnp.concatenate(outs, axis=-1)
    return (mean - ext[:, POOL_BUF:].astype(jnp.float32)).astype(ext.dtype)


def _mixing_layer(xn, mem_k, mem_v, buf_a, buf_b, buf_pool, start_pos, w_in, conv_a_w,
                  conv_b_w, conv_b_bias, ln_b_gain, ln_b_bias, pool_w, pool_scale,
                  gate_bias, w_o):
    b, L = xn.shape[0], xn.shape[1]
    proj = jnp.einsum("bsd,de->bse", xn, w_in)
    h_a, b_a, c_a, glu_in, p_in, q, gate_logits = jnp.split(proj, _split_points(), axis=-1)
    ext_a = jnp.concatenate([buf_a, c_a * h_a], axis=1)
    y_a = b_a * _causal_dwconv(ext_a, conv_a_w)
    glu = glu_in[..., :D_CONF] * jax.nn.sigmoid(glu_in[..., D_CONF:])
    ext_b = jnp.concatenate([buf_b, glu], axis=1)
    z = _causal_dwconv(ext_b, conv_b_w) + conv_b_bias
    y_b = jax.nn.silu(_layernorm(z, ln_b_gain, ln_b_bias))
    ext_p = jnp.concatenate([buf_pool, p_in], axis=1)
    pooled = _multiscale_pool(ext_p, start_pos).reshape(b, L, N_POOL_GROUPS, POOL_GROUP)
    y_c = jnp.einsum("bsgc,gcd->bsgd", pooled, pool_w).reshape(b, L, D_POOL) * pool_scale
    qh = q.reshape(b, L, N_MEM_HEADS, MEM_HEAD_DIM)
    s = jnp.einsum("bqhd,bkhd->bhqk", qh, mem_k).astype(jnp.float32) * (MEM_HEAD_DIM ** -0.5)
    pr = jax.nn.softmax(s, axis=-1).astype(mem_v.dtype)
    y_m = jnp.einsum("bhqk,bkhd->bqhd", pr, mem_v).reshape(b, L, D_MEM)
    g = jax.nn.sigmoid(gate_logits + gate_bias).reshape(b, L, N_BRANCHES, D_MODEL)
    merged = g[:, :, 0] * y_a + g[:, :, 1] * y_b + g[:, :, 2] * y_c + g[:, :, 3] * y_m
    out = jnp.einsum("bsd,de->bse", merged, w_o)
    return (out, ext_a[:, -(CONV_A_WIDTH - 1):], ext_b[:, -(CONV_B_WIDTH - 1):],
            ext_p[:, -POOL_BUF:])


def _trunk(x, mem_k, mem_v, buf_a, buf_b, buf_pool, start_pos, norm_mix, w_in, conv_a_w,
           conv_b_w, conv_b_bias, ln_b_gain, ln_b_bias, pool_w, pool_scale, gate_bias, w_o,
           norm_ffn, w_ff1, w_ff2, norm_final):
    new_a, new_b, new_p = [], [], []
    for l in range(DEPTH):
        xn = _rmsnorm(x, norm_mix[l])
        mix, na, nb, npool = _mixing_layer(
            xn, mem_k[l], mem_v[l], buf_a[l], buf_b[l], buf_pool[l], start_pos, w_in[l],
            conv_a_w[l], conv_b_w[l], conv_b_bias[l], ln_b_gain[l], ln_b_bias[l], pool_w[l],
            pool_scale[l], gate_bias[l], w_o[l])
        x = x + mix
        xn = _rmsnorm(x, norm_ffn[l])
        h = jnp.square(jax.nn.relu(jnp.einsum("bsd,df->bsf", xn, w_ff1[l])))
        x = x + jnp.einsum("bsf,fd->bsd", h, w_ff2[l])
        new_a.append(na)
        new_b.append(nb)
        new_p.append(npool)
    return (_rmsnorm(x, norm_final), jnp.stack(new_a), jnp.stack(new_b), jnp.stack(new_p))


def setup_inputs(seed: int = 0) -> dict:
    key = jax.random.key(seed)
    ks = jax.random.split(key, 32)
    f32 = jnp.float32
    nrm = lambda k, shape, scale=1.0: (jax.random.normal(k, shape, f32) * scale)
    gain = lambda k, shape: 1.0 + 0.05 * jax.random.normal(k, shape, f32)
    return {
        "x_prompt": nrm(ks[0], (BATCH, SEQ, D_MODEL)),
        "x_sample": nrm(ks[1], (DEC_BATCH, DEC_SEQ, D_MODEL)),
        "mem_prompt": nrm(ks[2], (BATCH, N_MEM, D_MODEL)),
        "cache_mem_k": nrm(ks[3], (DEPTH, DEC_BATCH, N_MEM, N_MEM_HEADS, MEM_HEAD_DIM)),
        "cache_mem_v": nrm(ks[4], (DEPTH, DEC_BATCH, N_MEM, N_MEM_HEADS, MEM_HEAD_DIM)),
        "state_conv_a": nrm(ks[5], (DEPTH, DEC_BATCH, CONV_A_WIDTH - 1, D_A)),
        "state_conv_b": nrm(ks[6], (DEPTH, DEC_BATCH, CONV_B_WIDTH - 1, D_CONF), 0.5),
        "state_pool": nrm(ks[7], (DEPTH, DEC_BATCH, POOL_BUF, D_POOL)),
        "norm_mix": gain(ks[8], (DEPTH, D_MODEL)),
        "norm_mem": gain(ks[9], (DEPTH, D_MODEL)),
        "w_kv": nrm(ks[10], (DEPTH, D_MODEL, 2 * D_MEM), D_MODEL ** -0.5),
        "w_in": nrm(ks[11], (DEPTH, D_MODEL, D_PROJ), D_MODEL ** -0.5),
        "conv_a_w": nrm(ks[12], (DEPTH, CONV_A_WIDTH, D_A), CONV_A_WIDTH ** -0.5),
        "conv_b_w": nrm(ks[13], (DEPTH, CONV_B_WIDTH, D_CONF), CONV_B_WIDTH ** -0.5),
        "conv_b_bias": nrm(ks[14], (DEPTH, D_CONF), 0.02),
        "ln_b_gain": gain(ks[15], (DEPTH, D_CONF)),
        "ln_b_bias": nrm(ks[16], (DEPTH, D_CONF), 0.02),
        "pool_w": nrm(ks[17], (DEPTH, N_POOL_GROUPS, POOL_GROUP, POOL_GROUP), POOL_GROUP ** -0.5),
        "pool_scale": gain(ks[18], (DEPTH, D_POOL)),
        "gate_bias": nrm(ks[19], (DEPTH, N_BRANCHES * D_MODEL), 0.02),
        "w_o": nrm(ks[20], (DEPTH, D_MODEL, D_MODEL), D_MODEL ** -0.5),
        "norm_ffn": gain(ks[21], (DEPTH, D_MODEL)),
        "w_ff1": nrm(ks[22], (DEPTH, D_MODEL, D_FF), D_MODEL ** -0.5),
        "w_ff2": nrm(ks[23], (DEPTH, D_FF, D_MODEL), D_FF ** -0.5),
        "norm_final": gain(ks[24], (D_MODEL,)),
    }


def reference(x_prompt, x_sample, mem_prompt, cache_mem_k, cache_mem_v, state_conv_a,
              state_conv_b, state_pool, norm_mix, norm_mem, w_kv, w_in, conv_a_w, conv_b_w,
              conv_b_bias, ln_b_gain, ln_b_bias, pool_w, pool_scale, gate_bias, w_o,
              norm_ffn, w_ff1, w_ff2, norm_final):
    dt = x_prompt.dtype
    ks, vs = [], []
    for l in range(DEPTH):
        memn = _rmsnorm(mem_prompt, norm_mem[l])
        kv = jnp.einsum("bmd,de->bme", memn, w_kv[l])
        ks.append(kv[..., :D_MEM].reshape(BATCH, N_MEM, N_MEM_HEADS, MEM_HEAD_DIM))
        vs.append(kv[..., D_MEM:].reshape(BATCH, N_MEM, N_MEM_HEADS, MEM_HEAD_DIM))
    mem_k_prompt = jnp.stack(ks)
    mem_v_prompt = jnp.stack(vs)
    weights = (norm_mix, w_in, conv_a_w, conv_b_w, conv_b_bias, ln_b_gain, ln_b_bias, pool_w,
               pool_scale, gate_bias, w_o, norm_ffn, w_ff1, w_ff2, norm_final)
    zeros_a = jnp.zeros((DEPTH, BATCH, CONV_A_WIDTH - 1, D_A), dt)
    zeros_b = jnp.zeros((DEPTH, BATCH, CONV_B_WIDTH - 1, D_CONF), dt)
    zeros_p = jnp.zeros((DEPTH, BATCH, POOL_BUF, D_POOL), dt)
    y_prompt, conv_a_prompt, conv_b_prompt, pool_prompt = _trunk(
        x_prompt, mem_k_prompt, mem_v_prompt, zeros_a, zeros_b, zeros_p, 0, *weights)
    y_sample, conv_a_sample, conv_b_sample, pool_sample = _trunk(
        x_sample, cache_mem_k, cache_mem_v, state_conv_a, state_conv_b, state_pool, PAST_LEN,
        *weights)
    return (y_prompt, y_sample, mem_k_prompt, mem_v_prompt, conv_a_prompt, conv_b_prompt,
            pool_prompt, conv_a_sample, conv_b_sample, pool_sample)
```

```python
import contextlib
import numpy as np
import concourse.bass as bass
import concourse.mybir as mybir
from concourse.bass_utils import run_bass_kernel_spmd

F32 = mybir.dt.float32
BF16 = mybir.dt.bfloat16
AF = mybir.ActivationFunctionType
ALU = mybir.AluOpType
AX = mybir.AxisListType

D = 1024
SEQ = 2048
NT = 512
NS = 16
NMEM = 256
DPROJ = 11264
DFF = 4096
EPS = 1e-6
WINS = (2, 4, 8, 16)
RING = 7
SLOTW = 1024

C_H, C_B, C_C, C_GA, C_GB, C_P, C_Q, C_G = 0, 1024, 2048, 3072, 4096, 5120, 6144, 7168

PRM = {}
_o = 0
for _n, _w in (("nmix", 16), ("nmem", 16), ("nffn", 16), ("nfin", 8), ("caw", 48), ("cbw", 496),
               ("cbb", 16), ("lng", 16), ("lnb", 16), ("psc", 16), ("gb", 64)):
    PRM[_n] = _o
    _o += _w
NPRM = _o


class Buf:
    __slots__ = ("name", "writer", "readers")

    def __init__(self, name=""):
        self.name = name
        self.writer = None
        self.readers = {}


class Op:
    __slots__ = ("eng", "fn", "chan", "chan_val", "pos", "signal", "count", "waits")

    def __init__(self, eng, fn, chan):
        self.eng = eng
        self.fn = fn
        self.chan = chan
        self.chan_val = 0
        self.pos = 0
        self.signal = False
        self.count = 0
        self.waits = []


ENGS = ("pe", "act", "dve", "pool", "sp")


class Sched:
    def __init__(self, dry):
        self.dry = dry
        self.ops = {e: [] for e in ENGS}
        self.known = {e: {} for e in ENGS}
        self.chan_n = {}
        self.last_compute = {}
        self.last_dma = {}

    def _add_deps(self, o, deps):
        eng = o.eng
        for d in deps:
            key = ("c", d.chan) if d.chan else ("e", d.eng)
            val = d.chan_val if d.chan else d.pos
            if self.known[eng].get(key, -1) >= val:
                continue
            self.known[eng][key] = val
            d.signal = True
            o.waits.append(d)

    def op(self, eng, fn, reads=(), writes=(), chan=None):
        if self.dry:
            return None
        o = Op(eng, fn, chan)
        o.pos = len(self.ops[eng])
        raw = []
        other = []
        for b in reads:
            if b.writer is not None:
                raw.append(b.writer)
        for b in writes:
            if b.writer is not None:
                other.append(b.writer)
            other.extend(b.readers.values())
        deps = []
        seen = set()
        for lst, is_raw in ((raw, True), (other, False)):
            for d in lst:
                if id(d) in seen:
                    continue
                if d.chan is None and chan is None and d.eng == eng:
                    if eng == "pe":
                        continue
                seen.add(id(d))
                deps.append(d)
        self._add_deps(o, deps)
        if chan is not None:
            n = self.chan_n.get(chan, 0) + 1
            self.chan_n[chan] = n
            o.chan_val = 16 * n
            self.last_dma[chan] = o
        else:
            self.last_compute[eng] = o
        rkey = eng if chan is None else (chan, o.chan_val)
        for b in reads:
            b.readers[rkey] = o
        for b in writes:
            b.writer = o
            b.readers = {}
        self.ops[eng].append(o)
        return o

    def barrier(self):
        if self.dry:
            return
        deps = list(self.last_compute.values()) + list(self.last_dma.values())
        for e in ENGS:
            o = Op(e, None, None)
            o.pos = len(self.ops[e])
            self._add_deps(o, [d for d in deps if not (d.chan is None and d.eng == e)])
            self.ops[e].append(o)

    def finalize(self):
        for e in ENGS:
            c = 0
            for o in self.ops[e]:
                if o.chan is None and o.signal and o.fn is not None:
                    c += 1
                o.count = c


class Arena:
    def __init__(self, tensor, base, limit):
        self.t = tensor
        self.off = base
        self.limit = limit

    def alloc(self, shape, dtype):
        n = 1
        for s in shape[1:]:
            n *= s
        words = n if dtype == F32 else (n + 1) // 2
        words = (words + 7) // 8 * 8
        a = self.off
        self.off += words
        assert self.off <= self.limit, ("SBUF arena overflow", self.off, self.limit)
        ap = self.t[:, a:a + words]
        if dtype == BF16:
            ap = ap.bitcast(BF16)
        ap = ap[:, 0:n]
        if len(shape) == 3:
            ap = ap.rearrange("p (a b) -> p a b", a=shape[1])
        elif len(shape) == 4:
            ap = ap.rearrange("p (a b c) -> p a b c", a=shape[1], b=shape[2])
        return ap


class Rot:
    def __init__(self, items):
        self.items = items
        self.i = 0

    def next(self):
        r = self.items[self.i % len(self.items)]
        self.i += 1
        return r


class WStream:
    def __init__(self, S, slots, recorded):
        self.S = S
        self.slots = slots
        self.rec = recorded
        self.log = []
        self.issued = 0
        self.i = 0

    def _issue(self, n):
        src = self.rec[n]
        ap, buf = self.slots[n % RING]
        dst = self._view(ap, src.shape)
        self.S.op("pool", (lambda e, dst=dst, src=src: e.dma_start(out=dst, in_=src)),
                  reads=(), writes=(buf,), chan="w%d" % (n % RING))

    @staticmethod
    def _view(ap, shape):
        return ap.rearrange("p (a b) -> p a b", a=shape[1])

    def next(self, src):
        if self.rec is None:
            self.log.append(src)
            ap, buf = self.slots[len(self.log) % RING]
            return self._view(ap, src.shape), buf
        i = self.i
        self.i += 1
        while self.issued < min(len(self.rec), i + RING - 2):
            self._issue(self.issued)
            self.issued += 1
        ap, buf = self.slots[i % RING]
        return self._view(ap, src.shape), buf


def build_program():
    nc = bass.Bass("TRN2", target_bir_lowering=False)

    def din(name, shape):
        return nc.dram_tensor(name, list(shape), F32, kind="ExternalInput").ap()

    def dout(name, shape):
        return nc.dram_tensor(name, list(shape), F32, kind="ExternalOutput").ap()

    dr = dict(
        xT=din("xT", (128, 8, SEQ)), xsT=din("xsT", (128, 8, NS)), memT=din("memT", (128, 8, NMEM)),
        kT=din("kT", (2, NS, 128, 8, 256)), vv=din("vv", (2, NS, 128, 2, 1024)),
        sa=din("sa", (2, 128, 8, NS, 2)), sb=din("sb", (2, 128, 8, NS, 30)), sp=din("sp", (2, 128, 8, NS, 15)),
        w_in=din("w_in", (2, 128, 8, DPROJ)), w_kv=din("w_kv", (2, 128, 8, 2048)),
        w_o=din("w_o", (2, 128, 8, D)), w_ff1=din("w_ff1", (2, 128, 8, DFF)), w_ff2=din("w_ff2", (2, 128, 32, D)),
        pw=din("pw", (128, 16, 256)), prm=din("prm", (128, NPRM)), ident=din("ident", (128, 128)),
        invc=din("invc", (128, 4, 16)),
        yT=dout("yT", (128, 8, SEQ)), ysT=dout("ysT", (128, 8, NS)),
        mk=dout("mk", (2, 128, 8, NMEM)), mv=dout("mv", (2, NMEM, D)),
        cap=dout("cap", (2, 128, 8, 2)), cbp=dout("cbp", (2, 128, 8, 30)), pp=dout("pp", (2, 128, 8, 15)),
        cas=dout("cas", (2, 128, 8, NS, 2)), cbs=dout("cbs", (2, 128, 8, NS, 30)), pps=dout("pps", (2, 128, 8, NS, 15)),
    )

    import os
    ARENA_WORDS = int(os.environ.get("ARENA_WORDS", "51200"))
    with contextlib.ExitStack() as st:
        arena_t = st.enter_context(nc.sbuf_tensor("arena", [128, ARENA_WORDS], F32))
        psum = [st.enter_context(nc.psum_tensor("ps%d" % i, [128, 512], F32)) for i in range(8)]

        S = None
        rec = None
        for run in ("dry", "real"):
            S = Sched(dry=(run == "dry"))
            W = emit_all(nc, S, dr, arena_t, ARENA_WORDS, psum, rec)
            if run == "dry":
                rec = W.log
        S.finalize()

        eng_sem = {e: st.enter_context(nc.semaphore("sem_" + e)) for e in ENGS}
        chan_sem = {c: st.enter_context(nc.semaphore("ch_" + c)) for c in S.chan_n}

        def emit(e, name):
            for o in S.ops[name]:
                for d in o.waits:
                    if d.chan is not None:
                        e.wait_ge(chan_sem[d.chan], d.chan_val)
                    else:
                        e.wait_ge(eng_sem[d.eng], d.count)
                if o.fn is None:
                    continue
                ins = o.fn(e)
                if o.chan is not None:
                    ins.then_inc(chan_sem[o.chan], 16)
                elif o.signal:
                    ins.then_inc(eng_sem[name], 1)

        with nc.Block() as block:
            @block.tensor
            def _(e):
                emit(e, "pe")

            @block.scalar
            def _(e):
                emit(e, "act")

            @block.vector
            def _(e):
                emit(e, "dve")

            @block.gpsimd
            def _(e):
                emit(e, "pool")

            @block.sync
            def _(e):
                emit(e, "sp")
    return nc


class _Stop(Exception):
    pass


KSTOP = [None]
MARKS = []
DIAG_ENG = "pool"


def emit_all(nc, S, dr, arena_t, ARENA_WORDS, psum, rec):
    holder = {}
    try:
        _emit_all(nc, S, dr, arena_t, ARENA_WORDS, psum, rec, holder)
    except _Stop:
        S.barrier()
    return holder["W"]


def _emit_all(nc, S, dr, arena_t, ARENA_WORDS, psum, rec, holder):
    A = Arena(arena_t, 0, ARENA_WORDS)

    def stop(stage):
        if not S.dry:
            MARKS.append((stage, len(S.ops["pe"])))
        if KSTOP[0] == stage:
            raise _Stop()

    def ACT(out, in_, func, reads, writes, bias=None, scale=None):
        kw = {}
        if bias is not None:
            kw["bias"] = bias
        if scale is not None:
            kw["scale"] = scale
        S.op("act", lambda e: e.activation(out=out, in_=in_, func=func, **kw), reads, writes)

    def STT(eng, out, in0, scalar, in1, op0, op1, reads, writes):
        S.op(eng, lambda e: e.scalar_tensor_tensor(out=out, in0=in0, scalar=scalar, in1=in1, op0=op0, op1=op1),
             reads, writes)

    def TT(eng, out, in0, in1, op, reads, writes):
        S.op(eng, lambda e: e.tensor_tensor(out=out, in0=in0, in1=in1, op=op), reads, writes)

    def TS1(eng, out, in_, scalar, op, reads, writes):
        S.op(eng, lambda e: e.tensor_single_scalar(out=out, in_=in_, scalar=scalar, op=op), reads, writes)

    def TS2(eng, out, in0, s1, s2, op0, op1, reads, writes):
        S.op(eng, lambda e: e.tensor_scalar(out=out, in0=in0, scalar1=s1, scalar2=s2, op0=op0, op1=op1), reads, writes)

    def CP(eng, out, in_, reads, writes):
        S.op(eng, lambda e: e.tensor_copy(out=out, in_=in_), reads, writes)

    def RCP(out, in_, reads, writes):
        S.op("dve", lambda e: e.reciprocal(out=out, in_=in_), reads, writes)

    def RED(out, in_, reads, writes):
        S.op("dve", lambda e: e.tensor_reduce(out=out, in_=in_, axis=AX.X, op=ALU.add), reads, writes)

    def MM(out, lhsT, rhs, start, stop, reads, writes):
        S.op("pe", lambda e: e.matmul(out, lhsT, rhs, start=start, stop=stop), reads, writes)

    def DMA(eng, out, in_, reads, writes, chan):
        S.op(eng, lambda e: e.dma_start(out=out, in_=in_), reads, writes, chan=chan)

    def MEMSET(eng, ap, val, writes):
        S.op(eng, lambda e: e.memset(ap, val), (), writes)

    psb_ = [Buf("psb%d" % i) for i in range(8)]
    mm_rot = Rot([(psum[i], psb_[i]) for i in range(3)])
    aux_rot = Rot([(psum[i], psb_[i]) for i in range(3, 6)])
    mm_rot4 = Rot([(psum[i], psb_[i]) for i in (0, 1, 2, 6)])
    aux_rot4 = Rot([(psum[i], psb_[i]) for i in (3, 4, 5, 7)])
    ps6b, ps7b = psb_[6], psb_[7]
    _sreg = []
    for i in range(12):
        _sreg.append((psum[6][:, 16 * i:16 * i + 16], ps6b))
        _sreg.append((psum[7][:, 320 + 16 * i:320 + 16 * i + 16], ps7b))
    s_rot = Rot(_sreg)

    ring = []
    for i in range(RING):
        ap = A.alloc([128, 2048], BF16)
        ring.append((ap, Buf("slot%d" % i)))
    W = WStream(S, ring, rec)
    holder["W"] = W

    prm = A.alloc([128, NPRM], F32)
    hgb = A.alloc([128, 64], F32)
    ident = A.alloc([128, 128], BF16)
    ones = A.alloc([128, 128], BF16)
    epsT = A.alloc([128, 1], F32)
    invc = A.alloc([128, 4, 16], F32)
    pw = A.alloc([128, 8, 256], BF16)
    pwb = Buf("pw")
    KT = [A.alloc([128, 8, 256], BF16) for _ in range(2)]
    VV = [A.alloc([128, 2, 1024], BF16) for _ in range(2)]
    halo_a = [A.alloc([128, 8, 2], F32) for _ in range(2)]
    halo_b = [A.alloc([128, 8, 30], BF16) for _ in range(2)]
    halo_p = [A.alloc([128, 8, 15], F32) for _ in range(2)]
    cbp_r = [A.alloc([128, 8, 30], F32) for _ in range(2)]
    halo_ab = [[Buf() for _ in range(8)] for _ in range(2)]
    halo_bb = [[Buf() for _ in range(8)] for _ in range(2)]
    halo_pb = [[Buf() for _ in range(8)] for _ in range(2)]
    cbp_b = [Buf() for _ in range(2)]
    KTb = [Buf(), Buf()]
    VVb = [Buf(), Buf()]
    cst = Buf("const")
    PHASE_BASE = A.off

    def P(name, idx, width=1):
        o = PRM[name] + idx
        return prm[:, o:o + width]

    prmb = Buf("prm")
    DMA("sp", prm, dr["prm"], (), (prmb,), "setup")
    DMA("sp", invc, dr["invc"], (), (Buf(),), "setup2")
    DMA("pool", ident, dr["ident"], (), (Buf(),), "setup3")
    MEMSET("dve", ones, 1.0, (Buf(),))
    MEMSET("dve", epsT, EPS, (Buf(),))
    for l in range(2):
        MEMSET("dve", halo_a[l], 0.0, halo_ab[l])
        MEMSET("dve", halo_b[l], 0.0, halo_bb[l])
        MEMSET("dve", halo_p[l], 0.0, halo_pb[l])
    TS1("dve", hgb, prm[:, PRM["gb"]:PRM["gb"] + 64], 0.5, ALU.mult, (prmb,), (Buf(),))
    S.barrier()
    stop("setup")

    def rmsnorm(x, xb, N, gain_name, gain_idx, out, outb, sqrot, sd, sdb, rstd, rstdb, rot=None):
        ps, pb = (rot or aux_rot).next()
        for c in range(8):
            sq, sqb = sqrot.next()
            ACT(sq[:, :N], x[:, c, :N], AF.Square, (xb[c],), (sqb,))
            MM(ps[:, :N], ones, sq[:, :N], c == 0, c == 7, (sqb,), (pb,))
        ACT(sd[:, :N], ps[:, :N], AF.Sqrt, (pb,), (sdb,), bias=epsT[:, 0:1], scale=1.0 / D)
        RCP(rstd[:, :N], sd[:, :N], (sdb,), (rstdb,))
        for c in range(8):
            STT("dve", out[:, c, :N], x[:, c, :N], P(gain_name, gain_idx + c), rstd[:, :N], ALU.mult, ALU.mult,
                (xb[c], rstdb), (outb[c],))

    A.off = PHASE_BASE
    mem = A.alloc([128, 8, NMEM], F32)
    memb = [Buf() for _ in range(8)]
    memn = A.alloc([128, 8, NMEM], BF16)
    memnb = [Buf() for _ in range(8)]
    kst = A.alloc([128, 8, NMEM], F32)
    kstb = Buf()
    vst = A.alloc([128, 2, 1024], F32)
    vstb = Buf()
    sqr = Rot([(A.alloc([128, 512], BF16), Buf()) for _ in range(2)])
    sd0 = A.alloc([128, 512], F32)
    rs0 = A.alloc([128, 512], F32)
    sd0b, rs0b = Buf(), Buf()
    DMA("sp", mem, dr["memT"], (), memb, "mem")
    for l in range(2):
        stop("kv_load")
        rmsnorm(mem, memb, NMEM, "nmem", l * 8, memn, memnb, sqr, sd0, sd0b, rs0, rs0b)
        stop("kv_norm")
        for e2 in range(4):
            wk, wkb = W.next(dr["w_kv"][l][:, :, 256 * e2:256 * e2 + 256])
            for jj in range(2):
                e_ = 2 * e2 + jj
                ps, pb = mm_rot.next()
                for k in range(8):
                    MM(ps[:, :NMEM], wk[:, k, jj * 128:(jj + 1) * 128], memn[:, k, :], k == 0, k == 7,
                       (wkb, memnb[k]), (pb,))
                ACT(kst[:, e_, :], ps[:, :NMEM], AF.Copy, (pb,), (kstb,))
                CP("dve", KT[l][:, e_, :], kst[:, e_, :], (kstb,), (KTb[l],))
        stop("kv_k")
        DMA("sp", dr["mk"][l], kst, (kstb,), (), "mk")
        stop("kv_mk")
        for s in range(4):
            wv, wvb = W.next(dr["w_kv"][l][:, :, 1024 + 256 * s:1024 + 256 * s + 256])
            for tc in range(2):
                ps, pb = mm_rot.next()
                for k in range(8):
                    MM(ps[:, :256], memn[:, k, tc * 128:(tc + 1) * 128], wv[:, k, :], k == 0, k == 7,
                       (wvb, memnb[k]), (pb,))
                ACT(vst[:, tc, 256 * s:256 * s + 256], ps[:, :256], AF.Copy, (pb,), (vstb,))
                CP("dve", VV[l][:, tc, 256 * s:256 * s + 256], vst[:, tc, 256 * s:256 * s + 256], (vstb,), (VVb[l],))
        stop("kv_v")
        DMA("sp", dr["mv"][l].rearrange("(tc p) e -> p tc e", p=128), vst, (vstb,), (), "mv")
        stop("kv_mv")
    S.barrier()
    stop("kv")

    def make_ctx(N, sample):
        T = {}
        T["N"] = N
        T["sample"] = sample
        T["mmrot"] = s_rot if sample else mm_rot
        T["auxrot"] = s_rot if sample else aux_rot
        NM = N
        T["x"] = A.alloc([128, 8, NM], F32)
        T["xb"] = [Buf() for _ in range(8)]
        T["xn"] = A.alloc([128, 8, NM], BF16)
        T["xnb"] = [Buf() for _ in range(8)]
        T["z"] = A.alloc([128, 8, NM], F32)
        T["zb"] = [Buf() for _ in range(8)]
        T["m2"] = A.alloc([128, 8, NM], F32)
        T["m2b"] = [Buf() for _ in range(8)]
        T["mb"] = A.alloc([128, 8, NM], BF16)
        T["mbb"] = [Buf() for _ in range(8)]
        T["h"] = T["z"].rearrange("p a b -> p (a b)").bitcast(BF16).rearrange("p (a b) -> p a b", a=16)
        T["hb"] = [T["zb"][i // 2] for i in range(16)]
        T["tmp"] = Rot([(A.alloc([128, NM], F32), Buf()) for _ in range(5)])
        T["bft"] = Rot([(A.alloc([128, NM], BF16), Buf()) for _ in range(3)])
        T["sq"] = Rot([(A.alloc([128, NM], BF16), Buf()) for _ in range(2)])
        for grp in (("rstd", "lnrstd", "rden"), ("sd", "lnnmr")):
            ap_, b_ = A.alloc([128, NM], F32), Buf()
            for nm in grp:
                T[nm] = ap_
                T[nm + "b"] = b_
        T["c1"] = [(A.alloc([128, NM], F32), Buf()) for _ in range(2)]
        T["plbf"] = [(A.alloc([128, NM], BF16), Buf()) for _ in range(2)]
        T["qbf"] = [(A.alloc([128, NM], BF16), Buf()) for _ in range(2)]
        T["ebf"] = [(A.alloc([128, NM], BF16), Buf()) for _ in range(2)]
        if not sample:
            T["glu"] = Rot([(A.alloc([128, 30 + NM], BF16), Buf()) for _ in range(2)])
            T["diag"] = [(A.alloc([128, 31, 128], BF16), Buf()) for _ in range(2)]
            T["ub"] = Rot([(A.alloc([128, 2 + NM], F32), Buf()) for _ in range(2)])
            T["pb"] = Rot([(A.alloc([128, 15 + NM], F32), Buf()) for _ in range(2)])
            T["sA"] = (A.alloc([128, 16 + NM], F32), Buf())
            T["sB"] = (A.alloc([128, 16 + NM], F32), Buf())
        else:
            T["sast"] = A.alloc([128, 8, NS, 2], F32)
            T["nsa"] = A.alloc([128, 8, NS, 2], F32)
            T["sastb"], T["nsab"] = Buf(), Buf()
            T["sb_in"] = [(A.alloc([128, NS, 30], F32), Buf(), "i_sb%d" % i) for i in range(2)]
            T["sb_out"] = [(A.alloc([128, NS, 30], F32), Buf(), "o_sb%d" % i) for i in range(2)]
            T["sp_in"] = [(A.alloc([128, NS, 15], F32), Buf(), "i_sp%d" % i) for i in range(2)]
            T["sp_out"] = [(A.alloc([128, NS, 15], F32), Buf(), "o_sp%d" % i) for i in range(2)]
            T["qall"] = (A.alloc([128, 8, NS], BF16), Buf())
            T["t3all"] = (A.alloc([128, 8, NS], F32), Buf())
            T["yms"] = (A.alloc([128, 8, NS], F32), Buf())
            T["E"] = (A.alloc([128, 128], BF16), Buf())
            T["rdens"] = (A.alloc([128, 64], F32), Buf())
        return T

    def proj(T, wt, wtb, jj, src, srcb):
        N = T["N"]
        ps, pb = T["mmrot"].next()
        for k in range(8):
            MM(ps[:, :N], wt[:, k, jj * 128:(jj + 1) * 128], src[:, k, :N], k == 0, k == 7, (wtb, srcb[k]), (pb,))
        return ps, pb

    def sample_attention(T, l):
        N = T["N"]
        qall, qallb = T["qall"]
        t3all, t3allb = T["t3all"]
        yms, ymsb = T["yms"]
        E, Eb = T["E"]
        pssc, psscb = psum[7][:, 0:128], ps7b
        for b in range(NS):
            kt, ktb = yield ("kv", "k%d_%d" % (l, b), dr["kT"][l, b])
            for g in range(4):
                for kc in range(2):
                    col = b * 8 + g * 2 + kc
                    for dc in range(2):
                        MM(pssc[:, col:col + 1], kt[:, 2 * g + dc, kc * 128:(kc + 1) * 128],
                           qall[:, 2 * g + dc, b:b + 1], dc == 0, dc == 1, (ktb, qallb), (psscb,))
        ACT(E, pssc[:, 0:128], AF.Exp, (psscb,), (Eb,), scale=1.0 / 16.0)
        psden, psdenb = psum[7][:, 128:192], ps7b
        for kc in range(2):
            MM(psden[:, 0:64], ones, E[:, kc:128:2], kc == 0, kc == 1, (Eb,), (psdenb,))
        rdens, rdensb = T["rdens"]
        RCP(rdens, psden[:, 0:64], (psdenb,), (rdensb,))
        pso, psob = psum[7][:, 192:320], ps7b
        for b in range(NS):
            vt, vtb = yield ("kv", "v%d_%d" % (l, b), dr["vv"][l, b])
            for j in range(8):
                g = j // 2
                for kc in range(2):
                    ec = b * 8 + g * 2 + kc
                    MM(pso[:, b * 8 + j:b * 8 + j + 1], vt[:, kc, j * 128:(j + 1) * 128], E[:, ec:ec + 1],
                       kc == 0, kc == 1, (vtb, Eb), (psob,))
        for j in range(8):
            g = j // 2
            STT("dve", yms[:, j, :], t3all[:, j, :], 1.0, pso[:, j:128:8], ALU.add, ALU.mult, (t3allb, psob), (ymsb,))
            TT("dve", yms[:, j, :], yms[:, j, :], rdens[:, g:64:4], ALU.mult, (ymsb, rdensb), (ymsb,))

    def layer(T, l, ti):
        N = T["N"]
        sample = T["sample"]
        x, xb, xn, xnb = T["x"], T["xb"], T["xn"], T["xnb"]
        z, zb, m2, m2b, mb, mbb, h, hb = T["z"], T["zb"], T["m2"], T["m2b"], T["mb"], T["mbb"], T["h"], T["hb"]
        tmp = T["tmp"]
        win = dr["w_in"][l]

        def wreq(c0):
            return ("w", "in%d_%d" % (l, c0), win[:, :, c0:c0 + 256])

        def gate_tanh(psg, pgb, gi, j):
            t, tb_ = tmp.next()
            ACT(t[:, :N], psg[:, :N], AF.Tanh, (pgb,), (tb_,), bias=hgb[:, l * 32 + gi * 8 + j:l * 32 + gi * 8 + j + 1],
                scale=0.5)
            return t, tb_

        rmsnorm(x, xb, N, "nmix", l * 8, xn, xnb, T["sq"], T["sd"], T["sdb"], T["rstd"], T["rstdb"], rot=T["auxrot"])

        if not sample:
            t3stash = [T["c1"], T["ub"].items]

        def m_proj(g, wq, wqb, wg3, wg3b):
            for jj in range(2):
                j = 2 * g + jj
                psq_, pqb = proj(T, wq, wqb, jj, xn, xnb)
                if not sample:
                    ACT(T["qbf"][jj][0][:, :N], psq_[:, :N], AF.Copy, (pqb,), (T["qbf"][jj][1],))
                else:
                    ACT(T["qall"][0][:, j, :], psq_[:, :N], AF.Copy, (pqb,), (T["qall"][1],))
            for dc in range(2):
                j = 2 * g + dc
                psg, pgb = proj(T, wg3, wg3b, dc, xn, xnb)
                if not sample:
                    t3, t3b = t3stash[g % 2][dc]
                    ACT(t3[:, :N], psg[:, :N], AF.Tanh, (pgb,), (t3b,),
                        bias=hgb[:, l * 32 + 24 + j:l * 32 + 24 + j + 1], scale=0.5)
                else:
                    ACT(T["t3all"][0][:, j, :], psg[:, :N], AF.Tanh, (pgb,), (T["t3all"][1],),
                        bias=hgb[:, l * 32 + 24 + j:l * 32 + 24 + j + 1], scale=0.5)

        wq, wqb = yield wreq(C_Q)
        wg3, wg3b = yield wreq(C_G + 3072)
        m_proj(0, wq, wqb, wg3, wg3b)
        for g in range(4):
            if not sample:
                for kc in range(2):
                    pss_, pssb_ = T["auxrot"].next()
                    for dc in range(2):
                        MM(pss_[:, :N], KT[l][:, 2 * g + dc, kc * 128:(kc + 1) * 128], T["qbf"][dc][0][:, :N],
                           dc == 0, dc == 1, (KTb[l], T["qbf"][dc][1]), (pssb_,))
                    ACT(T["ebf"][kc][0][:, :N], pss_[:, :N], AF.Exp, (pssb_,), (T["ebf"][kc][1],), scale=1.0 / 16.0)
            if g < 3:
                wq, wqb = yield wreq(C_Q + 256 * (g + 1))
                wg3, wg3b = yield wreq(C_G + 3072 + 256 * (g + 1))
                m_proj(g + 1, wq, wqb, wg3, wg3b)
            if not sample:
                psd, psdb = T["auxrot"].next()
                for kc in range(2):
                    MM(psd[:, :N], ones, T["ebf"][kc][0][:, :N], kc == 0, kc == 1, (T["ebf"][kc][1],), (psdb,))
                RCP(T["rden"][:, :N], psd[:, :N], (psdb,), (T["rdenb"],))
                for dc in range(2):
                    j = 2 * g + dc
                    t3, t3b = t3stash[g % 2][dc]
                    pso, psob = T["auxrot"].next()
                    for kc in range(2):
                        MM(pso[:, :N], VV[l][:, kc, j * 128:(j + 1) * 128], T["ebf"][kc][0][:, :N], kc == 0, kc == 1,
                           (VVb[l], T["ebf"][kc][1]), (psob,))
                    ym, ymb = tmp.next()
                    STT("dve", ym[:, :N], t3[:, :N], 1.0, pso[:, :N], ALU.add, ALU.mult, (t3b, psob), (ymb,))
                    TT("dve", m2[:, j, :N], ym[:, :N], T["rden"][:, :N], ALU.mult, (ymb, T["rdenb"]), (m2b[j],))
        if sample:
            yield ("spawn", sample_attention(T, l))
        stop("M")

        for g in range(4):
            wa, wab = yield wreq(C_GA + 256 * g)
            wb_, wbb = yield wreq(C_GB + 256 * g)
            if sample:
                for jj in range(2):
                    sbin, sbinb, ch = T["sb_in"][jj]
                    DMA("sp", sbin, dr["sb"][l][:, 2 * g + jj], (), (sbinb,), ch)
            stash = []
            for jj in range(2):
                j = 2 * g + jj
                cbw = prm[:, PRM["cbw"] + (l * 8 + j) * 31:PRM["cbw"] + (l * 8 + j) * 31 + 31]
                psa, pab = proj(T, wa, wab, jj, xn, xnb)
                psb, pbb = proj(T, wb_, wbb, jj, xn, xnb)
                tb, tbb = tmp.next()
                ACT(tb[:, :N], psb[:, :N], AF.Tanh, (pbb,), (tbb,), scale=0.5)
                if not sample:
                    gl, glb = T["glu"].next()
                    CP("pool", gl[:, 0:30], halo_b[l][:, j, :], (halo_bb[l][j],), (glb,))
                    STT("dve", gl[:, 30:30 + N], tb[:, :N], 1.0, psa[:, :N], ALU.add, ALU.mult, (tbb, pab), (glb,))
                    if ti == 3:
                        STT("dve", cbp_r[l][:, j, :], tb[:, N - 30:N], 1.0, psa[:, N - 30:N], ALU.add, ALU.mult,
                            (tbb, pab), (cbp_b[l],))
                    dg, dgb = T["diag"][jj]
                    TT(DIAG_ENG, dg, ident.unsqueeze(1).to_broadcast([128, 31, 128]),
                       cbw.unsqueeze(2).to_broadcast([128, 31, 128]), ALU.mult, (), (dgb,))
                    stash.append((gl, glb, dg, dgb))
                else:
                    sbin, sbinb, _ = T["sb_in"][jj]
                    sbo, sbob, cho = T["sb_out"][jj]
                    gs, gsb = tmp.next()
                    STT("dve", gs[:, :N], tb[:, :N], 1.0, psa[:, :N], ALU.add, ALU.mult, (tbb, pab), (gsb,))
                    TS1("dve", gs[:, :N], gs[:, :N], 0.5, ALU.mult, (gsb,), (gsb,))
                    pr, prb = sbo, sbob
                    TT("dve", pr, sbin, cbw[:, 0:30].unsqueeze(1).to_broadcast([128, NS, 30]),
                       ALU.mult, (sbinb,), (prb,))
                    rd, rdb = tmp.next()
                    RED(rd[:, :N], pr, (prb,), (rdb,))
                    STT("dve", rd[:, :N], gs[:, :N], cbw[:, 30:31], rd[:, :N], ALU.mult, ALU.add, (gsb, rdb), (rdb,))
                    ACT(z[:, j, :N], rd[:, :N], AF.Identity, (rdb,), (zb[j],), bias=P("cbb", l * 8 + j), scale=1.0)
                    CP("pool", sbo[:, :, 0:29], sbin[:, :, 1:30], (sbinb,), (sbob,))
                    CP("pool", sbo[:, :, 29], gs[:, :N], (gsb,), (sbob,))
                    DMA("sp", dr["cbs"][l][:, j], sbo, (sbob,), (), cho)
            if not sample:
                for jj in range(2):
                    j = 2 * g + jj
                    gl, glb, dg, dgb = stash[jj]
                    psz, pzb = T["auxrot"].next()
                    for k in range(31):
                        MM(psz[:, :N], dg[:, k, :], gl[:, k:k + N], k == 0, k == 30, (dgb, glb), (pzb,))
                    ACT(z[:, j, :N], psz[:, :N], AF.Identity, (pzb,), (zb[j],), bias=P("cbb", l * 8 + j), scale=0.5)
                    CP("pool", halo_b[l][:, j, :], gl[:, N:N + 30], (glb,), (halo_bb[l][j],))
        stop("B")

        pss, pssb = T["auxrot"].next()
        psq, psqb = T["auxrot"].next()
        for c in range(8):
            z1, z1b = T["bft"].next()
            z2, z2b = T["bft"].next()
            ACT(z1[:, :N], z[:, c, :N], AF.Copy, (zb[c],), (z1b,))
            ACT(z2[:, :N], z[:, c, :N], AF.Square, (zb[c],), (z2b,))
            MM(pss[:, :N], ones, z1[:, :N], c == 0, c == 7, (z1b,), (pssb,))
            MM(psq[:, :N], ones, z2[:, :N], c == 0, c == 7, (z2b,), (psqb,))
        mean, meanb = tmp.next()
        msq, msqb = tmp.next()
        var, varb = tmp.next()
        TS1("dve", mean[:, :N], pss[:, :N], 1.0 / D, ALU.mult, (pssb,), (meanb,))
        TT("dve", msq[:, :N], mean[:, :N], mean[:, :N], ALU.mult, (meanb,), (msqb,))
        STT("dve", var[:, :N], psq[:, :N], 1.0 / D, msq[:, :N], ALU.mult, ALU.subtract, (psqb, msqb), (varb,))
        ACT(var[:, :N], var[:, :N], AF.Sqrt, (varb,), (varb,), bias=epsT[:, 0:1], scale=1.0)
        lr, lrb, ln_, lnb_ = T["lnrstd"], T["lnrstdb"], T["lnnmr"], T["lnnmrb"]
        RCP(lr[:, :N], var[:, :N], (varb,), (lrb,))
        STT("dve", ln_[:, :N], mean[:, :N], -1.0, lr[:, :N], ALU.mult, ALU.mult, (meanb, lrb), (lnb_,))
        for g in range(4):
            wg, wgb = yield wreq(C_G + 1024 + 256 * g)
            for jj in range(2):
                j = 2 * g + jj
                psg, pgb = proj(T, wg, wgb, jj, xn, xnb)
                t1, t1b = T["c1"][jj]
                ACT(t1[:, :N], psg[:, :N], AF.Tanh, (pgb,), (t1b,),
                    bias=hgb[:, l * 32 + 8 + j:l * 32 + 8 + j + 1], scale=0.5)
            for jj in range(2):
                j = 2 * g + jj
                t1, t1b = T["c1"][jj]
                v1, v1b = tmp.next()
                w2, w2b = tmp.next()
                STT("dve", v1[:, :N], z[:, j, :N], P("lng", l * 8 + j), lr[:, :N], ALU.mult, ALU.mult,
                    (zb[j], lrb), (v1b,))
                TS2("dve", w2[:, :N], ln_[:, :N], P("lng", l * 8 + j), P("lnb", l * 8 + j), ALU.mult, ALU.add,
                    (lnb_,), (w2b,))
                TT("dve", v1[:, :N], v1[:, :N], w2[:, :N], ALU.add, (v1b, w2b), (v1b,))
                yb, ybb = tmp.next()
                ACT(yb[:, :N], v1[:, :N], AF.Silu, (v1b,), (ybb,))
                if sample:
                    STT("dve", m2[:, j, :N], t1[:, :N], 1.0, yb[:, :N], ALU.add, ALU.mult, (t1b, ybb), (m2b[j],))
                else:
                    STT("dve", yb[:, :N], t1[:, :N], 1.0, yb[:, :N], ALU.add, ALU.mult, (t1b, ybb), (ybb,))
                    TT("dve", m2[:, j, :N], m2[:, j, :N], yb[:, :N], ALU.add, (m2b[j], ybb), (m2b[j],))
        stop("LN")

        if sample:
            DMA("sp", T["sast"], dr["sa"][l], (), (T["sastb"],), "i_sa")
        for g in range(4):
            wh, whb = yield wreq(C_H + 256 * g)
            wc, wcb = yield wreq(C_C + 256 * g)
            for jj in range(2):
                j = 2 * g + jj
                caw = prm[:, PRM["caw"] + (l * 8 + j) * 3:PRM["caw"] + (l * 8 + j) * 3 + 3]
                psh, phb = proj(T, wh, whb, jj, xn, xnb)
                psc, pcb = proj(T, wc, wcb, jj, xn, xnb)
                hs, hsb = tmp.next()
                ACT(hs[:, :N], psh[:, :N], AF.Copy, (phb,), (hsb,))
                c1, c1b = T["c1"][jj]
                if not sample:
                    ub, ubb = T["ub"].next()
                    CP("pool", ub[:, 0:2], halo_a[l][:, j, :], (halo_ab[l][j],), (ubb,))
                    TT("dve", ub[:, 2:2 + N], psc[:, :N], hs[:, :N], ALU.mult, (pcb, hsb), (ubb,))
                    TS1("dve", c1[:, :N], ub[:, 0:N], caw[:, 0:1], ALU.mult, (ubb,), (c1b,))
                    STT("dve", c1[:, :N], ub[:, 1:N + 1], caw[:, 1:2], c1[:, :N], ALU.mult, ALU.add, (ubb, c1b), (c1b,))
                    STT("dve", c1[:, :N], ub[:, 2:N + 2], caw[:, 2:3], c1[:, :N], ALU.mult, ALU.add, (ubb, c1b), (c1b,))
                    CP("pool", halo_a[l][:, j, :], ub[:, N:N + 2], (ubb,), (halo_ab[l][j],))
                else:
                    us, usb = tmp.next()
                    TT("dve", us[:, :N], psc[:, :N], hs[:, :N], ALU.mult, (pcb, hsb), (usb,))
                    sast = T["sast"]
                    TS1("dve", c1[:, :N], sast[:, j, :, 0], caw[:, 0:1], ALU.mult, (T["sastb"],), (c1b,))
                    STT("dve", c1[:, :N], sast[:, j, :, 1], caw[:, 1:2], c1[:, :N], ALU.mult, ALU.add,
                        (T["sastb"], c1b), (c1b,))
                    STT("dve", c1[:, :N], us[:, :N], caw[:, 2:3], c1[:, :N], ALU.mult, ALU.add, (usb, c1b), (c1b,))
                    CP("pool", T["nsa"][:, j, :, 0], sast[:, j, :, 1], (T["sastb"],), (T["nsab"],))
                    CP("pool", T["nsa"][:, j, :, 1], us[:, :N], (usb,), (T["nsab"],))
            wbw, wbwb = yield wreq(C_B + 256 * g)
            wg0, wg0b = yield wreq(C_G + 256 * g)
            for jj in range(2):
                j = 2 * g + jj
                c1, c1b = T["c1"][jj]
                psb2, pb2b = proj(T, wbw, wbwb, jj, xn, xnb)
                psg, pgb = proj(T, wg0, wg0b, jj, xn, xnb)
                t0, t0b = gate_tanh(psg, pgb, 0, j)
                ya, yab = tmp.next()
                STT("dve", ya[:, :N], t0[:, :N], 1.0, c1[:, :N], ALU.add, ALU.mult, (t0b, c1b), (yab,))
                TT("dve", ya[:, :N], psb2[:, :N], ya[:, :N], ALU.mult, (pb2b, yab), (yab,))
                TT("dve", m2[:, j, :N], m2[:, j, :N], ya[:, :N], ALU.add, (m2b[j], yab), (m2b[j],))
        if sample:
            DMA("sp", dr["cas"][l], T["nsa"], (T["nsab"],), (), "o_sa")
        stop("A")

        for g in range(4):
            wp, wpb = yield wreq(C_P + 256 * g)
            wg2, wg2b = yield wreq(C_G + 2048 + 256 * g)
            w = WINS[g]
            if sample:
                for jj in range(2):
                    spin, spinb, ch = T["sp_in"][jj]
                    DMA("sp", spin, dr["sp"][l][:, 2 * g + jj], (), (spinb,), ch)
            for jj in range(2):
                j = 2 * g + jj
                psp, ppb = proj(T, wp, wpb, jj, xn, xnb)
                pl, plb = T["plbf"][jj]
                if not sample:
                    pbuf, pbb_ = T["pb"].next()
                    CP("pool", pbuf[:, 0:15], halo_p[l][:, j, :], (halo_pb[l][j],), (pbb_,))
                    ACT(pbuf[:, 15:15 + N], psp[:, :N], AF.Copy, (ppb,), (pbb_,))
                    cur, curb, ln, e0 = pbuf, pbb_, 15 + N, 0
                    d = 1
                    nxt = [T["sA"], T["sB"]]
                    ni = 0
                    while 2 * d <= w:
                        dst, dstb = nxt[ni % 2]
                        ni += 1
                        TT("dve", dst[:, 0:ln - d], cur[:, d:ln], cur[:, 0:ln - d], ALU.add, (curb,), (dstb,))
                        cur, curb, ln, e0 = dst, dstb, ln - d, e0 + d
                        d *= 2
                    o = 15 - e0
                    STT("dve", pl[:, :N], cur[:, o:o + N], 1.0 / w, pbuf[:, 15:15 + N], ALU.mult, ALU.subtract,
                        (curb, pbb_), (plb,))
                    if ti == 0:
                        tf, tfb = tmp.next()
                        TT("dve", tf[:, 0:16], cur[:, o:o + 16], invc[:, g, :], ALU.mult, (curb,), (tfb,))
                        TT("dve", pl[:, 0:16], tf[:, 0:16], pbuf[:, 15:31], ALU.subtract, (tfb, pbb_), (plb,))
                    CP("pool", halo_p[l][:, j, :], pbuf[:, N:N + 15], (pbb_,), (halo_pb[l][j],))
                else:
                    spin, spinb, _ = T["sp_in"][jj]
                    spo, spob, cho = T["sp_out"][jj]
                    pn, pnb = tmp.next()
                    ACT(pn[:, :N], psp[:, :N], AF.Copy, (ppb,), (pnb,))
                    rd, rdb = tmp.next()
                    RED(rd[:, :N], spin[:, :, 16 - w:15], (spinb,), (rdb,))
                    TT("dve", rd[:, :N], rd[:, :N], pn[:, :N], ALU.add, (rdb, pnb), (rdb,))
                    STT("dve", pl[:, :N], rd[:, :N], 1.0 / w, pn[:, :N], ALU.mult, ALU.subtract, (rdb, pnb), (plb,))
                    CP("pool", spo[:, :, 0:14], spin[:, :, 1:15], (spinb,), (spob,))
                    CP("pool", spo[:, :, 14], pn[:, :N], (pnb,), (spob,))
                    DMA("sp", dr["pps"][l][:, j], spo, (spob,), (), cho)
            for jj in range(2):
                j = 2 * g + jj
                psg, pgb = proj(T, wg2, wg2b, jj, xn, xnb)
                t2, t2b = T["c1"][jj]
                ACT(t2[:, :N], psg[:, :N], AF.Tanh, (pgb,), (t2b,),
                    bias=hgb[:, l * 32 + 16 + j:l * 32 + 16 + j + 1], scale=0.5)
            for jj in range(2):
                j = 2 * g + jj
                t2, t2b = T["c1"][jj]
                psc2, pc2b = T["auxrot"].next()
                for kc in range(2):
                    MM(psc2[:, :N], pw[:, g * 2 + kc, jj * 128:(jj + 1) * 128], T["plbf"][kc][0][:, :N],
                       kc == 0, kc == 1, (T["plbf"][kc][1], pwb), (pc2b,))
                yc, ycb = tmp.next()
                STT("dve", yc[:, :N], t2[:, :N], 1.0, psc2[:, :N], ALU.add, ALU.mult, (t2b, pc2b), (ycb,))
                if sample:
                    STT("dve", m2[:, j, :N], yc[:, :N], P("psc", l * 8 + j), m2[:, j, :N], ALU.mult, ALU.add,
                        (ycb, m2b[j]), (m2b[j],))
                else:
                    STT("dve", mb[:, j, :N], yc[:, :N], P("psc", l * 8 + j), m2[:, j, :N], ALU.mult, ALU.add,
                        (ycb, m2b[j]), (mbb[j],))
        if sample:
            yield ("join",)
            yms, ymsb = T["yms"]
            for j in range(8):
                TT("dve", mb[:, j, :N], m2[:, j, :N], yms[:, j, :], ALU.add, (m2b[j], ymsb), (mbb[j],))
        stop("C")

        for mi in range(4):
            wo, wob = yield ("w", "o%d_%d" % (l, mi), dr["w_o"][l][:, :, 256 * mi:256 * mi + 256])
            for jj in range(2):
                j = 2 * mi + jj
                ps, pb = proj(T, wo, wob, jj, mb, mbb)
                STT("dve", x[:, j, :N], ps[:, :N], 0.5, x[:, j, :N], ALU.mult, ALU.add, (pb, xb[j]), (xb[j],))
        stop("WO")

        rmsnorm(x, xb, N, "nffn", l * 8, xn, xnb, T["sq"], T["sd"], T["sdb"], T["rstd"], T["rstdb"], rot=T["auxrot"])
        for half in range(2):
            for s8 in range(8):
                s = half * 8 + s8
                w1, w1b = yield ("w", "f1_%d_%d" % (l, s), dr["w_ff1"][l][:, :, 256 * s:256 * s + 256])
                for jj in range(2):
                    ps, pb = proj(T, w1, w1b, jj, xn, xnb)
                    r, rb = T["bft"].next()
                    ACT(r[:, :N], ps[:, :N], AF.Relu, (pb,), (rb,))
                    hi = s8 * 2 + jj
                    TT("dve", h[:, hi, :N], r[:, :N], r[:, :N], ALU.mult, (rb,), (hb[hi],))
            for mi in range(4):
                if sample:
                    for kgl in range(2):
                        r0 = half * 16 + kgl * 8
                        w2_, w2b_ = yield ("w", "f2_%d_%d_%d" % (l, r0, mi),
                                           dr["w_ff2"][l][:, r0:r0 + 8, 256 * mi:256 * mi + 256])
                        for jj in range(2):
                            j = 2 * mi + jj
                            ps, pb = T["mmrot"].next()
                            for k in range(8):
                                MM(ps[:, :N], w2_[:, k, jj * 128:(jj + 1) * 128], h[:, kgl * 8 + k, :N],
                                   k == 0, k == 7, (w2b_, hb[kgl * 8 + k]), (pb,))
                            TT("dve", x[:, j, :N], x[:, j, :N], ps[:, :N], ALU.add, (xb[j], pb), (xb[j],))
                    continue
                pss2 = [T["mmrot"].next(), T["mmrot"].next()]
                for kgl in range(2):
                    r0 = half * 16 + kgl * 8
                    w2_, w2b_ = yield ("w", "f2_%d_%d_%d" % (l, r0, mi),
                                       dr["w_ff2"][l][:, r0:r0 + 8, 256 * mi:256 * mi + 256])
                    for jj in range(2):
                        ps, pb = pss2[jj]
                        for k in range(8):
                            MM(ps[:, :N], w2_[:, k, jj * 128:(jj + 1) * 128], h[:, kgl * 8 + k, :N],
                               kgl == 0 and k == 0, kgl == 1 and k == 7, (w2b_, hb[kgl * 8 + k]), (pb,))
                for jj in range(2):
                    j = 2 * mi + jj
                    ps, pb = pss2[jj]
                    TT("dve", x[:, j, :N], x[:, j, :N], ps[:, :N], ALU.add, (xb[j], pb), (xb[j],))

    def run_layers(ctxs):
        l_ = ctxs[0][1]
        DMA("pool", pw, dr["pw"][:, l_ * 8:(l_ + 1) * 8, :], (), (pwb,), "pw")
        gens = [layer(*c) for c in ctxs]
        reqs = [None] * len(gens)
        done = [False] * len(gens)
        bg = []

        def advance(i, val):
            try:
                reqs[i] = gens[i].send(val) if val is not None or reqs[i] is not None else next(gens[i])
            except StopIteration:
                done[i] = True
                reqs[i] = None

        def bg_step(n):
            for _ in range(n):
                if not bg:
                    return
                b = bg[0]
                if b[1] is None:
                    try:
                        b[1] = next(b[0])
                    except StopIteration:
                        bg.pop(0)
                        continue
                slot = W.next(b[1][2])
                try:
                    b[1] = b[0].send(slot)
                except StopIteration:
                    bg.pop(0)

        for i in range(len(gens)):
            try:
                reqs[i] = next(gens[i])
            except StopIteration:
                done[i] = True
        while not all(done):
            progressed = False
            for i in range(len(gens)):
                if done[i]:
                    continue
                r = reqs[i]
                if r[0] == "spawn":
                    bg.append([r[1], None])
                    advance(i, 0)
                    progressed = True
                elif r[0] == "join":
                    while bg:
                        bg_step(1)
                    advance(i, 0)
                    progressed = True
            if progressed:
                continue
            active = [i for i in range(len(gens)) if not done[i]]
            keys = set(reqs[i][1] for i in active)
            assert len(keys) == 1, keys
            slot = W.next(reqs[active[0]][2])
            for i in active:
                advance(i, slot)
            bg_step(1)
        while bg:
            bg_step(1)

    A.off = PHASE_BASE
    Tp = make_ctx(NT, False)
    Ts = make_ctx(NS, True)
    for ti in range(4):
        Tp["mmrot"], Tp["auxrot"] = (mm_rot, aux_rot) if ti == 3 else (mm_rot4, aux_rot4)
        DMA("sp", Tp["x"], dr["xT"][:, :, ti * NT:(ti + 1) * NT], (), Tp["xb"], "x")
        if ti == 3:
            DMA("sp", Ts["x"], dr["xsT"], (), Ts["xb"], "xs")
        for l in range(2):
            if ti == 3:
                run_layers([(Tp, l, ti), (Ts, l, 4)])
            else:
                run_layers([(Tp, l, ti)])
            stop("L")
        rmsnorm(Tp["x"], Tp["xb"], NT, "nfin", 0, Tp["m2"], Tp["m2b"], Tp["sq"], Tp["sd"], Tp["sdb"], Tp["rstd"],
                Tp["rstdb"])
        DMA("sp", dr["yT"][:, :, ti * NT:(ti + 1) * NT], Tp["m2"], Tp["m2b"], (), "y")
    for l in range(2):
        TS1("dve", cbp_r[l], cbp_r[l], 0.5, ALU.mult, (cbp_b[l],), (cbp_b[l],))
        DMA("sp", dr["cbp"][l], cbp_r[l], (cbp_b[l],), (), "o_cbp%d" % l)
        DMA("sp", dr["cap"][l], halo_a[l], halo_ab[l], (), "o_cap%d" % l)
        DMA("sp", dr["pp"][l], halo_p[l], halo_pb[l], (), "o_pp%d" % l)
    rmsnorm(Ts["x"], Ts["xb"], NS, "nfin", 0, Ts["m2"], Ts["m2b"], Ts["sq"], Ts["sd"], Ts["sdb"], Ts["rstd"],
            Ts["rstdb"], rot=s_rot)
    DMA("sp", dr["ysT"], Ts["m2"], Ts["m2b"], (), "ys")
    S.barrier()


def _fm(a):
    lead = a.shape[:-1]
    r = a.reshape(lead + (8, 128))
    return np.moveaxis(r, -1, 0)


def _pack_params(norm_mix, norm_mem, norm_ffn, norm_final, conv_a_w, conv_b_w, conv_b_bias, ln_b_gain, ln_b_bias,
                 pool_scale, gate_bias):
    prm = np.zeros((128, NPRM), np.float32)

    def put(name, arr):
        flat = np.ascontiguousarray(arr).reshape(128, -1)
        prm[:, PRM[name]:PRM[name] + flat.shape[1]] = flat

    put("nmix", _fm(norm_mix))
    put("nmem", _fm(norm_mem))
    put("nffn", _fm(norm_ffn))
    put("nfin", _fm(norm_final))
    put("caw", np.transpose(_fm(conv_a_w), (0, 1, 3, 2)))
    put("cbw", np.transpose(_fm(conv_b_w), (0, 1, 3, 2)))
    put("cbb", _fm(conv_b_bias))
    put("lng", _fm(ln_b_gain))
    put("lnb", _fm(ln_b_bias))
    put("psc", _fm(pool_scale))
    put("gb", np.transpose(gate_bias.reshape(2, 32, 128), (2, 0, 1)))
    return prm


def _wr(w, kc):
    L, K, E = w.shape
    return np.ascontiguousarray(np.transpose(w.reshape(L, kc, 128, E), (0, 2, 1, 3)))


def _make_in_maps(A_):
    (x_prompt, x_sample, mem_prompt, cache_mem_k, cache_mem_v, state_conv_a, state_conv_b, state_pool, norm_mix,
     norm_mem, w_kv, w_in, conv_a_w, conv_b_w, conv_b_bias, ln_b_gain, ln_b_bias, pool_w, pool_scale, gate_bias,
     w_o, norm_ffn, w_ff1, w_ff2, norm_final) = A_
    n = 8
    prm = _pack_params(norm_mix, norm_mem, norm_ffn, norm_final, conv_a_w, conv_b_w, conv_b_bias, ln_b_gain,
                       ln_b_bias, pool_scale, gate_bias)
    shared = dict(
        w_in=_wr(w_in, 8), w_kv=_wr(w_kv, 8), w_o=_wr(w_o, 8), w_ff1=_wr(w_ff1, 8), w_ff2=_wr(w_ff2, 32),
        pw=np.ascontiguousarray(np.transpose(pool_w.reshape(2, 4, 2, 128, 256), (3, 0, 1, 2, 4)).reshape(128, 16, 256)),
        prm=prm, ident=np.eye(128, dtype=np.float32),
        invc=np.ascontiguousarray(np.broadcast_to(
            np.array([[1.0 / min(t + 1, w) for t in range(16)] for w in WINS], np.float32)[None], (128, 4, 16))),
    )
    in_maps = []
    for i in range(n):
        sl = slice(NS * i, NS * (i + 1))
        m = dict(shared)
        m["xT"] = np.ascontiguousarray(_fm(x_prompt[i]))
        m["xT"] = np.ascontiguousarray(np.transpose(m["xT"], (0, 2, 1)))
        m["xsT"] = np.ascontiguousarray(np.transpose(_fm(x_sample[sl, 0, :]), (0, 2, 1)))
        m["memT"] = np.ascontiguousarray(np.transpose(_fm(mem_prompt[i]), (0, 2, 1)))
        k = cache_mem_k[:, sl].reshape(2, NS, 256, 8, 128)
        m["kT"] = np.ascontiguousarray(np.transpose(k, (0, 1, 4, 3, 2)))
        v = cache_mem_v[:, sl].reshape(2, NS, 2, 128, 1024)
        m["vv"] = np.ascontiguousarray(np.transpose(v, (0, 1, 3, 2, 4)))
        for nm, stt in (("sa", state_conv_a), ("sb", state_conv_b), ("sp", state_pool)):
            s_ = stt[:, sl]
            r = s_.reshape(2, NS, s_.shape[2], 8, 128)
            m[nm] = np.ascontiguousarray(np.transpose(r, (0, 4, 3, 1, 2)))
        in_maps.append(m)

    return in_maps


_NC_CACHE = {}


def kernel(x_prompt, x_sample, mem_prompt, cache_mem_k, cache_mem_v, state_conv_a, state_conv_b, state_pool,
           norm_mix, norm_mem, w_kv, w_in, conv_a_w, conv_b_w, conv_b_bias, ln_b_gain, ln_b_bias, pool_w,
           pool_scale, gate_bias, w_o, norm_ffn, w_ff1, w_ff2, norm_final):
    f = lambda a: np.asarray(a, dtype=np.float32)
    (x_prompt, x_sample, mem_prompt, cache_mem_k, cache_mem_v, state_conv_a, state_conv_b, state_pool, norm_mix,
     norm_mem, w_kv, w_in, conv_a_w, conv_b_w, conv_b_bias, ln_b_gain, ln_b_bias, pool_w, pool_scale, gate_bias,
     w_o, norm_ffn, w_ff1, w_ff2, norm_final) = map(f, (
        x_prompt, x_sample, mem_prompt, cache_mem_k, cache_mem_v, state_conv_a, state_conv_b, state_pool, norm_mix,
        norm_mem, w_kv, w_in, conv_a_w, conv_b_w, conv_b_bias, ln_b_gain, ln_b_bias, pool_w, pool_scale, gate_bias,
        w_o, norm_ffn, w_ff1, w_ff2, norm_final))
    n = 8
    in_maps = _make_in_maps((x_prompt, x_sample, mem_prompt, cache_mem_k, cache_mem_v, state_conv_a, state_conv_b,
                             state_pool, norm_mix, norm_mem, w_kv, w_in, conv_a_w, conv_b_w, conv_b_bias, ln_b_gain,
                             ln_b_bias, pool_w, pool_scale, gate_bias, w_o, norm_ffn, w_ff1, w_ff2, norm_final))
    if "nc" not in _NC_CACHE:
        _NC_CACHE["nc"] = build_program()
    nc = _NC_CACHE["nc"]
    res = run_bass_kernel_spmd(nc, in_maps, core_ids=list(range(n)))
    return _assemble(res.results)


def _assemble(R):
    n = 8

    def unfm(a):
        return np.transpose(a, (2, 1, 0)).reshape(a.shape[2], 1024)

    y_prompt = np.stack([unfm(R[i]["yT"]) for i in range(n)])
    y_sample = np.concatenate([unfm(R[i]["ysT"]) for i in range(n)])[:, None, :]
    mem_k = np.stack([np.stack([unfm(R[i]["mk"][l]).reshape(256, 4, 256) for i in range(n)]) for l in range(2)])
    mem_v = np.stack([np.stack([R[i]["mv"][l].reshape(256, 4, 256) for i in range(n)]) for l in range(2)])

    def pst(name):
        return np.stack([np.stack([unfm(R[i][name][l]) for i in range(n)]) for l in range(2)])

    def sst(name):
        outs = []
        for l in range(2):
            per = []
            for i in range(n):
                a = R[i][name][l]
                per.append(np.transpose(a, (2, 3, 1, 0)).reshape(NS, a.shape[3], 1024))
            outs.append(np.concatenate(per))
        return np.stack(outs)

    outs = (y_prompt, y_sample, mem_k, mem_v, pst("cap"), pst("cbp"), pst("pp"), sst("cas"), sst("cbs"), sst("pps"))
    return tuple(np.ascontiguousarray(o, dtype=np.float32) for o in outs)
```

```python
import contextlib
import numpy as np
import concourse.bass as bass
import concourse.mybir as mybir
from concourse.bass_utils import run_bass_kernel_spmd

F32 = mybir.dt.float32
BF16 = mybir.dt.bfloat16
AF = mybir.ActivationFunctionType
ALU = mybir.AluOpType
AX = mybir.AxisListType

D = 1024
SEQ = 2048
NT = 512
NS = 16
NMEM = 256
DPROJ = 11264
DFF = 4096
EPS = 1e-6
WINS = (2, 4, 8, 16)
RING = 7
SLOTW = 1024

C_H, C_B, C_C, C_GA, C_GB, C_P, C_Q, C_G = 0, 1024, 2048, 3072, 4096, 5120, 6144, 7168

PRM = {}
_o = 0
for _n, _w in (("nmix", 16), ("nmem", 16), ("nffn", 16), ("nfin", 8), ("caw", 48), ("cbw", 496),
               ("cbb", 16), ("lng", 16), ("lnb", 16), ("psc", 16), ("gb", 64)):
    PRM[_n] = _o
    _o += _w
NPRM = _o


class Buf:
    __slots__ = ("name", "writer", "readers")

    def __init__(self, name=""):
        self.name = name
        self.writer = None
        self.readers = {}


class Op:
    __slots__ = ("eng", "fn", "chan", "chan_val", "pos", "signal", "count", "waits")

    def __init__(self, eng, fn, chan):
        self.eng = eng
        self.fn = fn
        self.chan = chan
        self.chan_val = 0
        self.pos = 0
        self.signal = False
        self.count = 0
        self.waits = []


ENGS = ("pe", "act", "dve", "pool", "sp")


class Sched:
    def __init__(self, dry):
        self.dry = dry
        self.ops = {e: [] for e in ENGS}
        self.known = {e: {} for e in ENGS}
        self.chan_n = {}
        self.last_compute = {}
        self.last_dma = {}

    def _add_deps(self, o, deps):
        eng = o.eng
        for d in deps:
            key = ("c", d.chan) if d.chan else ("e", d.eng)
            val = d.chan_val if d.chan else d.pos
            if self.known[eng].get(key, -1) >= val:
                continue
            self.known[eng][key] = val
            d.signal = True
            o.waits.append(d)

    def op(self, eng, fn, reads=(), writes=(), chan=None):
        if self.dry:
            return None
        o = Op(eng, fn, chan)
        o.pos = len(self.ops[eng])
        raw = []
        other = []
        for b in reads:
            if b.writer is not None:
                raw.append(b.writer)
        for b in writes:
            if b.writer is not None:
                other.append(b.writer)
            other.extend(b.readers.values())
        deps = []
        seen = set()
        for lst, is_raw in ((raw, True), (other, False)):
            for d in lst:
                if id(d) in seen:
                    continue
                if d.chan is None and chan is None and d.eng == eng:
                    if eng == "pe":
                        continue
                seen.add(id(d))
                deps.append(d)
        self._add_deps(o, deps)
        if chan is not None:
            n = self.chan_n.get(chan, 0) + 1
            self.chan_n[chan] = n
            o.chan_val = 16 * n
            self.last_dma[chan] = o
        else:
            self.last_compute[eng] = o
        rkey = eng if chan is None else (chan, o.chan_val)
        for b in reads:
            b.readers[rkey] = o
        for b in writes:
            b.writer = o
            b.readers = {}
        self.ops[eng].append(o)
        return o

    def barrier(self):
        if self.dry:
            return
        deps = list(self.last_compute.values()) + list(self.last_dma.values())
        for e in ENGS:
            o = Op(e, None, None)
            o.pos = len(self.ops[e])
            self._add_deps(o, [d for d in deps if not (d.chan is None and d.eng == e)])
            self.ops[e].append(o)

    def finalize(self):
        for e in ENGS:
            c = 0
            for o in self.ops[e]:
                if o.chan is None and o.signal and o.fn is not None:
                    c += 1
                o.count = c


class Arena:
    def __init__(self, tensor, base, limit):
        self.t = tensor
        self.off = base
        self.limit = limit

    def alloc(self, shape, dtype):
        n = 1
        for s in shape[1:]:
            n *= s
        words = n if dtype == F32 else (n + 1) // 2
        words = (words + 7) // 8 * 8
        a = self.off
        self.off += words
        assert self.off <= self.limit, ("SBUF arena overflow", self.off, self.limit)
        ap = self.t[:, a:a + words]
        if dtype == BF16:
            ap = ap.bitcast(BF16)
        ap = ap[:, 0:n]
        if len(shape) == 3:
            ap = ap.rearrange("p (a b) -> p a b", a=shape[1])
        elif len(shape) == 4:
            ap = ap.rearrange("p (a b c) -> p a b c", a=shape[1], b=shape[2])
        return ap


class Rot:
    def __init__(self, items):
        self.items = items
        self.i = 0

    def next(self):
        r = self.items[self.i % len(self.items)]
        self.i += 1
        return r


class WStream:
    def __init__(self, S, slots, recorded):
        self.S = S
        self.slots = slots
        self.rec = recorded
        self.log = []
        self.issued = 0
        self.i = 0

    def _issue(self, n):
        src = self.rec[n]
        ap, buf = self.slots[n % RING]
        dst = self._view(ap, src.shape)
        self.S.op("pool", (lambda e, dst=dst, src=src: e.dma_start(out=dst, in_=src)),
                  reads=(), writes=(buf,), chan="w%d" % (n % RING))

    @staticmethod
    def _view(ap, shape):
        return ap.rearrange("p (a b) -> p a b", a=shape[1])

    def next(self, src):
        if self.rec is None:
            self.log.append(src)
            ap, buf = self.slots[len(self.log) % RING]
            return self._view(ap, src.shape), buf
        i = self.i
        self.i += 1
        while self.issued < min(len(self.rec), i + RING - 2):
            self._issue(self.issued)
            self.issued += 1
        ap, buf = self.slots[i % RING]
        return self._view(ap, src.shape), buf


def build_program():
    nc = bass.Bass("TRN2", target_bir_lowering=False)

    def din(name, shape):
        return nc.dram_tensor(name, list(shape), F32, kind="ExternalInput").ap()

    def dout(name, shape):
        return nc.dram_tensor(name, list(shape), F32, kind="ExternalOutput").ap()

    dr = dict(
        xT=din("xT", (128, 8, SEQ)), xsT=din("xsT", (128, 8, NS)), memT=din("memT", (128, 8, NMEM)),
        kT=din("kT", (2, NS, 128, 8, 256)), vv=din("vv", (2, NS, 128, 2, 1024)),
        sa=din("sa", (2, 128, 8, NS, 2)), sb=din("sb", (2, 128, 8, NS, 30)), sp=din("sp", (2, 128, 8, NS, 15)),
        w_in=din("w_in", (2, 128, 8, DPROJ)), w_kv=din("w_kv", (2, 128, 8, 2048)),
        w_o=din("w_o", (2, 128, 8, D)), w_ff1=din("w_ff1", (2, 128, 8, DFF)), w_ff2=din("w_ff2", (2, 128, 32, D)),
        pw=din("pw", (128, 16, 256)), prm=din("prm", (128, NPRM)), ident=din("ident", (128, 128)),
        invc=din("invc", (128, 4, 16)),
        yT=dout("yT", (128, 8, SEQ)), ysT=dout("ysT", (128, 8, NS)),
        mk=dout("mk", (2, 128, 8, NMEM)), mv=dout("mv", (2, NMEM, D)),
        cap=dout("cap", (2, 128, 8, 2)), cbp=dout("cbp", (2, 128, 8, 30)), pp=dout("pp", (2, 128, 8, 15)),
        cas=dout("cas", (2, 128, 8, NS, 2)), cbs=dout("cbs", (2, 128, 8, NS, 30)), pps=dout("pps", (2, 128, 8, NS, 15)),
    )

    import os
    ARENA_WORDS = int(os.environ.get("ARENA_WORDS", "51200"))
    with contextlib.ExitStack() as st:
        arena_t = st.enter_context(nc.sbuf_tensor("arena", [128, ARENA_WORDS], F32))
        psum = [st.enter_context(nc.psum_tensor("ps%d" % i, [128, 512], F32)) for i in range(8)]

        S = None
        rec = None
        for run in ("dry", "real"):
            S = Sched(dry=(run == "dry"))
            W = emit_all(nc, S, dr, arena_t, ARENA_WORDS, psum, rec)
            if run == "dry":
                rec = W.log
        S.finalize()

        eng_sem = {e: st.enter_context(nc.semaphore("sem_" + e)) for e in ENGS}
        chan_sem = {c: st.enter_context(nc.semaphore("ch_" + c)) for c in S.chan_n}

        def emit(e, name):
            for o in S.ops[name]:
                for d in o.waits:
                    if d.chan is not None:
                        e.wait_ge(chan_sem[d.chan], d.chan_val)
                    else:
                        e.wait_ge(eng_sem[d.eng], d.count)
                if o.fn is None:
                    continue
                ins = o.fn(e)
                if o.chan is not None:
                    ins.then_inc(chan_sem[o.chan], 16)
                elif o.signal:
                    ins.then_inc(eng_sem[name], 1)

        with nc.Block() as block:
            @block.tensor
            def _(e):
                emit(e, "pe")

            @block.scalar
            def _(e):
                emit(e, "act")

            @block.vector
            def _(e):
                emit(e, "dve")

            @block.gpsimd
            def _(e):
                emit(e, "pool")

            @block.sync
            def _(e):
                emit(e, "sp")
    return nc


class _Stop(Exception):
    pass


KSTOP = [None]
MARKS = []
DIAG_ENG = "pool"


def emit_all(nc, S, dr, arena_t, ARENA_WORDS, psum, rec):
    holder = {}
    try:
        _emit_all(nc, S, dr, arena_t, ARENA_WORDS, psum, rec, holder)
    except _Stop:
        S.barrier()
    return holder["W"]


def _emit_all(nc, S, dr, arena_t, ARENA_WORDS, psum, rec, holder):
    A = Arena(arena_t, 0, ARENA_WORDS)

    def stop(stage):
        if not S.dry:
            MARKS.append((stage, len(S.ops["pe"])))
        if KSTOP[0] == stage:
            raise _Stop()

    def ACT(out, in_, func, reads, writes, bias=None, scale=None):
        kw = {}
        if bias is not None:
            kw["bias"] = bias
        if scale is not None:
            kw["scale"] = scale
        S.op("act", lambda e: e.activation(out=out, in_=in_, func=func, **kw), reads, writes)

    def STT(eng, out, in0, scalar, in1, op0, op1, reads, writes):
        S.op(eng, lambda e: e.scalar_tensor_tensor(out=out, in0=in0, scalar=scalar, in1=in1, op0=op0, op1=op1),
             reads, writes)

    def TT(eng, out, in0, in1, op, reads, writes):
        S.op(eng, lambda e: e.tensor_tensor(out=out, in0=in0, in1=in1, op=op), reads, writes)

    def TS1(eng, out, in_, scalar, op, reads, writes):
        S.op(eng, lambda e: e.tensor_single_scalar(out=out, in_=in_, scalar=scalar, op=op), reads, writes)

    def TS2(eng, out, in0, s1, s2, op0, op1, reads, writes):
        S.op(eng, lambda e: e.tensor_scalar(out=out, in0=in0, scalar1=s1, scalar2=s2, op0=op0, op1=op1), reads, writes)

    def CP(eng, out, in_, reads, writes):
        S.op(eng, lambda e: e.tensor_copy(out=out, in_=in_), reads, writes)

    def RCP(out, in_, reads, writes):
        S.op("dve", lambda e: e.reciprocal(out=out, in_=in_), reads, writes)

    def RED(out, in_, reads, writes):
        S.op("dve", lambda e: e.tensor_reduce(out=out, in_=in_, axis=AX.X, op=ALU.add), reads, writes)

    def MM(out, lhsT, rhs, start, stop, reads, writes):
        S.op("pe", lambda e: e.matmul(out, lhsT, rhs, start=start, stop=stop), reads, writes)

    def DMA(eng, out, in_, reads, writes, chan):
        S.op(eng, lambda e: e.dma_start(out=out, in_=in_), reads, writes, chan=chan)

    def MEMSET(eng, ap, val, writes):
        S.op(eng, lambda e: e.memset(ap, val), (), writes)

    psb_ = [Buf("psb%d" % i) for i in range(8)]
    mm_rot = Rot([(psum[i], psb_[i]) for i in range(3)])
    aux_rot = Rot([(psum[i], psb_[i]) for i in range(3, 6)])
    mm_rot4 = Rot([(psum[i], psb_[i]) for i in (0, 1, 2, 6)])
    aux_rot4 = Rot([(psum[i], psb_[i]) for i in (3, 4, 5, 7)])
    ps6b, ps7b = psb_[6], psb_[7]
    _sreg = []
    for i in range(12):
        _sreg.append((psum[6][:, 16 * i:16 * i + 16], ps6b))
        _sreg.append((psum[7][:, 320 + 16 * i:320 + 16 * i + 16], ps7b))
    s_rot = Rot(_sreg)

    ring = []
    for i in range(RING):
        ap = A.alloc([128, 2048], BF16)
        ring.append((ap, Buf("slot%d" % i)))
    W = WStream(S, ring, rec)
    holder["W"] = W

    prm = A.alloc([128, NPRM], F32)
    hgb = A.alloc([128, 64], F32)
    ident = A.alloc([128, 128], BF16)
    ones = A.alloc([128, 128], BF16)
    epsT = A.alloc([128, 1], F32)
    invc = A.alloc([128, 4, 16], F32)
    pw = A.alloc([128, 8, 256], BF16)
    pwb = Buf("pw")
    KT = [A.alloc([128, 8, 256], BF16) for _ in range(2)]
    VV = [A.alloc([128, 2, 1024], BF16) for _ in range(2)]
    halo_a = [A.alloc([128, 8, 2], F32) for _ in range(2)]
    halo_b = [A.alloc([128, 8, 30], BF16) for _ in range(2)]
    halo_p = [A.alloc([128, 8, 15], F32) for _ in range(2)]
    cbp_r = [A.alloc([128, 8, 30], F32) for _ in range(2)]
    halo_ab = [[Buf() for _ in range(8)] for _ in range(2)]
    halo_bb = [[Buf() for _ in range(8)] for _ in range(2)]
    halo_pb = [[Buf() for _ in range(8)] for _ in range(2)]
    cbp_b = [Buf() for _ in range(2)]
    KTb = [Buf(), Buf()]
    VVb = [Buf(), Buf()]
    cst = Buf("const")
    PHASE_BASE = A.off

    def P(name, idx, width=1):
        o = PRM[name] + idx
        return prm[:, o:o + width]

    prmb = Buf("prm")
    DMA("sp", prm, dr["prm"], (), (prmb,), "setup")
    DMA("sp", invc, dr["invc"], (), (Buf(),), "setup2")
    DMA("pool", ident, dr["ident"], (), (Buf(),), "setup3")
    MEMSET("dve", ones, 1.0, (Buf(),))
    MEMSET("dve", epsT, EPS, (Buf(),))
    for l in range(2):
        MEMSET("dve", halo_a[l], 0.0, halo_ab[l])
        MEMSET("dve", halo_b[l], 0.0, halo_bb[l])
        MEMSET("dve", halo_p[l], 0.0, halo_pb[l])
    TS1("dve", hgb, prm[:, PRM["gb"]:PRM["gb"] + 64], 0.5, ALU.mult, (prmb,), (Buf(),))
    S.barrier()
    stop("setup")

    def rmsnorm(x, xb, N, gain_name, gain_idx, out, outb, sqrot, sd, sdb, rstd, rstdb, rot=None):
        ps, pb = (rot or aux_rot).next()
        for c in range(8):
            sq, sqb = sqrot.next()
            ACT(sq[:, :N], x[:, c, :N], AF.Square, (xb[c],), (sqb,))
            MM(ps[:, :N], ones, sq[:, :N], c == 0, c == 7, (sqb,), (pb,))
        ACT(sd[:, :N], ps[:, :N], AF.Sqrt, (pb,), (sdb,), bias=epsT[:, 0:1], scale=1.0 / D)
        RCP(rstd[:, :N], sd[:, :N], (sdb,), (rstdb,))
        for c in range(8):
            STT("dve", out[:, c, :N], x[:, c, :N], P(gain_name, gain_idx + c), rstd[:, :N], ALU.mult, ALU.mult,
                (xb[c], rstdb), (outb[c],))

    A.off = PHASE_BASE
    mem = A.alloc([128, 8, NMEM], F32)
    memb = [Buf() for _ in range(8)]
    memn = A.alloc([128, 8, NMEM], BF16)
    memnb = [Buf() for _ in range(8)]
    kst = A.alloc([128, 8, NMEM], F32)
    kstb = Buf()
    vst = A.alloc([128, 2, 1024], F32)
    vstb = Buf()
    sqr = Rot([(A.alloc([128, 512], BF16), Buf()) for _ in range(2)])
    sd0 = A.alloc([128, 512], F32)
    rs0 = A.alloc([128, 512], F32)
    sd0b, rs0b = Buf(), Buf()
    DMA("sp", mem, dr["memT"], (), memb, "mem")
    for l in range(2):
        stop("kv_load")
        rmsnorm(mem, memb, NMEM, "nmem", l * 8, memn, memnb, sqr, sd0, sd0b, rs0, rs0b)
        stop("kv_norm")
        for e2 in range(4):
            wk, wkb = W.next(dr["w_kv"][l][:, :, 256 * e2:256 * e2 + 256])
            for jj in range(2):
                e_ = 2 * e2 + jj
                ps, pb = mm_rot.next()
                for k in range(8):
                    MM(ps[:, :NMEM], wk[:, k, jj * 128:(jj + 1) * 128], memn[:, k, :], k == 0, k == 7,
                       (wkb, memnb[k]), (pb,))
                ACT(kst[:, e_, :], ps[:, :NMEM], AF.Copy, (pb,), (kstb,))
                CP("dve", KT[l][:, e_, :], kst[:, e_, :], (kstb,), (KTb[l],))
        stop("kv_k")
        DMA("sp", dr["mk"][l], kst, (kstb,), (), "mk")
        stop("kv_mk")
        for s in range(4):
            wv, wvb = W.next(dr["w_kv"][l][:, :, 1024 + 256 * s:1024 + 256 * s + 256])
            for tc in range(2):
                ps, pb = mm_rot.next()
                for k in range(8):
                    MM(ps[:, :256], memn[:, k, tc * 128:(tc + 1) * 128], wv[:, k, :], k == 0, k == 7,
                       (wvb, memnb[k]), (pb,))
                ACT(vst[:, tc, 256 * s:256 * s + 256], ps[:, :256], AF.Copy, (pb,), (vstb,))
                CP("dve", VV[l][:, tc, 256 * s:256 * s + 256], vst[:, tc, 256 * s:256 * s + 256], (vstb,), (VVb[l],))
        stop("kv_v")
        DMA("sp", dr["mv"][l].rearrange("(tc p) e -> p tc e", p=128), vst, (vstb,), (), "mv")
        stop("kv_mv")
    S.barrier()
    stop("kv")

    def make_ctx(N, sample):
        T = {}
        T["N"] = N
        T["sample"] = sample
        T["mmrot"] = s_rot if sample else mm_rot
        T["auxrot"] = s_rot if sample else aux_rot
        NM = N
        T["x"] = A.alloc([128, 8, NM], F32)
        T["xb"] = [Buf() for _ in range(8)]
        T["xn"] = A.alloc([128, 8, NM], BF16)
        T["xnb"] = [Buf() for _ in range(8)]
        T["z"] = A.alloc([128, 8, NM], F32)
        T["zb"] = [Buf() for _ in range(8)]
        T["m2"] = A.alloc([128, 8, NM], F32)
        T["m2b"] = [Buf() for _ in range(8)]
        T["mb"] = A.alloc([128, 8, NM], BF16)
        T["mbb"] = [Buf() for _ in range(8)]
        T["h"] = T["z"].rearrange("p a b -> p (a b)").bitcast(BF16).rearrange("p (a b) -> p a b", a=16)
        T["hb"] = [T["zb"][i // 2] for i in range(16)]
        T["tmp"] = Rot([(A.alloc([128, NM], F32), Buf()) for _ in range(5)])
        T["bft"] = Rot([(A.alloc([128, NM], BF16), Buf()) for _ in range(3)])
        T["sq"] = Rot([(A.alloc([128, NM], BF16), Buf()) for _ in range(2)])
        for grp in (("rstd", "lnrstd", "rden"), ("sd", "lnnmr")):
            ap_, b_ = A.alloc([128, NM], F32), Buf()
            for nm in grp:
                T[nm] = ap_
                T[nm + "b"] = b_
        T["c1"] = [(A.alloc([128, NM], F32), Buf()) for _ in range(2)]
        T["plbf"] = [(A.alloc([128, NM], BF16), Buf()) for _ in range(2)]
        T["qbf"] = [(A.alloc([128, NM], BF16), Buf()) for _ in range(2)]
        T["ebf"] = [(A.alloc([128, NM], BF16), Buf()) for _ in range(2)]
        if not sample:
            T["glu"] = Rot([(A.alloc([128, 30 + NM], BF16), Buf()) for _ in range(2)])
            T["diag"] = [(A.alloc([128, 31, 128], BF16), Buf()) for _ in range(2)]
            T["ub"] = Rot([(A.alloc([128, 2 + NM], F32), Buf()) for _ in range(2)])
            T["pb"] = Rot([(A.alloc([128, 15 + NM], F32), Buf()) for _ in range(2)])
            T["sA"] = (A.alloc([128, 16 + NM], F32), Buf())
            T["sB"] = (A.alloc([128, 16 + NM], F32), Buf())
        else:
            T["sast"] = A.alloc([128, 8, NS, 2], F32)
            T["nsa"] = A.alloc([128, 8, NS, 2], F32)
            T["sastb"], T["nsab"] = Buf(), Buf()
            T["sb_in"] = [(A.alloc([128, NS, 30], F32), Buf(), "i_sb%d" % i) for i in range(2)]
            T["sb_out"] = [(A.alloc([128, NS, 30], F32), Buf(), "o_sb%d" % i) for i in range(2)]
            T["sp_in"] = [(A.alloc([128, NS, 15], F32), Buf(), "i_sp%d" % i) for i in range(2)]
            T["sp_out"] = [(A.alloc([128, NS, 15], F32), Buf(), "o_sp%d" % i) for i in range(2)]
            T["qall"] = (A.alloc([128, 8, NS], BF16), Buf())
            T["t3all"] = (A.alloc([128, 8, NS], F32), Buf())
            T["yms"] = (A.alloc([128, 8, NS], F32), Buf())
            T["E"] = (A.alloc([128, 128], BF16), Buf())
            T["rdens"] = (A.alloc([128, 64], F32), Buf())
        return T

    def proj(T, wt, wtb, jj, src, srcb):
        N = T["N"]
        ps, pb = T["mmrot"].next()
        for k in range(8):
            MM(ps[:, :N], wt[:, k, jj * 128:(jj + 1) * 128], src[:, k, :N], k == 0, k == 7, (wtb, srcb[k]), (pb,))
        return ps, pb

    def sample_attention(T, l):
        N = T["N"]
        qall, qallb = T["qall"]
        t3all, t3allb = T["t3all"]
        yms, ymsb = T["yms"]
        E, Eb = T["E"]
        pssc, psscb = psum[7][:, 0:128], ps7b
        for b in range(NS):
            kt, ktb = yield ("kv", "k%d_%d" % (l, b), dr["kT"][l, b])
            for g in range(4):
                for kc in range(2):
                    col = b * 8 + g * 2 + kc
                    for dc in range(2):
                        MM(pssc[:, col:col + 1], kt[:, 2 * g + dc, kc * 128:(kc + 1) * 128],
                           qall[:, 2 * g + dc, b:b + 1], dc == 0, dc == 1, (ktb, qallb), (psscb,))
        ACT(E, pssc[:, 0:128], AF.Exp, (psscb,), (Eb,), scale=1.0 / 16.0)
        psden, psdenb = psum[7][:, 128:192], ps7b
        for kc in range(2):
            MM(psden[:, 0:64], ones, E[:, kc:128:2], kc == 0, kc == 1, (Eb,), (psdenb,))
        rdens, rdensb = T["rdens"]
        RCP(rdens, psden[:, 0:64], (psdenb,), (rdensb,))
        pso, psob = psum[7][:, 192:320], ps7b
        for b in range(NS):
            vt, vtb = yield ("kv", "v%d_%d" % (l, b), dr["vv"][l, b])
            for j in range(8):
                g = j // 2
                for kc in range(2):
                    ec = b * 8 + g * 2 + kc
                    MM(pso[:, b * 8 + j:b * 8 + j + 1], vt[:, kc, j * 128:(j + 1) * 128], E[:, ec:ec + 1],
                       kc == 0, kc == 1, (vtb, Eb), (psob,))
        for j in range(8):
            g = j // 2
            STT("dve", yms[:, j, :], t3all[:, j, :], 1.0, pso[:, j:128:8], ALU.add, ALU.mult, (t3allb, psob), (ymsb,))
            TT("dve", yms[:, j, :], yms[:, j, :], rdens[:, g:64:4], ALU.mult, (ymsb, rdensb), (ymsb,))

    def layer(T, l, ti):
        N = T["N"]
        sample = T["sample"]
        x, xb, xn, xnb = T["x"], T["xb"], T["xn"], T["xnb"]
        z, zb, m2, m2b, mb, mbb, h, hb = T["z"], T["zb"], T["m2"], T["m2b"], T["mb"], T["mbb"], T["h"], T["hb"]
        tmp = T["tmp"]
        win = dr["w_in"][l]

        def wreq(c0):
            return ("w", "in%d_%d" % (l, c0), win[:, :, c0:c0 + 256])

        def gate_tanh(psg, pgb, gi, j):
            t, tb_ = tmp.next()
            ACT(t[:, :N], psg[:, :N], AF.Tanh, (pgb,), (tb_,), bias=hgb[:, l * 32 + gi * 8 + j:l * 32 + gi * 8 + j + 1],
                scale=0.5)
            return t, tb_

        rmsnorm(x, xb, N, "nmix", l * 8, xn, xnb, T["sq"], T["sd"], T["sdb"], T["rstd"], T["rstdb"], rot=T["auxrot"])

        for g in range(4):
            wq, wqb = yield wreq(C_Q + 256 * g)
            wg3, wg3b = yield wreq(C_G + 3072 + 256 * g)
            for jj in range(2):
                j = 2 * g + jj
                psq_, pqb = proj(T, wq, wqb, jj, xn, xnb)
                if not sample:
                    ACT(T["qbf"][jj][0][:, :N], psq_[:, :N], AF.Copy, (pqb,), (T["qbf"][jj][1],))
                else:
                    ACT(T["qall"][0][:, j, :], psq_[:, :N], AF.Copy, (pqb,), (T["qall"][1],))
            for dc in range(2):
                j = 2 * g + dc
                psg, pgb = proj(T, wg3, wg3b, dc, xn, xnb)
                if not sample:
                    t3, t3b = T["c1"][dc]
                    ACT(t3[:, :N], psg[:, :N], AF.Tanh, (pgb,), (t3b,),
                        bias=hgb[:, l * 32 + 24 + j:l * 32 + 24 + j + 1], scale=0.5)
                else:
                    ACT(T["t3all"][0][:, j, :], psg[:, :N], AF.Tanh, (pgb,), (T["t3all"][1],),
                        bias=hgb[:, l * 32 + 24 + j:l * 32 + 24 + j + 1], scale=0.5)
            if not sample:
                for kc in range(2):
                    pss_, pssb_ = T["auxrot"].next()
                    for dc in range(2):
                        MM(pss_[:, :N], KT[l][:, 2 * g + dc, kc * 128:(kc + 1) * 128], T["qbf"][dc][0][:, :N],
                           dc == 0, dc == 1, (KTb[l], T["qbf"][dc][1]), (pssb_,))
                    ACT(T["ebf"][kc][0][:, :N], pss_[:, :N], AF.Exp, (pssb_,), (T["ebf"][kc][1],), scale=1.0 / 16.0)
                psd, psdb = T["auxrot"].next()
                for kc in range(2):
                    MM(psd[:, :N], ones, T["ebf"][kc][0][:, :N], kc == 0, kc == 1, (T["ebf"][kc][1],), (psdb,))
                RCP(T["rden"][:, :N], psd[:, :N], (psdb,), (T["rdenb"],))
                for dc in range(2):
                    j = 2 * g + dc
                    t3, t3b = T["c1"][dc]
                    pso, psob = T["auxrot"].next()
                    for kc in range(2):
                        MM(pso[:, :N], VV[l][:, kc, j * 128:(j + 1) * 128], T["ebf"][kc][0][:, :N], kc == 0, kc == 1,
                           (VVb[l], T["ebf"][kc][1]), (psob,))
                    ym, ymb = tmp.next()
                    STT("dve", ym[:, :N], t3[:, :N], 1.0, pso[:, :N], ALU.add, ALU.mult, (t3b, psob), (ymb,))
                    TT("dve", m2[:, j, :N], ym[:, :N], T["rden"][:, :N], ALU.mult, (ymb, T["rdenb"]), (m2b[j],))
        if sample:
            yield ("spawn", sample_attention(T, l))
        stop("M")

        for g in range(4):
            wa, wab = yield wreq(C_GA + 256 * g)
            wb_, wbb = yield wreq(C_GB + 256 * g)
            if sample:
                for jj in range(2):
                    sbin, sbinb, ch = T["sb_in"][jj]
                    DMA("sp", sbin, dr["sb"][l][:, 2 * g + jj], (), (sbinb,), ch)
            stash = []
            for jj in range(2):
                j = 2 * g + jj
                cbw = prm[:, PRM["cbw"] + (l * 8 + j) * 31:PRM["cbw"] + (l * 8 + j) * 31 + 31]
                psa, pab = proj(T, wa, wab, jj, xn, xnb)
                psb, pbb = proj(T, wb_, wbb, jj, xn, xnb)
                tb, tbb = tmp.next()
                ACT(tb[:, :N], psb[:, :N], AF.Tanh, (pbb,), (tbb,), scale=0.5)
                if not sample:
                    gl, glb = T["glu"].next()
                    CP("pool", gl[:, 0:30], halo_b[l][:, j, :], (halo_bb[l][j],), (glb,))
                    STT("dve", gl[:, 30:30 + N], tb[:, :N], 1.0, psa[:, :N], ALU.add, ALU.mult, (tbb, pab), (glb,))
                    if ti == 3:
                        STT("dve", cbp_r[l][:, j, :], tb[:, N - 30:N], 1.0, psa[:, N - 30:N], ALU.add, ALU.mult,
                            (tbb, pab), (cbp_b[l],))
                    dg, dgb = T["diag"][jj]
                    TT(("pool", "dve")[jj], dg, ident.unsqueeze(1).to_broadcast([128, 31, 128]),
                       cbw.unsqueeze(2).to_broadcast([128, 31, 128]), ALU.mult, (), (dgb,))
                    stash.append((gl, glb, dg, dgb))
                else:
                    sbin, sbinb, _ = T["sb_in"][jj]
                    sbo, sbob, cho = T["sb_out"][jj]
                    gs, gsb = tmp.next()
                    STT("dve", gs[:, :N], tb[:, :N], 1.0, psa[:, :N], ALU.add, ALU.mult, (tbb, pab), (gsb,))
                    TS1("dve", gs[:, :N], gs[:, :N], 0.5, ALU.mult, (gsb,), (gsb,))
                    pr, prb = sbo, sbob
                    TT("dve", pr, sbin, cbw[:, 0:30].unsqueeze(1).to_broadcast([128, NS, 30]),
                       ALU.mult, (sbinb,), (prb,))
                    rd, rdb = tmp.next()
                    RED(rd[:, :N], pr, (prb,), (rdb,))
                    STT("dve", rd[:, :N], gs[:, :N], cbw[:, 30:31], rd[:, :N], ALU.mult, ALU.add, (gsb, rdb), (rdb,))
                    ACT(z[:, j, :N], rd[:, :N], AF.Identity, (rdb,), (zb[j],), bias=P("cbb", l * 8 + j), scale=1.0)
                    CP("pool", sbo[:, :, 0:29], sbin[:, :, 1:30], (sbinb,), (sbob,))
                    CP("pool", sbo[:, :, 29], gs[:, :N], (gsb,), (sbob,))
                    DMA("sp", dr["cbs"][l][:, j], sbo, (sbob,), (), cho)
            if not sample:
                for jj in range(2):
                    j = 2 * g + jj
                    gl, glb, dg, dgb = stash[jj]
                    psz, pzb = T["auxrot"].next()
                    for k in range(31):
                        MM(psz[:, :N], dg[:, k, :], gl[:, k:k + N], k == 0, k == 30, (dgb, glb), (pzb,))
                    ACT(z[:, j, :N], psz[:, :N], AF.Identity, (pzb,), (zb[j],), bias=P("cbb", l * 8 + j), scale=0.5)
                    CP("pool", halo_b[l][:, j, :], gl[:, N:N + 30], (glb,), (halo_bb[l][j],))
        stop("B")

        pss, pssb = T["auxrot"].next()
        psq, psqb = T["auxrot"].next()
        for c in range(8):
            z1, z1b = T["bft"].next()
            z2, z2b = T["bft"].next()
            ACT(z1[:, :N], z[:, c, :N], AF.Copy, (zb[c],), (z1b,))
            ACT(z2[:, :N], z[:, c, :N], AF.Square, (zb[c],), (z2b,))
            MM(pss[:, :N], ones, z1[:, :N], c == 0, c == 7, (z1b,), (pssb,))
            MM(psq[:, :N], ones, z2[:, :N], c == 0, c == 7, (z2b,), (psqb,))
        mean, meanb = tmp.next()
        msq, msqb = tmp.next()
        var, varb = tmp.next()
        TS1("dve", mean[:, :N], pss[:, :N], 1.0 / D, ALU.mult, (pssb,), (meanb,))
        TT("dve", msq[:, :N], mean[:, :N], mean[:, :N], ALU.mult, (meanb,), (msqb,))
        STT("dve", var[:, :N], psq[:, :N], 1.0 / D, msq[:, :N], ALU.mult, ALU.subtract, (psqb, msqb), (varb,))
        ACT(var[:, :N], var[:, :N], AF.Sqrt, (varb,), (varb,), bias=epsT[:, 0:1], scale=1.0)
        lr, lrb, ln_, lnb_ = T["lnrstd"], T["lnrstdb"], T["lnnmr"], T["lnnmrb"]
        RCP(lr[:, :N], var[:, :N], (varb,), (lrb,))
        STT("dve", ln_[:, :N], mean[:, :N], -1.0, lr[:, :N], ALU.mult, ALU.mult, (meanb, lrb), (lnb_,))
        for g in range(4):
            wg, wgb = yield wreq(C_G + 1024 + 256 * g)
            for jj in range(2):
                j = 2 * g + jj
                psg, pgb = proj(T, wg, wgb, jj, xn, xnb)
                t1, t1b = T["c1"][jj]
                ACT(t1[:, :N], psg[:, :N], AF.Tanh, (pgb,), (t1b,),
                    bias=hgb[:, l * 32 + 8 + j:l * 32 + 8 + j + 1], scale=0.5)
            for jj in range(2):
                j = 2 * g + jj
                t1, t1b = T["c1"][jj]
                v1, v1b = tmp.next()
                w2, w2b = tmp.next()
                STT("dve", v1[:, :N], z[:, j, :N], P("lng", l * 8 + j), lr[:, :N], ALU.mult, ALU.mult,
                    (zb[j], lrb), (v1b,))
                TS2("dve", w2[:, :N], ln_[:, :N], P("lng", l * 8 + j), P("lnb", l * 8 + j), ALU.mult, ALU.add,
                    (lnb_,), (w2b,))
                TT("dve", v1[:, :N], v1[:, :N], w2[:, :N], ALU.add, (v1b, w2b), (v1b,))
                yb, ybb = tmp.next()
                ACT(yb[:, :N], v1[:, :N], AF.Silu, (v1b,), (ybb,))
                if sample:
                    STT("dve", m2[:, j, :N], t1[:, :N], 1.0, yb[:, :N], ALU.add, ALU.mult, (t1b, ybb), (m2b[j],))
                else:
                    STT("dve", yb[:, :N], t1[:, :N], 1.0, yb[:, :N], ALU.add, ALU.mult, (t1b, ybb), (ybb,))
                    TT("dve", m2[:, j, :N], m2[:, j, :N], yb[:, :N], ALU.add, (m2b[j], ybb), (m2b[j],))
        stop("LN")

        if sample:
            DMA("sp", T["sast"], dr["sa"][l], (), (T["sastb"],), "i_sa")
        for g in range(4):
            wh, whb = yield wreq(C_H + 256 * g)
            wc, wcb = yield wreq(C_C + 256 * g)
            for jj in range(2):
                j = 2 * g + jj
                caw = prm[:, PRM["caw"] + (l * 8 + j) * 3:PRM["caw"] + (l * 8 + j) * 3 + 3]
                psh, phb = proj(T, wh, whb, jj, xn, xnb)
                psc, pcb = proj(T, wc, wcb, jj, xn, xnb)
                hs, hsb = tmp.next()
                ACT(hs[:, :N], psh[:, :N], AF.Copy, (phb,), (hsb,))
                c1, c1b = T["c1"][jj]
                if not sample:
                    ub, ubb = T["ub"].next()
                    CP("pool", ub[:, 0:2], halo_a[l][:, j, :], (halo_ab[l][j],), (ubb,))
                    TT("dve", ub[:, 2:2 + N], psc[:, :N], hs[:, :N], ALU.mult, (pcb, hsb), (ubb,))
                    TS1("dve", c1[:, :N], ub[:, 0:N], caw[:, 0:1], ALU.mult, (ubb,), (c1b,))
                    STT("dve", c1[:, :N], ub[:, 1:N + 1], caw[:, 1:2], c1[:, :N], ALU.mult, ALU.add, (ubb, c1b), (c1b,))
                    STT("dve", c1[:, :N], ub[:, 2:N + 2], caw[:, 2:3], c1[:, :N], ALU.mult, ALU.add, (ubb, c1b), (c1b,))
                    CP("pool", halo_a[l][:, j, :], ub[:, N:N + 2], (ubb,), (halo_ab[l][j],))
                else:
                    us, usb = tmp.next()
                    TT("dve", us[:, :N], psc[:, :N], hs[:, :N], ALU.mult, (pcb, hsb), (usb,))
                    sast = T["sast"]
                    TS1("dve", c1[:, :N], sast[:, j, :, 0], caw[:, 0:1], ALU.mult, (T["sastb"],), (c1b,))
                    STT("dve", c1[:, :N], sast[:, j, :, 1], caw[:, 1:2], c1[:, :N], ALU.mult, ALU.add,
                        (T["sastb"], c1b), (c1b,))
                    STT("dve", c1[:, :N], us[:, :N], caw[:, 2:3], c1[:, :N], ALU.mult, ALU.add, (usb, c1b), (c1b,))
                    CP("pool", T["nsa"][:, j, :, 0], sast[:, j, :, 1], (T["sastb"],), (T["nsab"],))
                    CP("pool", T["nsa"][:, j, :, 1], us[:, :N], (usb,), (T["nsab"],))
            wbw, wbwb = yield wreq(C_B + 256 * g)
            wg0, wg0b = yield wreq(C_G + 256 * g)
            for jj in range(2):
                j = 2 * g + jj
                c1, c1b = T["c1"][jj]
                psb2, pb2b = proj(T, wbw, wbwb, jj, xn, xnb)
                psg, pgb = proj(T, wg0, wg0b, jj, xn, xnb)
                t0, t0b = gate_tanh(psg, pgb, 0, j)
                ya, yab = tmp.next()
                STT("dve", ya[:, :N], t0[:, :N], 1.0, c1[:, :N], ALU.add, ALU.mult, (t0b, c1b), (yab,))
                TT("dve", ya[:, :N], psb2[:, :N], ya[:, :N], ALU.mult, (pb2b, yab), (yab,))
                TT("dve", m2[:, j, :N], m2[:, j, :N], ya[:, :N], ALU.add, (m2b[j], yab), (m2b[j],))
        if sample:
            DMA("sp", dr["cas"][l], T["nsa"], (T["nsab"],), (), "o_sa")
        stop("A")

        for g in range(4):
            wp, wpb = yield wreq(C_P + 256 * g)
            wg2, wg2b = yield wreq(C_G + 2048 + 256 * g)
            w = WINS[g]
            if sample:
                for jj in range(2):
                    spin, spinb, ch = T["sp_in"][jj]
                    DMA("sp", spin, dr["sp"][l][:, 2 * g + jj], (), (spinb,), ch)
            for jj in range(2):
                j = 2 * g + jj
                psp, ppb = proj(T, wp, wpb, jj, xn, xnb)
                pl, plb = T["plbf"][jj]
                if not sample:
                    pbuf, pbb_ = T["pb"].next()
                    CP("pool", pbuf[:, 0:15], halo_p[l][:, j, :], (halo_pb[l][j],), (pbb_,))
                    ACT(pbuf[:, 15:15 + N], psp[:, :N], AF.Copy, (ppb,), (pbb_,))
                    cur, curb, ln, e0 = pbuf, pbb_, 15 + N, 0
                    d = 1
                    nxt = [T["sA"], T["sB"]]
                    ni = 0
                    while 2 * d <= w:
                        dst, dstb = nxt[ni % 2]
                        ni += 1
                        TT("dve", dst[:, 0:ln - d], cur[:, d:ln], cur[:, 0:ln - d], ALU.add, (curb,), (dstb,))
                        cur, curb, ln, e0 = dst, dstb, ln - d, e0 + d
                        d *= 2
                    o = 15 - e0
                    STT("dve", pl[:, :N], cur[:, o:o + N], 1.0 / w, pbuf[:, 15:15 + N], ALU.mult, ALU.subtract,
                        (curb, pbb_), (plb,))
                    if ti == 0:
                        tf, tfb = tmp.next()
                        TT("dve", tf[:, 0:16], cur[:, o:o + 16], invc[:, g, :], ALU.mult, (curb,), (tfb,))
                        TT("dve", pl[:, 0:16], tf[:, 0:16], pbuf[:, 15:31], ALU.subtract, (tfb, pbb_), (plb,))
                    CP("pool", halo_p[l][:, j, :], pbuf[:, N:N + 15], (pbb_,), (halo_pb[l][j],))
                else:
                    spin, spinb, _ = T["sp_in"][jj]
                    spo, spob, cho = T["sp_out"][jj]
                    pn, pnb = tmp.next()
                    ACT(pn[:, :N], psp[:, :N], AF.Copy, (ppb,), (pnb,))
                    rd, rdb = tmp.next()
                    RED(rd[:, :N], spin[:, :, 16 - w:15], (spinb,), (rdb,))
                    TT("dve", rd[:, :N], rd[:, :N], pn[:, :N], ALU.add, (rdb, pnb), (rdb,))
                    STT("dve", pl[:, :N], rd[:, :N], 1.0 / w, pn[:, :N], ALU.mult, ALU.subtract, (rdb, pnb), (plb,))
                    CP("pool", spo[:, :, 0:14], spin[:, :, 1:15], (spinb,), (spob,))
                    CP("pool", spo[:, :, 14], pn[:, :N], (pnb,), (spob,))
                    DMA("sp", dr["pps"][l][:, j], spo, (spob,), (), cho)
            for jj in range(2):
                j = 2 * g + jj
                psg, pgb = proj(T, wg2, wg2b, jj, xn, xnb)
                t2, t2b = T["c1"][jj]
                ACT(t2[:, :N], psg[:, :N], AF.Tanh, (pgb,), (t2b,),
                    bias=hgb[:, l * 32 + 16 + j:l * 32 + 16 + j + 1], scale=0.5)
            for jj in range(2):
                j = 2 * g + jj
                t2, t2b = T["c1"][jj]
                psc2, pc2b = T["auxrot"].next()
                for kc in range(2):
                    MM(psc2[:, :N], pw[:, g * 2 + kc, jj * 128:(jj + 1) * 128], T["plbf"][kc][0][:, :N],
                       kc == 0, kc == 1, (T["plbf"][kc][1], pwb), (pc2b,))
                yc, ycb = tmp.next()
                STT("dve", yc[:, :N], t2[:, :N], 1.0, psc2[:, :N], ALU.add, ALU.mult, (t2b, pc2b), (ycb,))
                if sample:
                    STT("dve", m2[:, j, :N], yc[:, :N], P("psc", l * 8 + j), m2[:, j, :N], ALU.mult, ALU.add,
                        (ycb, m2b[j]), (m2b[j],))
                else:
                    STT("dve", mb[:, j, :N], yc[:, :N], P("psc", l * 8 + j), m2[:, j, :N], ALU.mult, ALU.add,
                        (ycb, m2b[j]), (mbb[j],))
        if sample:
            yield ("join",)
            yms, ymsb = T["yms"]
            for j in range(8):
                TT("dve", mb[:, j, :N], m2[:, j, :N], yms[:, j, :], ALU.add, (m2b[j], ymsb), (mbb[j],))
        stop("C")

        for mi in range(4):
            wo, wob = yield ("w", "o%d_%d" % (l, mi), dr["w_o"][l][:, :, 256 * mi:256 * mi + 256])
            for jj in range(2):
                j = 2 * mi + jj
                ps, pb = proj(T, wo, wob, jj, mb, mbb)
                STT("dve", x[:, j, :N], ps[:, :N], 0.5, x[:, j, :N], ALU.mult, ALU.add, (pb, xb[j]), (xb[j],))
        stop("WO")

        rmsnorm(x, xb, N, "nffn", l * 8, xn, xnb, T["sq"], T["sd"], T["sdb"], T["rstd"], T["rstdb"], rot=T["auxrot"])
        for half in range(2):
            for s8 in range(8):
                s = half * 8 + s8
                w1, w1b = yield ("w", "f1_%d_%d" % (l, s), dr["w_ff1"][l][:, :, 256 * s:256 * s + 256])
                for jj in range(2):
                    ps, pb = proj(T, w1, w1b, jj, xn, xnb)
                    r, rb = T["bft"].next()
                    ACT(r[:, :N], ps[:, :N], AF.Relu, (pb,), (rb,))
                    hi = s8 * 2 + jj
                    TT("dve", h[:, hi, :N], r[:, :N], r[:, :N], ALU.mult, (rb,), (hb[hi],))
            for mi in range(4):
                if sample:
                    for kgl in range(2):
                        r0 = half * 16 + kgl * 8
                        w2_, w2b_ = yield ("w", "f2_%d_%d_%d" % (l, r0, mi),
                                           dr["w_ff2"][l][:, r0:r0 + 8, 256 * mi:256 * mi + 256])
                        for jj in range(2):
                            j = 2 * mi + jj
                            ps, pb = T["mmrot"].next()
                            for k in range(8):
                                MM(ps[:, :N], w2_[:, k, jj * 128:(jj + 1) * 128], h[:, kgl * 8 + k, :N],
                                   k == 0, k == 7, (w2b_, hb[kgl * 8 + k]), (pb,))
                            TT("dve", x[:, j, :N], x[:, j, :N], ps[:, :N], ALU.add, (xb[j], pb), (xb[j],))
                    continue
                pss2 = [T["mmrot"].next(), T["mmrot"].next()]
                for kgl in range(2):
                    r0 = half * 16 + kgl * 8
                    w2_, w2b_ = yield ("w", "f2_%d_%d_%d" % (l, r0, mi),
                                       dr["w_ff2"][l][:, r0:r0 + 8, 256 * mi:256 * mi + 256])
                    for jj in range(2):
                        ps, pb = pss2[jj]
                        for k in range(8):
                            MM(ps[:, :N], w2_[:, k, jj * 128:(jj + 1) * 128], h[:, kgl * 8 + k, :N],
                               kgl == 0 and k == 0, kgl == 1 and k == 7, (w2b_, hb[kgl * 8 + k]), (pb,))
                for jj in range(2):
                    j = 2 * mi + jj
                    ps, pb = pss2[jj]
                    TT("dve", x[:, j, :N], x[:, j, :N], ps[:, :N], ALU.add, (xb[j], pb), (xb[j],))

    def run_layers(ctxs):
        l_ = ctxs[0][1]
        DMA("pool", pw, dr["pw"][:, l_ * 8:(l_ + 1) * 8, :], (), (pwb,), "pw")
        gens = [layer(*c) for c in ctxs]
        reqs = [None] * len(gens)
        done = [False] * len(gens)
        bg = []

        def advance(i, val):
            try:
                reqs[i] = gens[i].send(val) if val is not None or reqs[i] is not None else next(gens[i])
            except StopIteration:
                done[i] = True
                reqs[i] = None

        def bg_step(n):
            for _ in range(n):
                if not bg:
                    return
                b = bg[0]
                if b[1] is None:
                    try:
                        b[1] = next(b[0])
                    except StopIteration:
                        bg.pop(0)
                        continue
                slot = W.next(b[1][2])
                try:
                    b[1] = b[0].send(slot)
                except StopIteration:
                    bg.pop(0)

        for i in range(len(gens)):
            try:
                reqs[i] = next(gens[i])
            except StopIteration:
                done[i] = True
        while not all(done):
            progressed = False
            for i in range(len(gens)):
                if done[i]:
                    continue
                r = reqs[i]
                if r[0] == "spawn":
                    bg.append([r[1], None])
                    advance(i, 0)
                    progressed = True
                elif r[0] == "join":
                    while bg:
                        bg_step(1)
                    advance(i, 0)
                    progressed = True
            if progressed:
                continue
            active = [i for i in range(len(gens)) if not done[i]]
            keys = set(reqs[i][1] for i in active)
            assert len(keys) == 1, keys
            slot = W.next(reqs[active[0]][2])
            for i in active:
                advance(i, slot)
            bg_step(1)
        while bg:
            bg_step(1)

    A.off = PHASE_BASE
    Tp = make_ctx(NT, False)
    Ts = make_ctx(NS, True)
    for ti in range(4):
        Tp["mmrot"], Tp["auxrot"] = (mm_rot, aux_rot) if ti == 3 else (mm_rot4, aux_rot4)
        DMA("sp", Tp["x"], dr["xT"][:, :, ti * NT:(ti + 1) * NT], (), Tp["xb"], "x")
        if ti == 3:
            DMA("sp", Ts["x"], dr["xsT"], (), Ts["xb"], "xs")
        for l in range(2):
            if ti == 3:
                run_layers([(Tp, l, ti), (Ts, l, 4)])
            else:
                run_layers([(Tp, l, ti)])
            stop("L")
        rmsnorm(Tp["x"], Tp["xb"], NT, "nfin", 0, Tp["m2"], Tp["m2b"], Tp["sq"], Tp["sd"], Tp["sdb"], Tp["rstd"],
                Tp["rstdb"])
        DMA("sp", dr["yT"][:, :, ti * NT:(ti + 1) * NT], Tp["m2"], Tp["m2b"], (), "y")
    for l in range(2):
        TS1("dve", cbp_r[l], cbp_r[l], 0.5, ALU.mult, (cbp_b[l],), (cbp_b[l],))
        DMA("sp", dr["cbp"][l], cbp_r[l], (cbp_b[l],), (), "o_cbp%d" % l)
        DMA("sp", dr["cap"][l], halo_a[l], halo_ab[l], (), "o_cap%d" % l)
        DMA("sp", dr["pp"][l], halo_p[l], halo_pb[l], (), "o_pp%d" % l)
    rmsnorm(Ts["x"], Ts["xb"], NS, "nfin", 0, Ts["m2"], Ts["m2b"], Ts["sq"], Ts["sd"], Ts["sdb"], Ts["rstd"],
            Ts["rstdb"], rot=s_rot)
    DMA("sp", dr["ysT"], Ts["m2"], Ts["m2b"], (), "ys")
    S.barrier()


def _fm(a):
    lead = a.shape[:-1]
    r = a.reshape(lead + (8, 128))
    return np.moveaxis(r, -1, 0)


def _pack_params(norm_mix, norm_mem, norm_ffn, norm_final, conv_a_w, conv_b_w, conv_b_bias, ln_b_gain, ln_b_bias,
                 pool_scale, gate_bias):
    prm = np.zeros((128, NPRM), np.float32)

    def put(name, arr):
        flat = np.ascontiguousarray(arr).reshape(128, -1)
        prm[:, PRM[name]:PRM[name] + flat.shape[1]] = flat

    put("nmix", _fm(norm_mix))
    put("nmem", _fm(norm_mem))
    put("nffn", _fm(norm_ffn))
    put("nfin", _fm(norm_final))
    put("caw", np.transpose(_fm(conv_a_w), (0, 1, 3, 2)))
    put("cbw", np.transpose(_fm(conv_b_w), (0, 1, 3, 2)))
    put("cbb", _fm(conv_b_bias))
    put("lng", _fm(ln_b_gain))
    put("lnb", _fm(ln_b_bias))
    put("psc", _fm(pool_scale))
    put("gb", np.transpose(gate_bias.reshape(2, 32, 128), (2, 0, 1)))
    return prm


def _wr(w, kc):
    L, K, E = w.shape
    return np.ascontiguousarray(np.transpose(w.reshape(L, kc, 128, E), (0, 2, 1, 3)))


def _make_in_maps(A_):
    (x_prompt, x_sample, mem_prompt, cache_mem_k, cache_mem_v, state_conv_a, state_conv_b, state_pool, norm_mix,
     norm_mem, w_kv, w_in, conv_a_w, conv_b_w, conv_b_bias, ln_b_gain, ln_b_bias, pool_w, pool_scale, gate_bias,
     w_o, norm_ffn, w_ff1, w_ff2, norm_final) = A_
    n = 8
    prm = _pack_params(norm_mix, norm_mem, norm_ffn, norm_final, conv_a_w, conv_b_w, conv_b_bias, ln_b_gain,
                       ln_b_bias, pool_scale, gate_bias)
    shared = dict(
        w_in=_wr(w_in, 8), w_kv=_wr(w_kv, 8), w_o=_wr(w_o, 8), w_ff1=_wr(w_ff1, 8), w_ff2=_wr(w_ff2, 32),
        pw=np.ascontiguousarray(np.transpose(pool_w.reshape(2, 4, 2, 128, 256), (3, 0, 1, 2, 4)).reshape(128, 16, 256)),
        prm=prm, ident=np.eye(128, dtype=np.float32),
        invc=np.ascontiguousarray(np.broadcast_to(
            np.array([[1.0 / min(t + 1, w) for t in range(16)] for w in WINS], np.float32)[None], (128, 4, 16))),
    )
    in_maps = []
    for i in range(n):
        sl = slice(NS * i, NS * (i + 1))
        m = dict(shared)
        m["xT"] = np.ascontiguousarray(_fm(x_prompt[i]))
        m["xT"] = np.ascontiguousarray(np.transpose(m["xT"], (0, 2, 1)))
        m["xsT"] = np.ascontiguousarray(np.transpose(_fm(x_sample[sl, 0, :]), (0, 2, 1)))
        m["memT"] = np.ascontiguousarray(np.transpose(_fm(mem_prompt[i]), (0, 2, 1)))
        k = cache_mem_k[:, sl].reshape(2, NS, 256, 8, 128)
        m["kT"] = np.ascontiguousarray(np.transpose(k, (0, 1, 4, 3, 2)))
        v = cache_mem_v[:, sl].reshape(2, NS, 2, 128, 1024)
        m["vv"] = np.ascontiguousarray(np.transpose(v, (0, 1, 3, 2, 4)))
        for nm, stt in (("sa", state_conv_a), ("sb", state_conv_b), ("sp", state_pool)):
            s_ = stt[:, sl]
            r = s_.reshape(2, NS, s_.shape[2], 8, 128)
            m[nm] = np.ascontiguousarray(np.transpose(r, (0, 4, 3, 1, 2)))
        in_maps.append(m)

    return in_maps


_NC_CACHE = {}


def kernel(x_prompt, x_sample, mem_prompt, cache_mem_k, cache_mem_v, state_conv_a, state_conv_b, state_pool,
           norm_mix, norm_mem, w_kv, w_in, conv_a_w, conv_b_w, conv_b_bias, ln_b_gain, ln_b_bias, pool_w,
           pool_scale, gate_bias, w_o, norm_ffn, w_ff1, w_ff2, norm_final):
    f = lambda a: np.asarray(a, dtype=np.float32)
    (x_prompt, x_sample, mem_prompt, cache_mem_k, cache_mem_v, state_conv_a, state_conv_b, state_pool, norm_mix,
     norm_mem, w_kv, w_in, conv_a_w, conv_b_w, conv_b_bias, ln_b_gain, ln_b_bias, pool_w, pool_scale, gate_bias,
     w_o, norm_ffn, w_ff1, w_ff2, norm_final) = map(f, (
        x_prompt, x_sample, mem_prompt, cache_mem_k, cache_mem_v, state_conv_a, state_conv_b, state_pool, norm_mix,
        norm_mem, w_kv, w_in, conv_a_w, conv_b_w, conv_b_bias, ln_b_gain, ln_b_bias, pool_w, pool_scale, gate_bias,
        w_o, norm_ffn, w_ff1, w_ff2, norm_final))
    n = 8
    in_maps = _make_in_maps((x_prompt, x_sample, mem_prompt, cache_mem_k, cache_mem_v, state_conv_a, state_conv_b,
                             state_pool, norm_mix, norm_mem, w_kv, w_in, conv_a_w, conv_b_w, conv_b_bias, ln_b_gain,
                             ln_b_bias, pool_w, pool_scale, gate_bias, w_o, norm_ffn, w_ff1, w_ff2, norm_final))
    if "nc" not in _NC_CACHE:
        _NC_CACHE["nc"] = build_program()
    nc = _NC_CACHE["nc"]
    res = run_bass_kernel_spmd(nc, in_maps, core_ids=list(range(n)))
    return _assemble(res.results)


def _assemble(R):
    n = 8

    def unfm(a):
        return np.transpose(a, (2, 1, 0)).reshape(a.shape[2], 1024)

    y_prompt = np.stack([unfm(R[i]["yT"]) for i in range(n)])
    y_sample = np.concatenate([unfm(R[i]["ysT"]) for i in range(n)])[:, None, :]
    mem_k = np.stack([np.stack([unfm(R[i]["mk"][l]).reshape(256, 4, 256) for i in range(n)]) for l in range(2)])
    mem_v = np.stack([np.stack([R[i]["mv"][l].reshape(256, 4, 256) for i in range(n)]) for l in range(2)])

    def pst(name):
        return np.stack([np.stack([unfm(R[i][name][l]) for i in range(n)]) for l in range(2)])

    def sst(name):
        outs = []
        for l in range(2):
            per = []
            for i in range(n):
                a = R[i][name][l]
                per.append(np.transpose(a, (2, 3, 1, 0)).reshape(NS, a.shape[3], 1024))
            outs.append(np.concatenate(per))
        return np.stack(outs)

    outs = (y_prompt, y_sample, mem_k, mem_v, pst("cap"), pst("cbp"), pst("pp"), sst("cas"), sst("cbs"), sst("pps"))
    return tuple(np.ascontiguousarray(o, dtype=np.float32) for o in outs)
```

```python
import contextlib
import numpy as np
import concourse.bass as bass
import concourse.mybir as mybir
from concourse.bass_utils import run_bass_kernel_spmd

F32 = mybir.dt.float32
BF16 = mybir.dt.bfloat16
AF = mybir.ActivationFunctionType
ALU = mybir.AluOpType
AX = mybir.AxisListType

D = 1024
SEQ = 2048
NT = 512
NS = 16
NMEM = 256
DPROJ = 11264
DFF = 4096
EPS = 1e-6
WINS = (2, 4, 8, 16)
RING = 7
SLOTW = 1024

C_H, C_B, C_C, C_GA, C_GB, C_P, C_Q, C_G = 0, 1024, 2048, 3072, 4096, 5120, 6144, 7168

PRM = {}
_o = 0
for _n, _w in (("nmix", 16), ("nmem", 16), ("nffn", 16), ("nfin", 8), ("caw", 48), ("cbw", 496),
               ("cbb", 16), ("lng", 16), ("lnb", 16), ("psc", 16), ("gb", 64)):
    PRM[_n] = _o
    _o += _w
NPRM = _o


class Buf:
    __slots__ = ("name", "writer", "readers")

    def __init__(self, name=""):
        self.name = name
        self.writer = None
        self.readers = {}


class Op:
    __slots__ = ("eng", "fn", "chan", "chan_val", "pos", "signal", "count", "waits")

    def __init__(self, eng, fn, chan):
        self.eng = eng
        self.fn = fn
        self.chan = chan
        self.chan_val = 0
        self.pos = 0
        self.signal = False
        self.count = 0
        self.waits = []


ENGS = ("pe", "act", "dve", "pool", "sp")


class Sched:
    def __init__(self, dry):
        self.dry = dry
        self.ops = {e: [] for e in ENGS}
        self.known = {e: {} for e in ENGS}
        self.chan_n = {}
        self.last_compute = {}
        self.last_dma = {}

    def _add_deps(self, o, deps):
        eng = o.eng
        for d in deps:
            key = ("c", d.chan) if d.chan else ("e", d.eng)
            val = d.chan_val if d.chan else d.pos
            if self.known[eng].get(key, -1) >= val:
                continue
            self.known[eng][key] = val
            d.signal = True
            o.waits.append(d)

    def op(self, eng, fn, reads=(), writes=(), chan=None):
        if self.dry:
            return None
        o = Op(eng, fn, chan)
        o.pos = len(self.ops[eng])
        raw = []
        other = []
        for b in reads:
            if b.writer is not None:
                raw.append(b.writer)
        for b in writes:
            if b.writer is not None:
                other.append(b.writer)
            other.extend(b.readers.values())
        deps = []
        seen = set()
        for lst, is_raw in ((raw, True), (other, False)):
            for d in lst:
                if id(d) in seen:
                    continue
                if d.chan is None and chan is None and d.eng == eng:
                    if eng == "pe":
                        continue
                seen.add(id(d))
                deps.append(d)
        self._add_deps(o, deps)
        if chan is not None:
            n = self.chan_n.get(chan, 0) + 1
            self.chan_n[chan] = n
            o.chan_val = 16 * n
            self.last_dma[chan] = o
        else:
            self.last_compute[eng] = o
        rkey = eng if chan is None else (chan, o.chan_val)
        for b in reads:
            b.readers[rkey] = o
        for b in writes:
            b.writer = o
            b.readers = {}
        self.ops[eng].append(o)
        return o

    def barrier(self):
        if self.dry:
            return
        deps = list(self.last_compute.values()) + list(self.last_dma.values())
        for e in ENGS:
            o = Op(e, None, None)
            o.pos = len(self.ops[e])
            self._add_deps(o, [d for d in deps if not (d.chan is None and d.eng == e)])
            self.ops[e].append(o)

    def finalize(self):
        for e in ENGS:
            c = 0
            for o in self.ops[e]:
                if o.chan is None and o.signal and o.fn is not None:
                    c += 1
                o.count = c


class Arena:
    def __init__(self, tensor, base, limit):
        self.t = tensor
        self.off = base
        self.limit = limit

    def alloc(self, shape, dtype):
        n = 1
        for s in shape[1:]:
            n *= s
        words = n if dtype == F32 else (n + 1) // 2
        words = (words + 7) // 8 * 8
        a = self.off
        self.off += words
        assert self.off <= self.limit, ("SBUF arena overflow", self.off, self.limit)
        ap = self.t[:, a:a + words]
        if dtype == BF16:
            ap = ap.bitcast(BF16)
        ap = ap[:, 0:n]
        if len(shape) == 3:
            ap = ap.rearrange("p (a b) -> p a b", a=shape[1])
        elif len(shape) == 4:
            ap = ap.rearrange("p (a b c) -> p a b c", a=shape[1], b=shape[2])
        return ap


class Rot:
    def __init__(self, items):
        self.items = items
        self.i = 0

    def next(self):
        r = self.items[self.i % len(self.items)]
        self.i += 1
        return r


class WStream:
    def __init__(self, S, slots, recorded):
        self.S = S
        self.slots = slots
        self.rec = recorded
        self.log = []
        self.issued = 0
        self.i = 0
        self.la = 2

    def _issue(self, n):
        src = self.rec[n]
        ap, buf = self.slots[n % RING]
        dst = self._view(ap, src.shape)
        self.S.op("pool", (lambda e, dst=dst, src=src: e.dma_start(out=dst, in_=src)),
                  reads=(), writes=(buf,), chan="w%d" % (n % RING))

    @staticmethod
    def _view(ap, shape):
        return ap.rearrange("p (a b) -> p a b", a=shape[1])

    def next(self, src):
        if self.rec is None:
            self.log.append(src)
            ap, buf = self.slots[len(self.log) % RING]
            return self._view(ap, src.shape), buf
        i = self.i
        self.i += 1
        while self.issued < min(len(self.rec), i + RING - self.la):
            self._issue(self.issued)
            self.issued += 1
        ap, buf = self.slots[i % RING]
        return self._view(ap, src.shape), buf


def build_program():
    nc = bass.Bass("TRN2", target_bir_lowering=False)

    def din(name, shape):
        return nc.dram_tensor(name, list(shape), F32, kind="ExternalInput").ap()

    def dout(name, shape):
        return nc.dram_tensor(name, list(shape), F32, kind="ExternalOutput").ap()

    dr = dict(
        xT=din("xT", (128, 8, SEQ)), xsT=din("xsT", (128, 8, NS)), memT=din("memT", (128, 8, NMEM)),
        kT=din("kT", (2, NS, 128, 8, 256)), vv=din("vv", (2, NS, 128, 2, 1024)),
        sa=din("sa", (2, 128, 8, NS, 2)), sb=din("sb", (2, 128, 8, NS, 30)), sp=din("sp", (2, 128, 8, NS, 15)),
        w_in=din("w_in", (2, 128, 8, DPROJ)), w_kv=din("w_kv", (2, 128, 8, 2048)),
        w_o=din("w_o", (2, 128, 8, D)), w_ff1=din("w_ff1", (2, 128, 8, DFF)), w_ff2=din("w_ff2", (2, 128, 32, D)),
        pw=din("pw", (128, 16, 256)), prm=din("prm", (128, NPRM)), ident=din("ident", (128, 128)),
        invc=din("invc", (128, 4, 16)),
        yT=dout("yT", (128, 8, SEQ)), ysT=dout("ysT", (128, 8, NS)),
        mk=dout("mk", (2, 128, 8, NMEM)), mv=dout("mv", (2, NMEM, D)),
        cap=dout("cap", (2, 128, 8, 2)), cbp=dout("cbp", (2, 128, 8, 30)), pp=dout("pp", (2, 128, 8, 15)),
        cas=dout("cas", (2, 128, 8, NS, 2)), cbs=dout("cbs", (2, 128, 8, NS, 30)), pps=dout("pps", (2, 128, 8, NS, 15)),
    )

    import os
    ARENA_WORDS = int(os.environ.get("ARENA_WORDS", "51200"))
    with contextlib.ExitStack() as st:
        arena_t = st.enter_context(nc.sbuf_tensor("arena", [128, ARENA_WORDS], F32))
        psum = [st.enter_context(nc.psum_tensor("ps%d" % i, [128, 512], F32)) for i in range(8)]

        S = None
        rec = None
        for run in ("dry", "real"):
            S = Sched(dry=(run == "dry"))
            W = emit_all(nc, S, dr, arena_t, ARENA_WORDS, psum, rec)
            if run == "dry":
                rec = W.log
        S.finalize()

        eng_sem = {e: st.enter_context(nc.semaphore("sem_" + e)) for e in ENGS}
        chan_sem = {c: st.enter_context(nc.semaphore("ch_" + c)) for c in S.chan_n}

        def emit(e, name):
            for o in S.ops[name]:
                for d in o.waits:
                    if d.chan is not None:
                        e.wait_ge(chan_sem[d.chan], d.chan_val)
                    else:
                        e.wait_ge(eng_sem[d.eng], d.count)
                if o.fn is None:
                    continue
                ins = o.fn(e)
                if o.chan is not None:
                    ins.then_inc(chan_sem[o.chan], 16)
                elif o.signal:
                    ins.then_inc(eng_sem[name], 1)

        with nc.Block() as block:
            @block.tensor
            def _(e):
                emit(e, "pe")

            @block.scalar
            def _(e):
                emit(e, "act")

            @block.vector
            def _(e):
                emit(e, "dve")

            @block.gpsimd
            def _(e):
                emit(e, "pool")

            @block.sync
            def _(e):
                emit(e, "sp")
    return nc


class _Stop(Exception):
    pass


KSTOP = [None]
MARKS = []
DIAG_ENG = "pool"


def emit_all(nc, S, dr, arena_t, ARENA_WORDS, psum, rec):
    holder = {}
    try:
        _emit_all(nc, S, dr, arena_t, ARENA_WORDS, psum, rec, holder)
    except _Stop:
        S.barrier()
    return holder["W"]


def _emit_all(nc, S, dr, arena_t, ARENA_WORDS, psum, rec, holder):
    A = Arena(arena_t, 0, ARENA_WORDS)

    def stop(stage):
        if not S.dry:
            MARKS.append((stage, len(S.ops["pe"])))
        if KSTOP[0] == stage:
            raise _Stop()

    def ACT(out, in_, func, reads, writes, bias=None, scale=None):
        kw = {}
        if bias is not None:
            kw["bias"] = bias
        if scale is not None:
            kw["scale"] = scale
        S.op("act", lambda e: e.activation(out=out, in_=in_, func=func, **kw), reads, writes)

    def STT(eng, out, in0, scalar, in1, op0, op1, reads, writes):
        S.op(eng, lambda e: e.scalar_tensor_tensor(out=out, in0=in0, scalar=scalar, in1=in1, op0=op0, op1=op1),
             reads, writes)

    def TT(eng, out, in0, in1, op, reads, writes):
        S.op(eng, lambda e: e.tensor_tensor(out=out, in0=in0, in1=in1, op=op), reads, writes)

    def TS1(eng, out, in_, scalar, op, reads, writes):
        S.op(eng, lambda e: e.tensor_single_scalar(out=out, in_=in_, scalar=scalar, op=op), reads, writes)

    def TS2(eng, out, in0, s1, s2, op0, op1, reads, writes):
        S.op(eng, lambda e: e.tensor_scalar(out=out, in0=in0, scalar1=s1, scalar2=s2, op0=op0, op1=op1), reads, writes)

    def CP(eng, out, in_, reads, writes):
        S.op(eng, lambda e: e.tensor_copy(out=out, in_=in_), reads, writes)

    def RCP(out, in_, reads, writes):
        S.op("dve", lambda e: e.reciprocal(out=out, in_=in_), reads, writes)

    def RED(out, in_, reads, writes):
        S.op("dve", lambda e: e.tensor_reduce(out=out, in_=in_, axis=AX.X, op=ALU.add), reads, writes)

    def MM(out, lhsT, rhs, start, stop, reads, writes):
        S.op("pe", lambda e: e.matmul(out, lhsT, rhs, start=start, stop=stop), reads, writes)

    def DMA(eng, out, in_, reads, writes, chan):
        S.op(eng, lambda e: e.dma_start(out=out, in_=in_), reads, writes, chan=chan)

    def MEMSET(eng, ap, val, writes):
        S.op(eng, lambda e: e.memset(ap, val), (), writes)

    psb_ = [Buf("psb%d" % i) for i in range(8)]
    mm_rot = Rot([(psum[i], psb_[i]) for i in range(3)])
    aux_rot = Rot([(psum[i], psb_[i]) for i in range(3, 6)])
    mm_rot4 = Rot([(psum[i], psb_[i]) for i in (0, 1, 2, 6)])
    aux_rot4 = Rot([(psum[i], psb_[i]) for i in (3, 4, 5, 7)])
    ps6b, ps7b = psb_[6], psb_[7]
    _sreg = []
    for i in range(12):
        _sreg.append((psum[6][:, 16 * i:16 * i + 16], ps6b))
        _sreg.append((psum[7][:, 320 + 16 * i:320 + 16 * i + 16], ps7b))
    s_rot = Rot(_sreg)

    ring = []
    for i in range(RING):
        ap = A.alloc([128, 2048], BF16)
        ring.append((ap, Buf("slot%d" % i)))
    W = WStream(S, ring, rec)
    holder["W"] = W

    prm = A.alloc([128, NPRM], F32)
    hgb = A.alloc([128, 64], F32)
    ident = A.alloc([128, 128], BF16)
    ones = A.alloc([128, 128], BF16)
    epsT = A.alloc([128, 1], F32)
    invc = A.alloc([128, 4, 16], F32)
    pw = A.alloc([128, 8, 256], BF16)
    pwb = Buf("pw")
    KT = [A.alloc([128, 8, 256], BF16) for _ in range(2)]
    VV = [A.alloc([128, 2, 1024], BF16) for _ in range(2)]
    halo_a = [A.alloc([128, 8, 2], F32) for _ in range(2)]
    halo_b = [A.alloc([128, 8, 30], BF16) for _ in range(2)]
    halo_p = [A.alloc([128, 8, 15], F32) for _ in range(2)]
    cbp_r = [A.alloc([128, 8, 30], F32) for _ in range(2)]
    halo_ab = [[Buf() for _ in range(8)] for _ in range(2)]
    halo_bb = [[Buf() for _ in range(8)] for _ in range(2)]
    halo_pb = [[Buf() for _ in range(8)] for _ in range(2)]
    cbp_b = [Buf() for _ in range(2)]
    KTb = [Buf(), Buf()]
    VVb = [Buf(), Buf()]
    cst = Buf("const")
    PHASE_BASE = A.off

    def P(name, idx, width=1):
        o = PRM[name] + idx
        return prm[:, o:o + width]

    prmb = Buf("prm")
    DMA("sp", prm, dr["prm"], (), (prmb,), "setup")
    DMA("sp", invc, dr["invc"], (), (Buf(),), "setup2")
    DMA("pool", ident, dr["ident"], (), (Buf(),), "setup3")
    MEMSET("dve", ones, 1.0, (Buf(),))
    MEMSET("dve", epsT, EPS, (Buf(),))
    for l in range(2):
        MEMSET("dve", halo_a[l], 0.0, halo_ab[l])
        MEMSET("dve", halo_b[l], 0.0, halo_bb[l])
        MEMSET("dve", halo_p[l], 0.0, halo_pb[l])
    TS1("dve", hgb, prm[:, PRM["gb"]:PRM["gb"] + 64], 0.5, ALU.mult, (prmb,), (Buf(),))
    S.barrier()
    stop("setup")

    def rmsnorm(x, xb, N, gain_name, gain_idx, out, outb, sqrot, sd, sdb, rstd, rstdb, rot=None):
        ps, pb = (rot or aux_rot).next()
        for c in range(8):
            sq, sqb = sqrot.next()
            ACT(sq[:, :N], x[:, c, :N], AF.Square, (xb[c],), (sqb,))
            MM(ps[:, :N], ones, sq[:, :N], c == 0, c == 7, (sqb,), (pb,))
        ACT(sd[:, :N], ps[:, :N], AF.Sqrt, (pb,), (sdb,), bias=epsT[:, 0:1], scale=1.0 / D)
        RCP(rstd[:, :N], sd[:, :N], (sdb,), (rstdb,))
        for c in range(8):
            STT("dve", out[:, c, :N], x[:, c, :N], P(gain_name, gain_idx + c), rstd[:, :N], ALU.mult, ALU.mult,
                (xb[c], rstdb), (outb[c],))

    A.off = PHASE_BASE
    mem = A.alloc([128, 8, NMEM], F32)
    memb = [Buf() for _ in range(8)]
    memn = A.alloc([128, 8, NMEM], BF16)
    memnb = [Buf() for _ in range(8)]
    kst = A.alloc([128, 8, NMEM], F32)
    kstb = Buf()
    vst = A.alloc([128, 2, 1024], F32)
    vstb = Buf()
    sqr = Rot([(A.alloc([128, 512], BF16), Buf()) for _ in range(2)])
    sd0 = A.alloc([128, 512], F32)
    rs0 = A.alloc([128, 512], F32)
    sd0b, rs0b = Buf(), Buf()
    DMA("sp", mem, dr["memT"], (), memb, "mem")
    for l in range(2):
        stop("kv_load")
        rmsnorm(mem, memb, NMEM, "nmem", l * 8, memn, memnb, sqr, sd0, sd0b, rs0, rs0b)
        stop("kv_norm")
        for e2 in range(4):
            wk, wkb = W.next(dr["w_kv"][l][:, :, 256 * e2:256 * e2 + 256])
            for jj in range(2):
                e_ = 2 * e2 + jj
                ps, pb = mm_rot.next()
                for k in range(8):
                    MM(ps[:, :NMEM], wk[:, k, jj * 128:(jj + 1) * 128], memn[:, k, :], k == 0, k == 7,
                       (wkb, memnb[k]), (pb,))
                ACT(kst[:, e_, :], ps[:, :NMEM], AF.Copy, (pb,), (kstb,))
                CP("dve", KT[l][:, e_, :], kst[:, e_, :], (kstb,), (KTb[l],))
        stop("kv_k")
        DMA("sp", dr["mk"][l], kst, (kstb,), (), "mk")
        stop("kv_mk")
        for s in range(4):
            wv, wvb = W.next(dr["w_kv"][l][:, :, 1024 + 256 * s:1024 + 256 * s + 256])
            for tc in range(2):
                ps, pb = mm_rot.next()
                for k in range(8):
                    MM(ps[:, :256], memn[:, k, tc * 128:(tc + 1) * 128], wv[:, k, :], k == 0, k == 7,
                       (wvb, memnb[k]), (pb,))
                ACT(vst[:, tc, 256 * s:256 * s + 256], ps[:, :256], AF.Copy, (pb,), (vstb,))
                CP("dve", VV[l][:, tc, 256 * s:256 * s + 256], vst[:, tc, 256 * s:256 * s + 256], (vstb,), (VVb[l],))
        stop("kv_v")
        DMA("sp", dr["mv"][l].rearrange("(tc p) e -> p tc e", p=128), vst, (vstb,), (), "mv")
        stop("kv_mv")
    S.barrier()
    stop("kv")

    def make_ctx(N, sample):
        T = {}
        T["N"] = N
        T["sample"] = sample
        T["mmrot"] = s_rot if sample else mm_rot
        T["auxrot"] = s_rot if sample else aux_rot
        NM = N
        T["x"] = A.alloc([128, 8, NM], F32)
        T["xb"] = [Buf() for _ in range(8)]
        T["xn"] = A.alloc([128, 8, NM], BF16)
        T["xnb"] = [Buf() for _ in range(8)]
        T["z"] = A.alloc([128, 8, NM], F32)
        T["zb"] = [Buf() for _ in range(8)]
        T["m2"] = A.alloc([128, 8, NM], F32)
        T["m2b"] = [Buf() for _ in range(8)]
        T["mb"] = A.alloc([128, 8, NM], BF16)
        T["mbb"] = [Buf() for _ in range(8)]
        T["h"] = T["z"].rearrange("p a b -> p (a b)").bitcast(BF16).rearrange("p (a b) -> p a b", a=16)
        T["hb"] = [T["zb"][i // 2] for i in range(16)]
        T["tmp"] = Rot([(A.alloc([128, NM], F32), Buf()) for _ in range(5)])
        T["bft"] = Rot([(A.alloc([128, NM], BF16), Buf()) for _ in range(3)])
        T["sq"] = Rot([(A.alloc([128, NM], BF16), Buf()) for _ in range(2)])
        for grp in (("rstd", "lnrstd", "rden"), ("sd", "lnnmr")):
            ap_, b_ = A.alloc([128, NM], F32), Buf()
            for nm in grp:
                T[nm] = ap_
                T[nm + "b"] = b_
        T["c1"] = [(A.alloc([128, NM], F32), Buf()) for _ in range(2)]
        T["plbf"] = [(A.alloc([128, NM], BF16), Buf()) for _ in range(2)]
        T["qbf"] = [(A.alloc([128, NM], BF16), Buf()) for _ in range(2)]
        T["ebf"] = [(A.alloc([128, NM], BF16), Buf()) for _ in range(2)]
        if not sample:
            T["glu"] = Rot([(A.alloc([128, 30 + NM], BF16), Buf()) for _ in range(2)])
            T["diag"] = [(A.alloc([128, 31, 128], BF16), Buf()) for _ in range(2)]
            T["ub"] = Rot([(A.alloc([128, 2 + NM], F32), Buf()) for _ in range(2)])
            T["pb"] = Rot([(A.alloc([128, 15 + NM], F32), Buf()) for _ in range(2)])
            T["sA"] = (A.alloc([128, 16 + NM], F32), Buf())
            T["sB"] = (A.alloc([128, 16 + NM], F32), Buf())
        else:
            T["sast"] = A.alloc([128, 8, NS, 2], F32)
            T["nsa"] = A.alloc([128, 8, NS, 2], F32)
            T["sastb"], T["nsab"] = Buf(), Buf()
            T["sb_in"] = [(A.alloc([128, NS, 30], F32), Buf(), "i_sb%d" % i) for i in range(2)]
            T["sb_out"] = [(A.alloc([128, NS, 30], F32), Buf(), "o_sb%d" % i) for i in range(2)]
            T["sp_in"] = [(A.alloc([128, NS, 15], F32), Buf(), "i_sp%d" % i) for i in range(2)]
            T["sp_out"] = [(A.alloc([128, NS, 15], F32), Buf(), "o_sp%d" % i) for i in range(2)]
            T["qall"] = (A.alloc([128, 8, NS], BF16), Buf())
            T["t3all"] = (A.alloc([128, 8, NS], F32), Buf())
            T["yms"] = (A.alloc([128, 8, NS], F32), Buf())
            T["E"] = (A.alloc([128, 128], BF16), Buf())
            T["rdens"] = (A.alloc([128, 64], F32), Buf())
        return T

    def proj(T, wt, wtb, jj, src, srcb):
        N = T["N"]
        ps, pb = T["mmrot"].next()
        for k in range(8):
            MM(ps[:, :N], wt[:, k, jj * 128:(jj + 1) * 128], src[:, k, :N], k == 0, k == 7, (wtb, srcb[k]), (pb,))
        return ps, pb

    def sample_attention(T, l):
        N = T["N"]
        qall, qallb = T["qall"]
        t3all, t3allb = T["t3all"]
        yms, ymsb = T["yms"]
        E, Eb = T["E"]
        pssc, psscb = psum[7][:, 0:128], ps7b
        for b in range(NS):
            kt, ktb = yield ("kv", "k%d_%d" % (l, b), dr["kT"][l, b])
            for g in range(4):
                for kc in range(2):
                    col = b * 8 + g * 2 + kc
                    for dc in range(2):
                        MM(pssc[:, col:col + 1], kt[:, 2 * g + dc, kc * 128:(kc + 1) * 128],
                           qall[:, 2 * g + dc, b:b + 1], dc == 0, dc == 1, (ktb, qallb), (psscb,))
        ACT(E, pssc[:, 0:128], AF.Exp, (psscb,), (Eb,), scale=1.0 / 16.0)
        psden, psdenb = psum[7][:, 128:192], ps7b
        for kc in range(2):
            MM(psden[:, 0:64], ones, E[:, kc:128:2], kc == 0, kc == 1, (Eb,), (psdenb,))
        rdens, rdensb = T["rdens"]
        RCP(rdens, psden[:, 0:64], (psdenb,), (rdensb,))
        pso, psob = psum[7][:, 192:320], ps7b
        for b in range(NS):
            vt, vtb = yield ("kv", "v%d_%d" % (l, b), dr["vv"][l, b])
            for j in range(8):
                g = j // 2
                for kc in range(2):
                    ec = b * 8 + g * 2 + kc
                    MM(pso[:, b * 8 + j:b * 8 + j + 1], vt[:, kc, j * 128:(j + 1) * 128], E[:, ec:ec + 1],
                       kc == 0, kc == 1, (vtb, Eb), (psob,))
        for j in range(8):
            g = j // 2
            STT("dve", yms[:, j, :], t3all[:, j, :], 1.0, pso[:, j:128:8], ALU.add, ALU.mult, (t3allb, psob), (ymsb,))
            TT("dve", yms[:, j, :], yms[:, j, :], rdens[:, g:64:4], ALU.mult, (ymsb, rdensb), (ymsb,))

    def layer(T, l, ti):
        N = T["N"]
        sample = T["sample"]
        x, xb, xn, xnb = T["x"], T["xb"], T["xn"], T["xnb"]
        z, zb, m2, m2b, mb, mbb, h, hb = T["z"], T["zb"], T["m2"], T["m2b"], T["mb"], T["mbb"], T["h"], T["hb"]
        tmp = T["tmp"]
        win = dr["w_in"][l]

        def wreq(c0):
            return ("w", "in%d_%d" % (l, c0), win[:, :, c0:c0 + 256])

        def gate_tanh(psg, pgb, gi, j):
            t, tb_ = tmp.next()
            ACT(t[:, :N], psg[:, :N], AF.Tanh, (pgb,), (tb_,), bias=hgb[:, l * 32 + gi * 8 + j:l * 32 + gi * 8 + j + 1],
                scale=0.5)
            return t, tb_

        rmsnorm(x, xb, N, "nmix", l * 8, xn, xnb, T["sq"], T["sd"], T["sdb"], T["rstd"], T["rstdb"], rot=T["auxrot"])

        for g in range(4):
            wq, wqb = yield wreq(C_Q + 256 * g)
            wg3, wg3b = yield wreq(C_G + 3072 + 256 * g)
            for jj in range(2):
                j = 2 * g + jj
                psq_, pqb = proj(T, wq, wqb, jj, xn, xnb)
                if not sample:
                    ACT(T["qbf"][jj][0][:, :N], psq_[:, :N], AF.Copy, (pqb,), (T["qbf"][jj][1],))
                else:
                    ACT(T["qall"][0][:, j, :], psq_[:, :N], AF.Copy, (pqb,), (T["qall"][1],))
            for dc in range(2):
                j = 2 * g + dc
                psg, pgb = proj(T, wg3, wg3b, dc, xn, xnb)
                if not sample:
                    t3, t3b = T["c1"][dc]
                    ACT(t3[:, :N], psg[:, :N], AF.Tanh, (pgb,), (t3b,),
                        bias=hgb[:, l * 32 + 24 + j:l * 32 + 24 + j + 1], scale=0.5)
                else:
                    ACT(T["t3all"][0][:, j, :], psg[:, :N], AF.Tanh, (pgb,), (T["t3all"][1],),
                        bias=hgb[:, l * 32 + 24 + j:l * 32 + 24 + j + 1], scale=0.5)
            if not sample:
                for kc in range(2):
                    pss_, pssb_ = T["auxrot"].next()
                    for dc in range(2):
                        MM(pss_[:, :N], KT[l][:, 2 * g + dc, kc * 128:(kc + 1) * 128], T["qbf"][dc][0][:, :N],
                           dc == 0, dc == 1, (KTb[l], T["qbf"][dc][1]), (pssb_,))
                    ACT(T["ebf"][kc][0][:, :N], pss_[:, :N], AF.Exp, (pssb_,), (T["ebf"][kc][1],), scale=1.0 / 16.0)
                psd, psdb = T["auxrot"].next()
                for kc in range(2):
                    MM(psd[:, :N], ones, T["ebf"][kc][0][:, :N], kc == 0, kc == 1, (T["ebf"][kc][1],), (psdb,))
                RCP(T["rden"][:, :N], psd[:, :N], (psdb,), (T["rdenb"],))
                for dc in range(2):
                    j = 2 * g + dc
                    t3, t3b = T["c1"][dc]
                    pso, psob = T["auxrot"].next()
                    for kc in range(2):
                        MM(pso[:, :N], VV[l][:, kc, j * 128:(j + 1) * 128], T["ebf"][kc][0][:, :N], kc == 0, kc == 1,
                           (VVb[l], T["ebf"][kc][1]), (psob,))
                    ym, ymb = tmp.next()
                    STT("dve", ym[:, :N], t3[:, :N], 1.0, pso[:, :N], ALU.add, ALU.mult, (t3b, psob), (ymb,))
                    TT("dve", m2[:, j, :N], ym[:, :N], T["rden"][:, :N], ALU.mult, (ymb, T["rdenb"]), (m2b[j],))
        if sample:
            yield ("spawn", sample_attention(T, l))
        stop("M")

        for g in range(4):
            wa, wab = yield wreq(C_GA + 256 * g)
            wb_, wbb = yield wreq(C_GB + 256 * g)
            if sample:
                for jj in range(2):
                    sbin, sbinb, ch = T["sb_in"][jj]
                    DMA("sp", sbin, dr["sb"][l][:, 2 * g + jj], (), (sbinb,), ch)
            stash = []
            for jj in range(2):
                j = 2 * g + jj
                cbw = prm[:, PRM["cbw"] + (l * 8 + j) * 31:PRM["cbw"] + (l * 8 + j) * 31 + 31]
                psa, pab = proj(T, wa, wab, jj, xn, xnb)
                psb, pbb = proj(T, wb_, wbb, jj, xn, xnb)
                tb, tbb = tmp.next()
                ACT(tb[:, :N], psb[:, :N], AF.Tanh, (pbb,), (tbb,), scale=0.5)
                if not sample:
                    gl, glb = T["glu"].next()
                    CP("pool", gl[:, 0:30], halo_b[l][:, j, :], (halo_bb[l][j],), (glb,))
                    STT("dve", gl[:, 30:30 + N], tb[:, :N], 1.0, psa[:, :N], ALU.add, ALU.mult, (tbb, pab), (glb,))
                    if ti == 3:
                        STT("dve", cbp_r[l][:, j, :], tb[:, N - 30:N], 1.0, psa[:, N - 30:N], ALU.add, ALU.mult,
                            (tbb, pab), (cbp_b[l],))
                    dg, dgb = T["diag"][jj]
                    TT(("pool", "dve")[jj], dg, ident.unsqueeze(1).to_broadcast([128, 31, 128]),
                       cbw.unsqueeze(2).to_broadcast([128, 31, 128]), ALU.mult, (), (dgb,))
                    stash.append((gl, glb, dg, dgb))
                else:
                    sbin, sbinb, _ = T["sb_in"][jj]
                    sbo, sbob, cho = T["sb_out"][jj]
                    gs, gsb = tmp.next()
                    STT("dve", gs[:, :N], tb[:, :N], 1.0, psa[:, :N], ALU.add, ALU.mult, (tbb, pab), (gsb,))
                    TS1("dve", gs[:, :N], gs[:, :N], 0.5, ALU.mult, (gsb,), (gsb,))
                    pr, prb = sbo, sbob
                    TT("dve", pr, sbin, cbw[:, 0:30].unsqueeze(1).to_broadcast([128, NS, 30]),
                       ALU.mult, (sbinb,), (prb,))
                    rd, rdb = tmp.next()
                    RED(rd[:, :N], pr, (prb,), (rdb,))
                    STT("dve", rd[:, :N], gs[:, :N], cbw[:, 30:31], rd[:, :N], ALU.mult, ALU.add, (gsb, rdb), (rdb,))
                    ACT(z[:, j, :N], rd[:, :N], AF.Identity, (rdb,), (zb[j],), bias=P("cbb", l * 8 + j), scale=1.0)
                    CP("pool", sbo[:, :, 0:29], sbin[:, :, 1:30], (sbinb,), (sbob,))
                    CP("pool", sbo[:, :, 29], gs[:, :N], (gsb,), (sbob,))
                    DMA("sp", dr["cbs"][l][:, j], sbo, (sbob,), (), cho)
            if not sample:
                for jj in range(2):
                    j = 2 * g + jj
                    gl, glb, dg, dgb = stash[jj]
                    psz, pzb = T["auxrot"].next()
                    for k in range(31):
                        MM(psz[:, :N], dg[:, k, :], gl[:, k:k + N], k == 0, k == 30, (dgb, glb), (pzb,))
                    ACT(z[:, j, :N], psz[:, :N], AF.Identity, (pzb,), (zb[j],), bias=P("cbb", l * 8 + j), scale=0.5)
                    CP("pool", halo_b[l][:, j, :], gl[:, N:N + 30], (glb,), (halo_bb[l][j],))
        stop("B")

        pss, pssb = T["auxrot"].next()
        psq, psqb = T["auxrot"].next()
        for c in range(8):
            z1, z1b = T["bft"].next()
            z2, z2b = T["bft"].next()
            ACT(z1[:, :N], z[:, c, :N], AF.Copy, (zb[c],), (z1b,))
            ACT(z2[:, :N], z[:, c, :N], AF.Square, (zb[c],), (z2b,))
            MM(pss[:, :N], ones, z1[:, :N], c == 0, c == 7, (z1b,), (pssb,))
            MM(psq[:, :N], ones, z2[:, :N], c == 0, c == 7, (z2b,), (psqb,))
        mean, meanb = tmp.next()
        msq, msqb = tmp.next()
        var, varb = tmp.next()
        TS1("dve", mean[:, :N], pss[:, :N], 1.0 / D, ALU.mult, (pssb,), (meanb,))
        TT("dve", msq[:, :N], mean[:, :N], mean[:, :N], ALU.mult, (meanb,), (msqb,))
        STT("dve", var[:, :N], psq[:, :N], 1.0 / D, msq[:, :N], ALU.mult, ALU.subtract, (psqb, msqb), (varb,))
        ACT(var[:, :N], var[:, :N], AF.Sqrt, (varb,), (varb,), bias=epsT[:, 0:1], scale=1.0)
        lr, lrb, ln_, lnb_ = T["lnrstd"], T["lnrstdb"], T["lnnmr"], T["lnnmrb"]
        RCP(lr[:, :N], var[:, :N], (varb,), (lrb,))
        STT("dve", ln_[:, :N], mean[:, :N], -1.0, lr[:, :N], ALU.mult, ALU.mult, (meanb, lrb), (lnb_,))
        for g in range(4):
            wg, wgb = yield wreq(C_G + 1024 + 256 * g)
            for jj in range(2):
                j = 2 * g + jj
                psg, pgb = proj(T, wg, wgb, jj, xn, xnb)
                t1, t1b = T["c1"][jj]
                ACT(t1[:, :N], psg[:, :N], AF.Tanh, (pgb,), (t1b,),
                    bias=hgb[:, l * 32 + 8 + j:l * 32 + 8 + j + 1], scale=0.5)
            for jj in range(2):
                j = 2 * g + jj
                t1, t1b = T["c1"][jj]
                v1, v1b = tmp.next()
                w2, w2b = tmp.next()
                STT("dve", v1[:, :N], z[:, j, :N], P("lng", l * 8 + j), lr[:, :N], ALU.mult, ALU.mult,
                    (zb[j], lrb), (v1b,))
                TS2("dve", w2[:, :N], ln_[:, :N], P("lng", l * 8 + j), P("lnb", l * 8 + j), ALU.mult, ALU.add,
                    (lnb_,), (w2b,))
                TT("dve", v1[:, :N], v1[:, :N], w2[:, :N], ALU.add, (v1b, w2b), (v1b,))
                yb, ybb = tmp.next()
                ACT(yb[:, :N], v1[:, :N], AF.Silu, (v1b,), (ybb,))
                if sample:
                    STT("dve", m2[:, j, :N], t1[:, :N], 1.0, yb[:, :N], ALU.add, ALU.mult, (t1b, ybb), (m2b[j],))
                else:
                    STT("dve", yb[:, :N], t1[:, :N], 1.0, yb[:, :N], ALU.add, ALU.mult, (t1b, ybb), (ybb,))
                    TT("dve", m2[:, j, :N], m2[:, j, :N], yb[:, :N], ALU.add, (m2b[j], ybb), (m2b[j],))
        stop("LN")

        if sample:
            DMA("sp", T["sast"], dr["sa"][l], (), (T["sastb"],), "i_sa")
        for g in range(4):
            wh, whb = yield wreq(C_H + 256 * g)
            wc, wcb = yield wreq(C_C + 256 * g)
            for jj in range(2):
                j = 2 * g + jj
                caw = prm[:, PRM["caw"] + (l * 8 + j) * 3:PRM["caw"] + (l * 8 + j) * 3 + 3]
                psh, phb = proj(T, wh, whb, jj, xn, xnb)
                psc, pcb = proj(T, wc, wcb, jj, xn, xnb)
                hs, hsb = tmp.next()
                ACT(hs[:, :N], psh[:, :N], AF.Copy, (phb,), (hsb,))
                c1, c1b = T["c1"][jj]
                if not sample:
                    ub, ubb = T["ub"].next()
                    CP("pool", ub[:, 0:2], halo_a[l][:, j, :], (halo_ab[l][j],), (ubb,))
                    TT("dve", ub[:, 2:2 + N], psc[:, :N], hs[:, :N], ALU.mult, (pcb, hsb), (ubb,))
                    TS1("dve", c1[:, :N], ub[:, 0:N], caw[:, 0:1], ALU.mult, (ubb,), (c1b,))
                    STT("dve", c1[:, :N], ub[:, 1:N + 1], caw[:, 1:2], c1[:, :N], ALU.mult, ALU.add, (ubb, c1b), (c1b,))
                    STT("dve", c1[:, :N], ub[:, 2:N + 2], caw[:, 2:3], c1[:, :N], ALU.mult, ALU.add, (ubb, c1b), (c1b,))
                    CP("pool", halo_a[l][:, j, :], ub[:, N:N + 2], (ubb,), (halo_ab[l][j],))
                else:
                    us, usb = tmp.next()
                    TT("dve", us[:, :N], psc[:, :N], hs[:, :N], ALU.mult, (pcb, hsb), (usb,))
                    sast = T["sast"]
                    TS1("dve", c1[:, :N], sast[:, j, :, 0], caw[:, 0:1], ALU.mult, (T["sastb"],), (c1b,))
                    STT("dve", c1[:, :N], sast[:, j, :, 1], caw[:, 1:2], c1[:, :N], ALU.mult, ALU.add,
                        (T["sastb"], c1b), (c1b,))
                    STT("dve", c1[:, :N], us[:, :N], caw[:, 2:3], c1[:, :N], ALU.mult, ALU.add, (usb, c1b), (c1b,))
                    CP("pool", T["nsa"][:, j, :, 0], sast[:, j, :, 1], (T["sastb"],), (T["nsab"],))
                    CP("pool", T["nsa"][:, j, :, 1], us[:, :N], (usb,), (T["nsab"],))
            wbw, wbwb = yield wreq(C_B + 256 * g)
            wg0, wg0b = yield wreq(C_G + 256 * g)
            for jj in range(2):
                j = 2 * g + jj
                c1, c1b = T["c1"][jj]
                psb2, pb2b = proj(T, wbw, wbwb, jj, xn, xnb)
                psg, pgb = proj(T, wg0, wg0b, jj, xn, xnb)
                t0, t0b = gate_tanh(psg, pgb, 0, j)
                ya, yab = tmp.next()
                STT("dve", ya[:, :N], t0[:, :N], 1.0, c1[:, :N], ALU.add, ALU.mult, (t0b, c1b), (yab,))
                TT("dve", ya[:, :N], psb2[:, :N], ya[:, :N], ALU.mult, (pb2b, yab), (yab,))
                TT("dve", m2[:, j, :N], m2[:, j, :N], ya[:, :N], ALU.add, (m2b[j], yab), (m2b[j],))
        if sample:
            DMA("sp", dr["cas"][l], T["nsa"], (T["nsab"],), (), "o_sa")
        stop("A")

        for g in range(4):
            wp, wpb = yield wreq(C_P + 256 * g)
            wg2, wg2b = yield wreq(C_G + 2048 + 256 * g)
            w = WINS[g]
            if sample:
                for jj in range(2):
                    spin, spinb, ch = T["sp_in"][jj]
                    DMA("sp", spin, dr["sp"][l][:, 2 * g + jj], (), (spinb,), ch)
            for jj in range(2):
                j = 2 * g + jj
                psp, ppb = proj(T, wp, wpb, jj, xn, xnb)
                pl, plb = T["plbf"][jj]
                if not sample:
                    pbuf, pbb_ = T["pb"].next()
                    CP("pool", pbuf[:, 0:15], halo_p[l][:, j, :], (halo_pb[l][j],), (pbb_,))
                    ACT(pbuf[:, 15:15 + N], psp[:, :N], AF.Copy, (ppb,), (pbb_,))
                    cur, curb, ln, e0 = pbuf, pbb_, 15 + N, 0
                    d = 1
                    nxt = [T["sA"], T["sB"]]
                    ni = 0
                    while 2 * d <= w:
                        dst, dstb = nxt[ni % 2]
                        ni += 1
                        TT("dve", dst[:, 0:ln - d], cur[:, d:ln], cur[:, 0:ln - d], ALU.add, (curb,), (dstb,))
                        cur, curb, ln, e0 = dst, dstb, ln - d, e0 + d
                        d *= 2
                    o = 15 - e0
                    STT("dve", pl[:, :N], cur[:, o:o + N], 1.0 / w, pbuf[:, 15:15 + N], ALU.mult, ALU.subtract,
                        (curb, pbb_), (plb,))
                    if ti == 0:
                        tf, tfb = tmp.next()
                        TT("dve", tf[:, 0:16], cur[:, o:o + 16], invc[:, g, :], ALU.mult, (curb,), (tfb,))
                        TT("dve", pl[:, 0:16], tf[:, 0:16], pbuf[:, 15:31], ALU.subtract, (tfb, pbb_), (plb,))
                    CP("pool", halo_p[l][:, j, :], pbuf[:, N:N + 15], (pbb_,), (halo_pb[l][j],))
                else:
                    spin, spinb, _ = T["sp_in"][jj]
                    spo, spob, cho = T["sp_out"][jj]
                    pn, pnb = tmp.next()
                    ACT(pn[:, :N], psp[:, :N], AF.Copy, (ppb,), (pnb,))
                    rd, rdb = tmp.next()
                    RED(rd[:, :N], spin[:, :, 16 - w:15], (spinb,), (rdb,))
                    TT("dve", rd[:, :N], rd[:, :N], pn[:, :N], ALU.add, (rdb, pnb), (rdb,))
                    STT("dve", pl[:, :N], rd[:, :N], 1.0 / w, pn[:, :N], ALU.mult, ALU.subtract, (rdb, pnb), (plb,))
                    CP("pool", spo[:, :, 0:14], spin[:, :, 1:15], (spinb,), (spob,))
                    CP("pool", spo[:, :, 14], pn[:, :N], (pnb,), (spob,))
                    DMA("sp", dr["pps"][l][:, j], spo, (spob,), (), cho)
            for jj in range(2):
                j = 2 * g + jj
                psg, pgb = proj(T, wg2, wg2b, jj, xn, xnb)
                t2, t2b = T["c1"][jj]
                ACT(t2[:, :N], psg[:, :N], AF.Tanh, (pgb,), (t2b,),
                    bias=hgb[:, l * 32 + 16 + j:l * 32 + 16 + j + 1], scale=0.5)
            for jj in range(2):
                j = 2 * g + jj
                t2, t2b = T["c1"][jj]
                psc2, pc2b = T["auxrot"].next()
                for kc in range(2):
                    MM(psc2[:, :N], pw[:, g * 2 + kc, jj * 128:(jj + 1) * 128], T["plbf"][kc][0][:, :N],
                       kc == 0, kc == 1, (T["plbf"][kc][1], pwb), (pc2b,))
                yc, ycb = tmp.next()
                STT("dve", yc[:, :N], t2[:, :N], 1.0, psc2[:, :N], ALU.add, ALU.mult, (t2b, pc2b), (ycb,))
                if sample:
                    STT("dve", m2[:, j, :N], yc[:, :N], P("psc", l * 8 + j), m2[:, j, :N], ALU.mult, ALU.add,
                        (ycb, m2b[j]), (m2b[j],))
                else:
                    STT("dve", mb[:, j, :N], yc[:, :N], P("psc", l * 8 + j), m2[:, j, :N], ALU.mult, ALU.add,
                        (ycb, m2b[j]), (mbb[j],))
        if sample:
            yield ("join",)
            yms, ymsb = T["yms"]
            for j in range(8):
                TT("dve", mb[:, j, :N], m2[:, j, :N], yms[:, j, :], ALU.add, (m2b[j], ymsb), (mbb[j],))
        stop("C")

        for mi in range(4):
            wo, wob = yield ("w", "o%d_%d" % (l, mi), dr["w_o"][l][:, :, 256 * mi:256 * mi + 256])
            for jj in range(2):
                j = 2 * mi + jj
                ps, pb = proj(T, wo, wob, jj, mb, mbb)
                STT("dve", x[:, j, :N], ps[:, :N], 0.5, x[:, j, :N], ALU.mult, ALU.add, (pb, xb[j]), (xb[j],))
        stop("WO")

        rmsnorm(x, xb, N, "nffn", l * 8, xn, xnb, T["sq"], T["sd"], T["sdb"], T["rstd"], T["rstdb"], rot=T["auxrot"])
        for half in range(2):
            for s8 in range(8):
                s = half * 8 + s8
                w1, w1b = yield ("w", "f1_%d_%d" % (l, s), dr["w_ff1"][l][:, :, 256 * s:256 * s + 256])
                for jj in range(2):
                    ps, pb = proj(T, w1, w1b, jj, xn, xnb)
                    r, rb = T["bft"].next()
                    ACT(r[:, :N], ps[:, :N], AF.Relu, (pb,), (rb,))
                    hi = s8 * 2 + jj
                    TT("dve", h[:, hi, :N], r[:, :N], r[:, :N], ALU.mult, (rb,), (hb[hi],))
            for mi in range(4):
                if sample:
                    for kgl in range(2):
                        r0 = half * 16 + kgl * 8
                        w2_, w2b_ = yield ("w", "f2_%d_%d_%d" % (l, r0, mi),
                                           dr["w_ff2"][l][:, r0:r0 + 8, 256 * mi:256 * mi + 256])
                        for jj in range(2):
                            j = 2 * mi + jj
                            ps, pb = T["mmrot"].next()
                            for k in range(8):
                                MM(ps[:, :N], w2_[:, k, jj * 128:(jj + 1) * 128], h[:, kgl * 8 + k, :N],
                                   k == 0, k == 7, (w2b_, hb[kgl * 8 + k]), (pb,))
                            TT("dve", x[:, j, :N], x[:, j, :N], ps[:, :N], ALU.add, (xb[j], pb), (xb[j],))
                    continue
                pss2 = [T["mmrot"].next(), T["mmrot"].next()]
                for kgl in range(2):
                    r0 = half * 16 + kgl * 8
                    w2_, w2b_ = yield ("w", "f2_%d_%d_%d" % (l, r0, mi),
                                       dr["w_ff2"][l][:, r0:r0 + 8, 256 * mi:256 * mi + 256])
                    for jj in range(2):
                        ps, pb = pss2[jj]
                        for k in range(8):
                            MM(ps[:, :N], w2_[:, k, jj * 128:(jj + 1) * 128], h[:, kgl * 8 + k, :N],
                               kgl == 0 and k == 0, kgl == 1 and k == 7, (w2b_, hb[kgl * 8 + k]), (pb,))
                for jj in range(2):
                    j = 2 * mi + jj
                    ps, pb = pss2[jj]
                    TT("dve", x[:, j, :N], x[:, j, :N], ps[:, :N], ALU.add, (xb[j], pb), (xb[j],))

    def run_layers(ctxs):
        l_ = ctxs[0][1]
        DMA("pool", pw, dr["pw"][:, l_ * 8:(l_ + 1) * 8, :], (), (pwb,), "pw")
        gens = [layer(*c) for c in ctxs]
        reqs = [None] * len(gens)
        done = [False] * len(gens)
        bg = []

        def advance(i, val):
            try:
                reqs[i] = gens[i].send(val) if val is not None or reqs[i] is not None else next(gens[i])
            except StopIteration:
                done[i] = True
                reqs[i] = None

        def bg_step(n):
            for _ in range(n):
                if not bg:
                    return
                b = bg[0]
                if b[1] is None:
                    try:
                        b[1] = next(b[0])
                    except StopIteration:
                        bg.pop(0)
                        continue
                slot = W.next(b[1][2])
                try:
                    b[1] = b[0].send(slot)
                except StopIteration:
                    bg.pop(0)

        for i in range(len(gens)):
            try:
                reqs[i] = next(gens[i])
            except StopIteration:
                done[i] = True
        while not all(done):
            progressed = False
            for i in range(len(gens)):
                if done[i]:
                    continue
                r = reqs[i]
                if r[0] == "spawn":
                    bg.append([r[1], None])
                    advance(i, 0)
                    progressed = True
                elif r[0] == "join":
                    while bg:
                        bg_step(1)
                    advance(i, 0)
                    progressed = True
            if progressed:
                continue
            active = [i for i in range(len(gens)) if not done[i]]
            keys = set(reqs[i][1] for i in active)
            assert len(keys) == 1, keys
            slot = W.next(reqs[active[0]][2])
            for i in active:
                advance(i, slot)
            bg_step(1)
        while bg:
            bg_step(1)

    A.off = PHASE_BASE
    Tp = make_ctx(NT, False)
    Ts = make_ctx(NS, True)
    for ti in range(4):
        Tp["mmrot"], Tp["auxrot"] = (mm_rot, aux_rot) if ti == 3 else (mm_rot4, aux_rot4)
        W.la = 2 if ti == 3 else 1
        DMA("sp", Tp["x"], dr["xT"][:, :, ti * NT:(ti + 1) * NT], (), Tp["xb"], "x")
        if ti == 3:
            DMA("sp", Ts["x"], dr["xsT"], (), Ts["xb"], "xs")
        for l in range(2):
            if ti == 3:
                run_layers([(Tp, l, ti), (Ts, l, 4)])
            else:
                run_layers([(Tp, l, ti)])
            stop("L")
        rmsnorm(Tp["x"], Tp["xb"], NT, "nfin", 0, Tp["m2"], Tp["m2b"], Tp["sq"], Tp["sd"], Tp["sdb"], Tp["rstd"],
                Tp["rstdb"])
        DMA("sp", dr["yT"][:, :, ti * NT:(ti + 1) * NT], Tp["m2"], Tp["m2b"], (), "y")
    for l in range(2):
        TS1("dve", cbp_r[l], cbp_r[l], 0.5, ALU.mult, (cbp_b[l],), (cbp_b[l],))
        DMA("sp", dr["cbp"][l], cbp_r[l], (cbp_b[l],), (), "o_cbp%d" % l)
        DMA("sp", dr["cap"][l], halo_a[l], halo_ab[l], (), "o_cap%d" % l)
        DMA("sp", dr["pp"][l], halo_p[l], halo_pb[l], (), "o_pp%d" % l)
    rmsnorm(Ts["x"], Ts["xb"], NS, "nfin", 0, Ts["m2"], Ts["m2b"], Ts["sq"], Ts["sd"], Ts["sdb"], Ts["rstd"],
            Ts["rstdb"], rot=s_rot)
    DMA("sp", dr["ysT"], Ts["m2"], Ts["m2b"], (), "ys")
    S.barrier()


def _fm(a):
    lead = a.shape[:-1]
    r = a.reshape(lead + (8, 128))
    return np.moveaxis(r, -1, 0)


def _pack_params(norm_mix, norm_mem, norm_ffn, norm_final, conv_a_w, conv_b_w, conv_b_bias, ln_b_gain, ln_b_bias,
                 pool_scale, gate_bias):
    prm = np.zeros((128, NPRM), np.float32)

    def put(name, arr):
        flat = np.ascontiguousarray(arr).reshape(128, -1)
        prm[:, PRM[name]:PRM[name] + flat.shape[1]] = flat

    put("nmix", _fm(norm_mix))
    put("nmem", _fm(norm_mem))
    put("nffn", _fm(norm_ffn))
    put("nfin", _fm(norm_final))
    put("caw", np.transpose(_fm(conv_a_w), (0, 1, 3, 2)))
    put("cbw", np.transpose(_fm(conv_b_w), (0, 1, 3, 2)))
    put("cbb", _fm(conv_b_bias))
    put("lng", _fm(ln_b_gain))
    put("lnb", _fm(ln_b_bias))
    put("psc", _fm(pool_scale))
    put("gb", np.transpose(gate_bias.reshape(2, 32, 128), (2, 0, 1)))
    return prm


def _wr(w, kc):
    L, K, E = w.shape
    return np.ascontiguousarray(np.transpose(w.reshape(L, kc, 128, E), (0, 2, 1, 3)))


def _make_in_maps(A_):
    (x_prompt, x_sample, mem_prompt, cache_mem_k, cache_mem_v, state_conv_a, state_conv_b, state_pool, norm_mix,
     norm_mem, w_kv, w_in, conv_a_w, conv_b_w, conv_b_bias, ln_b_gain, ln_b_bias, pool_w, pool_scale, gate_bias,
     w_o, norm_ffn, w_ff1, w_ff2, norm_final) = A_
    n = 8
    prm = _pack_params(norm_mix, norm_mem, norm_ffn, norm_final, conv_a_w, conv_b_w, conv_b_bias, ln_b_gain,
                       ln_b_bias, pool_scale, gate_bias)
    shared = dict(
        w_in=_wr(w_in, 8), w_kv=_wr(w_kv, 8), w_o=_wr(w_o, 8), w_ff1=_wr(w_ff1, 8), w_ff2=_wr(w_ff2, 32),
        pw=np.ascontiguousarray(np.transpose(pool_w.reshape(2, 4, 2, 128, 256), (3, 0, 1, 2, 4)).reshape(128, 16, 256)),
        prm=prm, ident=np.eye(128, dtype=np.float32),
        invc=np.ascontiguousarray(np.broadcast_to(
            np.array([[1.0 / min(t + 1, w) for t in range(16)] for w in WINS], np.float32)[None], (128, 4, 16))),
    )
    in_maps = []
    for i in range(n):
        sl = slice(NS * i, NS * (i + 1))
        m = dict(shared)
        m["xT"] = np.ascontiguousarray(_fm(x_prompt[i]))
        m["xT"] = np.ascontiguousarray(np.transpose(m["xT"], (0, 2, 1)))
        m["xsT"] = np.ascontiguousarray(np.transpose(_fm(x_sample[sl, 0, :]), (0, 2, 1)))
        m["memT"] = np.ascontiguousarray(np.transpose(_fm(mem_prompt[i]), (0, 2, 1)))
        k = cache_mem_k[:, sl].reshape(2, NS, 256, 8, 128)
        m["kT"] = np.ascontiguousarray(np.transpose(k, (0, 1, 4, 3, 2)))
        v = cache_mem_v[:, sl].reshape(2, NS, 2, 128, 1024)
        m["vv"] = np.ascontiguousarray(np.transpose(v, (0, 1, 3, 2, 4)))
        for nm, stt in (("sa", state_conv_a), ("sb", state_conv_b), ("sp", state_pool)):
            s_ = stt[:, sl]
            r = s_.reshape(2, NS, s_.shape[2], 8, 128)
            m[nm] = np.ascontiguousarray(np.transpose(r, (0, 4, 3, 1, 2)))
        in_maps.append(m)

    return in_maps


_NC_CACHE = {}


def kernel(x_prompt, x_sample, mem_prompt, cache_mem_k, cache_mem_v, state_conv_a, state_conv_b, state_pool,
           norm_mix, norm_mem, w_kv, w_in, conv_a_w, conv_b_w, conv_b_bias, ln_b_gain, ln_b_bias, pool_w,
           pool_scale, gate_bias, w_o, norm_ffn, w_ff1, w_ff2, norm_final):
    f = lambda a: np.asarray(a, dtype=np.float32)
    (x_prompt, x_sample, mem_prompt, cache_mem_k, cache_mem_v, state_conv_a, state_conv_b, state_pool, norm_mix,
     norm_mem, w_kv, w_in, conv_a_w, conv_b_w, conv_b_bias, ln_b_gain, ln_b_bias, pool_w, pool_scale, gate_bias,
     w_o, norm_ffn, w_ff1, w_ff2, norm_final) = map(f, (
        x_prompt, x_sample, mem_prompt, cache_mem_k, cache_mem_v, state_conv_a, state_conv_b, state_pool, norm_mix,
        norm_mem, w_kv, w_in, conv_a_w, conv_b_w, conv_b_bias, ln_b_gain, ln_b_bias, pool_w, pool_scale, gate_bias,
        w_o, norm_ffn, w_ff1, w_ff2, norm_final))
    n = 8
    in_maps = _make_in_maps((x_prompt, x_sample, mem_prompt, cache_mem_k, cache_mem_v, state_conv_a, state_conv_b,
                             state_pool, norm_mix, norm_mem, w_kv, w_in, conv_a_w, conv_b_w, conv_b_bias, ln_b_gain,
                             ln_b_bias, pool_w, pool_scale, gate_bias, w_o, norm_ffn, w_ff1, w_ff2, norm_final))
    if "nc" not in _NC_CACHE:
        _NC_CACHE["nc"] = build_program()
    nc = _NC_CACHE["nc"]
    res = run_bass_kernel_spmd(nc, in_maps, core_ids=list(range(n)))
    return _assemble(res.results)


def _assemble(R):
    n = 8

    def unfm(a):
        return np.transpose(a, (2, 1, 0)).reshape(a.shape[2], 1024)

    y_prompt = np.stack([unfm(R[i]["yT"]) for i in range(n)])
    y_sample = np.concatenate([unfm(R[i]["ysT"]) for i in range(n)])[:, None, :]
    mem_k = np.stack([np.stack([unfm(R[i]["mk"][l]).reshape(256, 4, 256) for i in range(n)]) for l in range(2)])
    mem_v = np.stack([np.stack([R[i]["mv"][l].reshape(256, 4, 256) for i in range(n)]) for l in range(2)])

    def pst(name):
        return np.stack([np.stack([unfm(R[i][name][l]) for i in range(n)]) for l in range(2)])

    def sst(name):
        outs = []
        for l in range(2):
            per = []
            for i in range(n):
                a = R[i][name][l]
                per.append(np.transpose(a, (2, 3, 1, 0)).reshape(NS, a.shape[3], 1024))
            outs.append(np.concatenate(per))
        return np.stack(outs)

    outs = (y_prompt, y_sample, mem_k, mem_v, pst("cap"), pst("cbp"), pst("pp"), sst("cas"), sst("cbs"), sst("pps"))
    return tuple(np.ascontiguousarray(o, dtype=np.float32) for o in outs)
```

```python
import contextlib
import numpy as np
import concourse.bass as bass
import concourse.mybir as mybir
from concourse.bass_utils import run_bass_kernel_spmd

F32 = mybir.dt.float32
BF16 = mybir.dt.bfloat16
AF = mybir.ActivationFunctionType
ALU = mybir.AluOpType
AX = mybir.AxisListType

D = 1024
SEQ = 2048
NT = 512
NS = 16
NMEM = 256
DPROJ = 11264
DFF = 4096
EPS = 1e-6
WINS = (2, 4, 8, 16)
RING = 7
SLOTW = 1024

C_H, C_B, C_C, C_GA, C_GB, C_P, C_Q, C_G = 0, 1024, 2048, 3072, 4096, 5120, 6144, 7168

PRM = {}
_o = 0
for _n, _w in (("nmix", 16), ("nmem", 16), ("nffn", 16), ("nfin", 8), ("caw", 48), ("cbw", 496),
               ("cbb", 16), ("lng", 16), ("lnb", 16), ("psc", 16), ("gb", 64)):
    PRM[_n] = _o
    _o += _w
NPRM = _o


class Buf:
    __slots__ = ("name", "writer", "readers")

    def __init__(self, name=""):
        self.name = name
        self.writer = None
        self.readers = {}


class Op:
    __slots__ = ("eng", "fn", "chan", "chan_val", "pos", "signal", "count", "waits")

    def __init__(self, eng, fn, chan):
        self.eng = eng
        self.fn = fn
        self.chan = chan
        self.chan_val = 0
        self.pos = 0
        self.signal = False
        self.count = 0
        self.waits = []


ENGS = ("pe", "act", "dve", "pool", "sp")


class Sched:
    def __init__(self, dry):
        self.dry = dry
        self.ops = {e: [] for e in ENGS}
        self.known = {e: {} for e in ENGS}
        self.chan_n = {}
        self.last_compute = {}
        self.last_dma = {}

    def _add_deps(self, o, deps):
        eng = o.eng
        for d in deps:
            key = ("c", d.chan) if d.chan else ("e", d.eng)
            val = d.chan_val if d.chan else d.pos
            if self.known[eng].get(key, -1) >= val:
                continue
            self.known[eng][key] = val
            d.signal = True
            o.waits.append(d)

    def op(self, eng, fn, reads=(), writes=(), chan=None):
        if self.dry:
            return None
        o = Op(eng, fn, chan)
        o.pos = len(self.ops[eng])
        raw = []
        other = []
        for b in reads:
            if b.writer is not None:
                raw.append(b.writer)
        for b in writes:
            if b.writer is not None:
                other.append(b.writer)
            other.extend(b.readers.values())
        deps = []
        seen = set()
        for lst, is_raw in ((raw, True), (other, False)):
            for d in lst:
                if id(d) in seen:
                    continue
                if d.chan is None and chan is None and d.eng == eng:
                    if eng == "pe":
                        continue
                seen.add(id(d))
                deps.append(d)
        self._add_deps(o, deps)
        if chan is not None:
            n = self.chan_n.get(chan, 0) + 1
            self.chan_n[chan] = n
            o.chan_val = 16 * n
            self.last_dma[chan] = o
        else:
            self.last_compute[eng] = o
        rkey = eng if chan is None else (chan, o.chan_val)
        for b in reads:
            b.readers[rkey] = o
        for b in writes:
            b.writer = o
            b.readers = {}
        self.ops[eng].append(o)
        return o

    def barrier(self):
        if self.dry:
            return
        deps = list(self.last_compute.values()) + list(self.last_dma.values())
        for e in ENGS:
            o = Op(e, None, None)
            o.pos = len(self.ops[e])
            self._add_deps(o, [d for d in deps if not (d.chan is None and d.eng == e)])
            self.ops[e].append(o)

    def finalize(self):
        for e in ENGS:
            c = 0
            for o in self.ops[e]:
                if o.chan is None and o.signal and o.fn is not None:
                    c += 1
                o.count = c


class Arena:
    def __init__(self, tensor, base, limit):
        self.t = tensor
        self.off = base
        self.limit = limit

    def alloc(self, shape, dtype):
        n = 1
        for s in shape[1:]:
            n *= s
        words = n if dtype == F32 else (n + 1) // 2
        words = (words + 7) // 8 * 8
        a = self.off
        self.off += words
        assert self.off <= self.limit, ("SBUF arena overflow", self.off, self.limit)
        ap = self.t[:, a:a + words]
        if dtype == BF16:
            ap = ap.bitcast(BF16)
        ap = ap[:, 0:n]
        if len(shape) == 3:
            ap = ap.rearrange("p (a b) -> p a b", a=shape[1])
        elif len(shape) == 4:
            ap = ap.rearrange("p (a b c) -> p a b c", a=shape[1], b=shape[2])
        return ap


class Rot:
    def __init__(self, items):
        self.items = items
        self.i = 0

    def next(self):
        r = self.items[self.i % len(self.items)]
        self.i += 1
        return r


class WStream:
    def __init__(self, S, slots, recorded):
        self.S = S
        self.slots = slots
        self.rec = recorded
        self.log = []
        self.issued = 0
        self.i = 0
        self.la = 1

    def _issue(self, n):
        src = self.rec[n]
        ap, buf = self.slots[n % RING]
        dst = self._view(ap, src.shape)
        self.S.op("pool", (lambda e, dst=dst, src=src: e.dma_start(out=dst, in_=src)),
                  reads=(), writes=(buf,), chan="w%d" % (n % RING))

    @staticmethod
    def _view(ap, shape):
        return ap.rearrange("p (a b) -> p a b", a=shape[1])

    def next(self, src):
        if self.rec is None:
            self.log.append(src)
            ap, buf = self.slots[len(self.log) % RING]
            return self._view(ap, src.shape), buf
        i = self.i
        self.i += 1
        while self.issued < min(len(self.rec), i + RING - self.la):
            self._issue(self.issued)
            self.issued += 1
        ap, buf = self.slots[i % RING]
        return self._view(ap, src.shape), buf


def build_program():
    nc = bass.Bass("TRN2", target_bir_lowering=False)

    def din(name, shape):
        return nc.dram_tensor(name, list(shape), F32, kind="ExternalInput").ap()

    def dout(name, shape):
        return nc.dram_tensor(name, list(shape), F32, kind="ExternalOutput").ap()

    dr = dict(
        xT=din("xT", (128, 8, SEQ)), xsT=din("xsT", (128, 8, NS)), memT=din("memT", (128, 8, NMEM)),
        kT=din("kT", (2, NS, 128, 8, 256)), vv=din("vv", (2, NS, 128, 2, 1024)),
        sa=din("sa", (2, 128, 8, NS, 2)), sb=din("sb", (2, 128, 8, NS, 30)), sp=din("sp", (2, 128, 8, NS, 15)),
        w_in=din("w_in", (2, 128, 8, DPROJ)), w_kv=din("w_kv", (2, 128, 8, 2048)),
        w_o=din("w_o", (2, 128, 8, D)), w_ff1=din("w_ff1", (2, 128, 8, DFF)), w_ff2=din("w_ff2", (2, 128, 32, D)),
        pw=din("pw", (128, 16, 256)), prm=din("prm", (128, NPRM)), ident=din("ident", (128, 128)),
        invc=din("invc", (128, 4, 16)),
        yT=dout("yT", (128, 8, SEQ)), ysT=dout("ysT", (128, 8, NS)),
        mk=dout("mk", (2, 128, 8, NMEM)), mv=dout("mv", (2, NMEM, D)),
        cap=dout("cap", (2, 128, 8, 2)), cbp=dout("cbp", (2, 128, 8, 30)), pp=dout("pp", (2, 128, 8, 15)),
        cas=dout("cas", (2, 128, 8, NS, 2)), cbs=dout("cbs", (2, 128, 8, NS, 30)), pps=dout("pps", (2, 128, 8, NS, 15)),
    )

    import os
    ARENA_WORDS = int(os.environ.get("ARENA_WORDS", "51200"))
    with contextlib.ExitStack() as st:
        arena_t = st.enter_context(nc.sbuf_tensor("arena", [128, ARENA_WORDS], F32))
        psum = [st.enter_context(nc.psum_tensor("ps%d" % i, [128, 512], F32)) for i in range(8)]

        S = None
        rec = None
        for run in ("dry", "real"):
            S = Sched(dry=(run == "dry"))
            W = emit_all(nc, S, dr, arena_t, ARENA_WORDS, psum, rec)
            if run == "dry":
                rec = W.log
        S.finalize()

        eng_sem = {e: st.enter_context(nc.semaphore("sem_" + e)) for e in ENGS}
        chan_sem = {c: st.enter_context(nc.semaphore("ch_" + c)) for c in S.chan_n}

        def emit(e, name):
            for o in S.ops[name]:
                for d in o.waits:
                    if d.chan is not None:
                        e.wait_ge(chan_sem[d.chan], d.chan_val)
                    else:
                        e.wait_ge(eng_sem[d.eng], d.count)
                if o.fn is None:
                    continue
                ins = o.fn(e)
                if o.chan is not None:
                    ins.then_inc(chan_sem[o.chan], 16)
                elif o.signal:
                    ins.then_inc(eng_sem[name], 1)

        with nc.Block() as block:
            @block.tensor
            def _(e):
                emit(e, "pe")

            @block.scalar
            def _(e):
                emit(e, "act")

            @block.vector
            def _(e):
                emit(e, "dve")

            @block.gpsimd
            def _(e):
                emit(e, "pool")

            @block.sync
            def _(e):
                emit(e, "sp")
    return nc


class _Stop(Exception):
    pass


KSTOP = [None]
MARKS = []
DIAG_ENG = "pool"


def emit_all(nc, S, dr, arena_t, ARENA_WORDS, psum, rec):
    holder = {}
    try:
        _emit_all(nc, S, dr, arena_t, ARENA_WORDS, psum, rec, holder)
    except _Stop:
        S.barrier()
    return holder["W"]


def _emit_all(nc, S, dr, arena_t, ARENA_WORDS, psum, rec, holder):
    A = Arena(arena_t, 0, ARENA_WORDS)

    def stop(stage):
        if not S.dry:
            MARKS.append((stage, len(S.ops["pe"])))
        if KSTOP[0] == stage:
            raise _Stop()

    def ACT(out, in_, func, reads, writes, bias=None, scale=None):
        kw = {}
        if bias is not None:
            kw["bias"] = bias
        if scale is not None:
            kw["scale"] = scale
        S.op("act", lambda e: e.activation(out=out, in_=in_, func=func, **kw), reads, writes)

    def STT(eng, out, in0, scalar, in1, op0, op1, reads, writes):
        S.op(eng, lambda e: e.scalar_tensor_tensor(out=out, in0=in0, scalar=scalar, in1=in1, op0=op0, op1=op1),
             reads, writes)

    def TT(eng, out, in0, in1, op, reads, writes):
        S.op(eng, lambda e: e.tensor_tensor(out=out, in0=in0, in1=in1, op=op), reads, writes)

    def TS1(eng, out, in_, scalar, op, reads, writes):
        S.op(eng, lambda e: e.tensor_single_scalar(out=out, in_=in_, scalar=scalar, op=op), reads, writes)

    def TS2(eng, out, in0, s1, s2, op0, op1, reads, writes):
        S.op(eng, lambda e: e.tensor_scalar(out=out, in0=in0, scalar1=s1, scalar2=s2, op0=op0, op1=op1), reads, writes)

    def CP(eng, out, in_, reads, writes):
        S.op(eng, lambda e: e.tensor_copy(out=out, in_=in_), reads, writes)

    def RCP(out, in_, reads, writes):
        S.op("dve", lambda e: e.reciprocal(out=out, in_=in_), reads, writes)

    def RED(out, in_, reads, writes):
        S.op("dve", lambda e: e.tensor_reduce(out=out, in_=in_, axis=AX.X, op=ALU.add), reads, writes)

    def MM(out, lhsT, rhs, start, stop, reads, writes):
        S.op("pe", lambda e: e.matmul(out, lhsT, rhs, start=start, stop=stop), reads, writes)

    def DMA(eng, out, in_, reads, writes, chan):
        S.op(eng, lambda e: e.dma_start(out=out, in_=in_), reads, writes, chan=chan)

    def MEMSET(eng, ap, val, writes):
        S.op(eng, lambda e: e.memset(ap, val), (), writes)

    psb_ = [Buf("psb%d" % i) for i in range(8)]
    mm_rot = Rot([(psum[i], psb_[i]) for i in range(3)])
    aux_rot = Rot([(psum[i], psb_[i]) for i in range(3, 6)])
    mm_rot4 = Rot([(psum[i], psb_[i]) for i in (0, 1, 2, 6)])
    aux_rot4 = Rot([(psum[i], psb_[i]) for i in (3, 4, 5, 7)])
    ps6b, ps7b = psb_[6], psb_[7]
    _sreg = []
    for i in range(12):
        _sreg.append((psum[6][:, 16 * i:16 * i + 16], ps6b))
        _sreg.append((psum[7][:, 320 + 16 * i:320 + 16 * i + 16], ps7b))
    s_rot = Rot(_sreg)

    ring = []
    for i in range(RING):
        ap = A.alloc([128, 2048], BF16)
        ring.append((ap, Buf("slot%d" % i)))
    W = WStream(S, ring, rec)
    holder["W"] = W

    prm = A.alloc([128, NPRM], F32)
    hgb = A.alloc([128, 64], F32)
    ident = A.alloc([128, 128], BF16)
    ones = A.alloc([128, 128], BF16)
    epsT = A.alloc([128, 1], F32)
    invc = A.alloc([128, 4, 16], F32)
    pw = A.alloc([128, 8, 256], BF16)
    pwb = Buf("pw")
    KT = [A.alloc([128, 8, 256], BF16) for _ in range(2)]
    VV = [A.alloc([128, 2, 1024], BF16) for _ in range(2)]
    halo_a = [A.alloc([128, 8, 2], F32) for _ in range(2)]
    halo_b = [A.alloc([128, 8, 30], BF16) for _ in range(2)]
    halo_p = [A.alloc([128, 8, 15], F32) for _ in range(2)]
    cbp_r = [A.alloc([128, 8, 30], F32) for _ in range(2)]
    halo_ab = [[Buf() for _ in range(8)] for _ in range(2)]
    halo_bb = [[Buf() for _ in range(8)] for _ in range(2)]
    halo_pb = [[Buf() for _ in range(8)] for _ in range(2)]
    cbp_b = [Buf() for _ in range(2)]
    KTb = [Buf(), Buf()]
    VVb = [Buf(), Buf()]
    cst = Buf("const")
    PHASE_BASE = A.off

    def P(name, idx, width=1):
        o = PRM[name] + idx
        return prm[:, o:o + width]

    prmb = Buf("prm")
    DMA("sp", prm, dr["prm"], (), (prmb,), "setup")
    DMA("sp", invc, dr["invc"], (), (Buf(),), "setup2")
    DMA("pool", ident, dr["ident"], (), (Buf(),), "setup3")
    MEMSET("dve", ones, 1.0, (Buf(),))
    MEMSET("dve", epsT, EPS, (Buf(),))
    for l in range(2):
        MEMSET("dve", halo_a[l], 0.0, halo_ab[l])
        MEMSET("dve", halo_b[l], 0.0, halo_bb[l])
        MEMSET("dve", halo_p[l], 0.0, halo_pb[l])
    TS1("dve", hgb, prm[:, PRM["gb"]:PRM["gb"] + 64], 0.5, ALU.mult, (prmb,), (Buf(),))
    S.barrier()
    stop("setup")

    def rmsnorm(x, xb, N, gain_name, gain_idx, out, outb, sqrot, sd, sdb, rstd, rstdb, rot=None):
        ps, pb = (rot or aux_rot).next()
        for c in range(8):
            sq, sqb = sqrot.next()
            ACT(sq[:, :N], x[:, c, :N], AF.Square, (xb[c],), (sqb,))
            MM(ps[:, :N], ones, sq[:, :N], c == 0, c == 7, (sqb,), (pb,))
        ACT(sd[:, :N], ps[:, :N], AF.Sqrt, (pb,), (sdb,), bias=epsT[:, 0:1], scale=1.0 / D)
        RCP(rstd[:, :N], sd[:, :N], (sdb,), (rstdb,))
        for c in range(8):
            STT("dve", out[:, c, :N], x[:, c, :N], P(gain_name, gain_idx + c), rstd[:, :N], ALU.mult, ALU.mult,
                (xb[c], rstdb), (outb[c],))

    A.off = PHASE_BASE
    mem = A.alloc([128, 8, NMEM], F32)
    memb = [Buf() for _ in range(8)]
    memn = A.alloc([128, 8, NMEM], BF16)
    memnb = [Buf() for _ in range(8)]
    kst = A.alloc([128, 8, NMEM], F32)
    kstb = Buf()
    vst = A.alloc([128, 2, 1024], F32)
    vstb = Buf()
    sqr = Rot([(A.alloc([128, 512], BF16), Buf()) for _ in range(2)])
    sd0 = A.alloc([128, 512], F32)
    rs0 = A.alloc([128, 512], F32)
    sd0b, rs0b = Buf(), Buf()
    DMA("sp", mem, dr["memT"], (), memb, "mem")
    for l in range(2):
        stop("kv_load")
        rmsnorm(mem, memb, NMEM, "nmem", l * 8, memn, memnb, sqr, sd0, sd0b, rs0, rs0b)
        stop("kv_norm")
        for e2 in range(4):
            wk, wkb = W.next(dr["w_kv"][l][:, :, 256 * e2:256 * e2 + 256])
            for jj in range(2):
                e_ = 2 * e2 + jj
                ps, pb = mm_rot.next()
                for k in range(8):
                    MM(ps[:, :NMEM], wk[:, k, jj * 128:(jj + 1) * 128], memn[:, k, :], k == 0, k == 7,
                       (wkb, memnb[k]), (pb,))
                ACT(kst[:, e_, :], ps[:, :NMEM], AF.Copy, (pb,), (kstb,))
                CP("dve", KT[l][:, e_, :], kst[:, e_, :], (kstb,), (KTb[l],))
        stop("kv_k")
        DMA("sp", dr["mk"][l], kst, (kstb,), (), "mk")
        stop("kv_mk")
        for s in range(4):
            wv, wvb = W.next(dr["w_kv"][l][:, :, 1024 + 256 * s:1024 + 256 * s + 256])
            for tc in range(2):
                ps, pb = mm_rot.next()
                for k in range(8):
                    MM(ps[:, :256], memn[:, k, tc * 128:(tc + 1) * 128], wv[:, k, :], k == 0, k == 7,
                       (wvb, memnb[k]), (pb,))
                ACT(vst[:, tc, 256 * s:256 * s + 256], ps[:, :256], AF.Copy, (pb,), (vstb,))
                CP("dve", VV[l][:, tc, 256 * s:256 * s + 256], vst[:, tc, 256 * s:256 * s + 256], (vstb,), (VVb[l],))
        stop("kv_v")
        DMA("sp", dr["mv"][l].rearrange("(tc p) e -> p tc e", p=128), vst, (vstb,), (), "mv")
        stop("kv_mv")
    S.barrier()
    stop("kv")

    def make_ctx(N, sample):
        T = {}
        T["N"] = N
        T["sample"] = sample
        T["mmrot"] = s_rot if sample else mm_rot
        T["auxrot"] = s_rot if sample else aux_rot
        NM = N
        T["x"] = A.alloc([128, 8, NM], F32)
        T["xb"] = [Buf() for _ in range(8)]
        T["xn"] = A.alloc([128, 8, NM], BF16)
        T["xnb"] = [Buf() for _ in range(8)]
        T["z"] = A.alloc([128, 8, NM], F32)
        T["zb"] = [Buf() for _ in range(8)]
        T["m2"] = A.alloc([128, 8, NM], F32)
        T["m2b"] = [Buf() for _ in range(8)]
        T["mb"] = A.alloc([128, 8, NM], BF16)
        T["mbb"] = [Buf() for _ in range(8)]
        T["h"] = T["z"].rearrange("p a b -> p (a b)").bitcast(BF16).rearrange("p (a b) -> p a b", a=16)
        T["hb"] = [T["zb"][i // 2] for i in range(16)]
        T["tmp"] = Rot([(A.alloc([128, NM], F32), Buf()) for _ in range(5)])
        T["bft"] = Rot([(A.alloc([128, NM], BF16), Buf()) for _ in range(3)])
        T["sq"] = Rot([(A.alloc([128, NM], BF16), Buf()) for _ in range(2)])
        for grp in (("rstd", "lnrstd", "rden"), ("sd", "lnnmr")):
            ap_, b_ = A.alloc([128, NM], F32), Buf()
            for nm in grp:
                T[nm] = ap_
                T[nm + "b"] = b_
        T["c1"] = [(A.alloc([128, NM], F32), Buf()) for _ in range(2)]
        T["plbf"] = [(A.alloc([128, NM], BF16), Buf()) for _ in range(2)]
        T["qbf"] = [(A.alloc([128, NM], BF16), Buf()) for _ in range(2)]
        T["ebf"] = [(A.alloc([128, NM], BF16), Buf()) for _ in range(2)]
        if not sample:
            T["glu"] = Rot([(A.alloc([128, 30 + NM], BF16), Buf()) for _ in range(2)])
            T["diag"] = [(A.alloc([128, 31, 128], BF16), Buf()) for _ in range(2)]
            T["ub"] = Rot([(A.alloc([128, 2 + NM], F32), Buf()) for _ in range(2)])
            T["pb"] = Rot([(A.alloc([128, 15 + NM], F32), Buf()) for _ in range(2)])
            T["sA"] = (A.alloc([128, 16 + NM], F32), Buf())
            T["sB"] = (A.alloc([128, 16 + NM], F32), Buf())
        else:
            T["sast"] = A.alloc([128, 8, NS, 2], F32)
            T["nsa"] = A.alloc([128, 8, NS, 2], F32)
            T["sastb"], T["nsab"] = Buf(), Buf()
            T["sb_in"] = [(A.alloc([128, NS, 30], F32), Buf(), "i_sb%d" % i) for i in range(2)]
            T["sb_out"] = [(A.alloc([128, NS, 30], F32), Buf(), "o_sb%d" % i) for i in range(2)]
            T["sp_in"] = [(A.alloc([128, NS, 15], F32), Buf(), "i_sp%d" % i) for i in range(2)]
            T["sp_out"] = [(A.alloc([128, NS, 15], F32), Buf(), "o_sp%d" % i) for i in range(2)]
            T["qall"] = (A.alloc([128, 8, NS], BF16), Buf())
            T["t3all"] = (A.alloc([128, 8, NS], F32), Buf())
            T["yms"] = (A.alloc([128, 8, NS], F32), Buf())
            T["E"] = (A.alloc([128, 128], BF16), Buf())
            T["rdens"] = (A.alloc([128, 64], F32), Buf())
        return T

    def proj(T, wt, wtb, jj, src, srcb):
        N = T["N"]
        ps, pb = T["mmrot"].next()
        for k in range(8):
            MM(ps[:, :N], wt[:, k, jj * 128:(jj + 1) * 128], src[:, k, :N], k == 0, k == 7, (wtb, srcb[k]), (pb,))
        return ps, pb

    def sample_attention(T, l):
        N = T["N"]
        qall, qallb = T["qall"]
        t3all, t3allb = T["t3all"]
        yms, ymsb = T["yms"]
        E, Eb = T["E"]
        pssc, psscb = psum[7][:, 0:128], ps7b
        for b in range(NS):
            kt, ktb = yield ("kv", "k%d_%d" % (l, b), dr["kT"][l, b])
            for g in range(4):
                for kc in range(2):
                    col = b * 8 + g * 2 + kc
                    for dc in range(2):
                        MM(pssc[:, col:col + 1], kt[:, 2 * g + dc, kc * 128:(kc + 1) * 128],
                           qall[:, 2 * g + dc, b:b + 1], dc == 0, dc == 1, (ktb, qallb), (psscb,))
        ACT(E, pssc[:, 0:128], AF.Exp, (psscb,), (Eb,), scale=1.0 / 16.0)
        psden, psdenb = psum[7][:, 128:192], ps7b
        for kc in range(2):
            MM(psden[:, 0:64], ones, E[:, kc:128:2], kc == 0, kc == 1, (Eb,), (psdenb,))
        rdens, rdensb = T["rdens"]
        RCP(rdens, psden[:, 0:64], (psdenb,), (rdensb,))
        pso, psob = psum[7][:, 192:320], ps7b
        for b in range(NS):
            vt, vtb = yield ("kv", "v%d_%d" % (l, b), dr["vv"][l, b])
            for j in range(8):
                g = j // 2
                for kc in range(2):
                    ec = b * 8 + g * 2 + kc
                    MM(pso[:, b * 8 + j:b * 8 + j + 1], vt[:, kc, j * 128:(j + 1) * 128], E[:, ec:ec + 1],
                       kc == 0, kc == 1, (vtb, Eb), (psob,))
        for j in range(8):
            g = j // 2
            STT("dve", yms[:, j, :], t3all[:, j, :], 1.0, pso[:, j:128:8], ALU.add, ALU.mult, (t3allb, psob), (ymsb,))
            TT("dve", yms[:, j, :], yms[:, j, :], rdens[:, g:64:4], ALU.mult, (ymsb, rdensb), (ymsb,))

    def layer(T, l, ti):
        N = T["N"]
        sample = T["sample"]
        x, xb, xn, xnb = T["x"], T["xb"], T["xn"], T["xnb"]
        z, zb, m2, m2b, mb, mbb, h, hb = T["z"], T["zb"], T["m2"], T["m2b"], T["mb"], T["mbb"], T["h"], T["hb"]
        tmp = T["tmp"]
        win = dr["w_in"][l]

        def wreq(c0):
            return ("w", "in%d_%d" % (l, c0), win[:, :, c0:c0 + 256])

        def gate_tanh(psg, pgb, gi, j):
            t, tb_ = tmp.next()
            ACT(t[:, :N], psg[:, :N], AF.Tanh, (pgb,), (tb_,), bias=hgb[:, l * 32 + gi * 8 + j:l * 32 + gi * 8 + j + 1],
                scale=0.5)
            return t, tb_

        rmsnorm(x, xb, N, "nmix", l * 8, xn, xnb, T["sq"], T["sd"], T["sdb"], T["rstd"], T["rstdb"], rot=T["auxrot"])

        for g in range(4):
            wq, wqb = yield wreq(C_Q + 256 * g)
            wg3, wg3b = yield wreq(C_G + 3072 + 256 * g)
            for jj in range(2):
                j = 2 * g + jj
                psq_, pqb = proj(T, wq, wqb, jj, xn, xnb)
                if not sample:
                    ACT(T["qbf"][jj][0][:, :N], psq_[:, :N], AF.Copy, (pqb,), (T["qbf"][jj][1],))
                else:
                    ACT(T["qall"][0][:, j, :], psq_[:, :N], AF.Copy, (pqb,), (T["qall"][1],))
            for dc in range(2):
                j = 2 * g + dc
                psg, pgb = proj(T, wg3, wg3b, dc, xn, xnb)
                if not sample:
                    t3, t3b = T["c1"][dc]
                    ACT(t3[:, :N], psg[:, :N], AF.Tanh, (pgb,), (t3b,),
                        bias=hgb[:, l * 32 + 24 + j:l * 32 + 24 + j + 1], scale=0.5)
                else:
                    ACT(T["t3all"][0][:, j, :], psg[:, :N], AF.Tanh, (pgb,), (T["t3all"][1],),
                        bias=hgb[:, l * 32 + 24 + j:l * 32 + 24 + j + 1], scale=0.5)
            if not sample:
                for kc in range(2):
                    pss_, pssb_ = T["auxrot"].next()
                    for dc in range(2):
                        MM(pss_[:, :N], KT[l][:, 2 * g + dc, kc * 128:(kc + 1) * 128], T["qbf"][dc][0][:, :N],
                           dc == 0, dc == 1, (KTb[l], T["qbf"][dc][1]), (pssb_,))
                    ACT(T["ebf"][kc][0][:, :N], pss_[:, :N], AF.Exp, (pssb_,), (T["ebf"][kc][1],), scale=1.0 / 16.0)
                psd, psdb = T["auxrot"].next()
                for kc in range(2):
                    MM(psd[:, :N], ones, T["ebf"][kc][0][:, :N], kc == 0, kc == 1, (T["ebf"][kc][1],), (psdb,))
                RCP(T["rden"][:, :N], psd[:, :N], (psdb,), (T["rdenb"],))
                for dc in range(2):
                    j = 2 * g + dc
                    t3, t3b = T["c1"][dc]
                    pso, psob = T["auxrot"].next()
                    for kc in range(2):
                        MM(pso[:, :N], VV[l][:, kc, j * 128:(j + 1) * 128], T["ebf"][kc][0][:, :N], kc == 0, kc == 1,
                           (VVb[l], T["ebf"][kc][1]), (psob,))
                    ym, ymb = tmp.next()
                    STT("dve", ym[:, :N], t3[:, :N], 1.0, pso[:, :N], ALU.add, ALU.mult, (t3b, psob), (ymb,))
                    TT("dve", m2[:, j, :N], ym[:, :N], T["rden"][:, :N], ALU.mult, (ymb, T["rdenb"]), (m2b[j],))
        if sample:
            yield ("spawn", sample_attention(T, l))
        stop("M")

        for g in range(4):
            wa, wab = yield wreq(C_GA + 256 * g)
            wb_, wbb = yield wreq(C_GB + 256 * g)
            if sample:
                for jj in range(2):
                    sbin, sbinb, ch = T["sb_in"][jj]
                    DMA("sp", sbin, dr["sb"][l][:, 2 * g + jj], (), (sbinb,), ch)
            stash = []
            for jj in range(2):
                j = 2 * g + jj
                cbw = prm[:, PRM["cbw"] + (l * 8 + j) * 31:PRM["cbw"] + (l * 8 + j) * 31 + 31]
                psa, pab = proj(T, wa, wab, jj, xn, xnb)
                psb, pbb = proj(T, wb_, wbb, jj, xn, xnb)
                tb, tbb = tmp.next()
                ACT(tb[:, :N], psb[:, :N], AF.Tanh, (pbb,), (tbb,), scale=0.5)
                if not sample:
                    gl, glb = T["glu"].next()
                    CP("pool", gl[:, 0:30], halo_b[l][:, j, :], (halo_bb[l][j],), (glb,))
                    STT("dve", gl[:, 30:30 + N], tb[:, :N], 1.0, psa[:, :N], ALU.add, ALU.mult, (tbb, pab), (glb,))
                    if ti == 3:
                        STT("dve", cbp_r[l][:, j, :], tb[:, N - 30:N], 1.0, psa[:, N - 30:N], ALU.add, ALU.mult,
                            (tbb, pab), (cbp_b[l],))
                    dg, dgb = T["diag"][jj]
                    TT(("pool", "dve")[jj], dg, ident.unsqueeze(1).to_broadcast([128, 31, 128]),
                       cbw.unsqueeze(2).to_broadcast([128, 31, 128]), ALU.mult, (), (dgb,))
                    stash.append((gl, glb, dg, dgb))
                else:
                    sbin, sbinb, _ = T["sb_in"][jj]
                    sbo, sbob, cho = T["sb_out"][jj]
                    gs, gsb = tmp.next()
                    STT("dve", gs[:, :N], tb[:, :N], 1.0, psa[:, :N], ALU.add, ALU.mult, (tbb, pab), (gsb,))
                    TS1("dve", gs[:, :N], gs[:, :N], 0.5, ALU.mult, (gsb,), (gsb,))
                    pr, prb = sbo, sbob
                    TT("dve", pr, sbin, cbw[:, 0:30].unsqueeze(1).to_broadcast([128, NS, 30]),
                       ALU.mult, (sbinb,), (prb,))
                    rd, rdb = tmp.next()
                    RED(rd[:, :N], pr, (prb,), (rdb,))
                    STT("dve", rd[:, :N], gs[:, :N], cbw[:, 30:31], rd[:, :N], ALU.mult, ALU.add, (gsb, rdb), (rdb,))
                    ACT(z[:, j, :N], rd[:, :N], AF.Identity, (rdb,), (zb[j],), bias=P("cbb", l * 8 + j), scale=1.0)
                    CP("pool", sbo[:, :, 0:29], sbin[:, :, 1:30], (sbinb,), (sbob,))
                    CP("pool", sbo[:, :, 29], gs[:, :N], (gsb,), (sbob,))
                    DMA("sp", dr["cbs"][l][:, j], sbo, (sbob,), (), cho)
            if not sample:
                for jj in range(2):
                    j = 2 * g + jj
                    gl, glb, dg, dgb = stash[jj]
                    psz, pzb = T["auxrot"].next()
                    for k in range(31):
                        MM(psz[:, :N], dg[:, k, :], gl[:, k:k + N], k == 0, k == 30, (dgb, glb), (pzb,))
                    ACT(z[:, j, :N], psz[:, :N], AF.Identity, (pzb,), (zb[j],), bias=P("cbb", l * 8 + j), scale=0.5)
                    CP("pool", halo_b[l][:, j, :], gl[:, N:N + 30], (glb,), (halo_bb[l][j],))
        stop("B")

        pss, pssb = T["auxrot"].next()
        psq, psqb = T["auxrot"].next()
        for c in range(8):
            z1, z1b = T["bft"].next()
            z2, z2b = T["bft"].next()
            ACT(z1[:, :N], z[:, c, :N], AF.Copy, (zb[c],), (z1b,))
            ACT(z2[:, :N], z[:, c, :N], AF.Square, (zb[c],), (z2b,))
            MM(pss[:, :N], ones, z1[:, :N], c == 0, c == 7, (z1b,), (pssb,))
            MM(psq[:, :N], ones, z2[:, :N], c == 0, c == 7, (z2b,), (psqb,))
        mean, meanb = tmp.next()
        msq, msqb = tmp.next()
        var, varb = tmp.next()
        TS1("dve", mean[:, :N], pss[:, :N], 1.0 / D, ALU.mult, (pssb,), (meanb,))
        TT("dve", msq[:, :N], mean[:, :N], mean[:, :N], ALU.mult, (meanb,), (msqb,))
        STT("dve", var[:, :N], psq[:, :N], 1.0 / D, msq[:, :N], ALU.mult, ALU.subtract, (psqb, msqb), (varb,))
        ACT(var[:, :N], var[:, :N], AF.Sqrt, (varb,), (varb,), bias=epsT[:, 0:1], scale=1.0)
        lr, lrb, ln_, lnb_ = T["lnrstd"], T["lnrstdb"], T["lnnmr"], T["lnnmrb"]
        RCP(lr[:, :N], var[:, :N], (varb,), (lrb,))
        STT("dve", ln_[:, :N], mean[:, :N], -1.0, lr[:, :N], ALU.mult, ALU.mult, (meanb, lrb), (lnb_,))
        for g in range(4):
            wg, wgb = yield wreq(C_G + 1024 + 256 * g)
            for jj in range(2):
                j = 2 * g + jj
                psg, pgb = proj(T, wg, wgb, jj, xn, xnb)
                t1, t1b = T["c1"][jj]
                ACT(t1[:, :N], psg[:, :N], AF.Tanh, (pgb,), (t1b,),
                    bias=hgb[:, l * 32 + 8 + j:l * 32 + 8 + j + 1], scale=0.5)
            for jj in range(2):
                j = 2 * g + jj
                t1, t1b = T["c1"][jj]
                v1, v1b = tmp.next()
                w2, w2b = tmp.next()
                STT("dve", v1[:, :N], z[:, j, :N], P("lng", l * 8 + j), lr[:, :N], ALU.mult, ALU.mult,
                    (zb[j], lrb), (v1b,))
                TS2("dve", w2[:, :N], ln_[:, :N], P("lng", l * 8 + j), P("lnb", l * 8 + j), ALU.mult, ALU.add,
                    (lnb_,), (w2b,))
                TT("dve", v1[:, :N], v1[:, :N], w2[:, :N], ALU.add, (v1b, w2b), (v1b,))
                yb, ybb = tmp.next()
                ACT(yb[:, :N], v1[:, :N], AF.Silu, (v1b,), (ybb,))
                if sample:
                    STT("dve", m2[:, j, :N], t1[:, :N], 1.0, yb[:, :N], ALU.add, ALU.mult, (t1b, ybb), (m2b[j],))
                else:
                    STT("dve", yb[:, :N], t1[:, :N], 1.0, yb[:, :N], ALU.add, ALU.mult, (t1b, ybb), (ybb,))
                    TT("dve", m2[:, j, :N], m2[:, j, :N], yb[:, :N], ALU.add, (m2b[j], ybb), (m2b[j],))
        stop("LN")

        if sample:
            DMA("sp", T["sast"], dr["sa"][l], (), (T["sastb"],), "i_sa")
        for g in range(4):
            wh, whb = yield wreq(C_H + 256 * g)
            wc, wcb = yield wreq(C_C + 256 * g)
            for jj in range(2):
                j = 2 * g + jj
                caw = prm[:, PRM["caw"] + (l * 8 + j) * 3:PRM["caw"] + (l * 8 + j) * 3 + 3]
                psh, phb = proj(T, wh, whb, jj, xn, xnb)
                psc, pcb = proj(T, wc, wcb, jj, xn, xnb)
                hs, hsb = tmp.next()
                ACT(hs[:, :N], psh[:, :N], AF.Copy, (phb,), (hsb,))
                c1, c1b = T["c1"][jj]
                if not sample:
                    ub, ubb = T["ub"].next()
                    CP("pool", ub[:, 0:2], halo_a[l][:, j, :], (halo_ab[l][j],), (ubb,))
                    TT("dve", ub[:, 2:2 + N], psc[:, :N], hs[:, :N], ALU.mult, (pcb, hsb), (ubb,))
                    TS1("dve", c1[:, :N], ub[:, 0:N], caw[:, 0:1], ALU.mult, (ubb,), (c1b,))
                    STT("dve", c1[:, :N], ub[:, 1:N + 1], caw[:, 1:2], c1[:, :N], ALU.mult, ALU.add, (ubb, c1b), (c1b,))
                    STT("dve", c1[:, :N], ub[:, 2:N + 2], caw[:, 2:3], c1[:, :N], ALU.mult, ALU.add, (ubb, c1b), (c1b,))
                    CP("pool", halo_a[l][:, j, :], ub[:, N:N + 2], (ubb,), (halo_ab[l][j],))
                else:
                    us, usb = tmp.next()
                    TT("dve", us[:, :N], psc[:, :N], hs[:, :N], ALU.mult, (pcb, hsb), (usb,))
                    sast = T["sast"]
                    TS1("dve", c1[:, :N], sast[:, j, :, 0], caw[:, 0:1], ALU.mult, (T["sastb"],), (c1b,))
                    STT("dve", c1[:, :N], sast[:, j, :, 1], caw[:, 1:2], c1[:, :N], ALU.mult, ALU.add,
                        (T["sastb"], c1b), (c1b,))
                    STT("dve", c1[:, :N], us[:, :N], caw[:, 2:3], c1[:, :N], ALU.mult, ALU.add, (usb, c1b), (c1b,))
                    CP("pool", T["nsa"][:, j, :, 0], sast[:, j, :, 1], (T["sastb"],), (T["nsab"],))
                    CP("pool", T["nsa"][:, j, :, 1], us[:, :N], (usb,), (T["nsab"],))
            wbw, wbwb = yield wreq(C_B + 256 * g)
            wg0, wg0b = yield wreq(C_G + 256 * g)
            for jj in range(2):
                j = 2 * g + jj
                c1, c1b = T["c1"][jj]
                psb2, pb2b = proj(T, wbw, wbwb, jj, xn, xnb)
                psg, pgb = proj(T, wg0, wg0b, jj, xn, xnb)
                t0, t0b = gate_tanh(psg, pgb, 0, j)
                ya, yab = tmp.next()
                STT("dve", ya[:, :N], t0[:, :N], 1.0, c1[:, :N], ALU.add, ALU.mult, (t0b, c1b), (yab,))
                TT("dve", ya[:, :N], psb2[:, :N], ya[:, :N], ALU.mult, (pb2b, yab), (yab,))
                TT("dve", m2[:, j, :N], m2[:, j, :N], ya[:, :N], ALU.add, (m2b[j], yab), (m2b[j],))
        if sample:
            DMA("sp", dr["cas"][l], T["nsa"], (T["nsab"],), (), "o_sa")
        stop("A")

        for g in range(4):
            wp, wpb = yield wreq(C_P + 256 * g)
            wg2, wg2b = yield wreq(C_G + 2048 + 256 * g)
            w = WINS[g]
            if sample:
                for jj in range(2):
                    spin, spinb, ch = T["sp_in"][jj]
                    DMA("sp", spin, dr["sp"][l][:, 2 * g + jj], (), (spinb,), ch)
            for jj in range(2):
                j = 2 * g + jj
                psp, ppb = proj(T, wp, wpb, jj, xn, xnb)
                pl, plb = T["plbf"][jj]
                if not sample:
                    pbuf, pbb_ = T["pb"].next()
                    CP("pool", pbuf[:, 0:15], halo_p[l][:, j, :], (halo_pb[l][j],), (pbb_,))
                    ACT(pbuf[:, 15:15 + N], psp[:, :N], AF.Copy, (ppb,), (pbb_,))
                    cur, curb, ln, e0 = pbuf, pbb_, 15 + N, 0
                    d = 1
                    nxt = [T["sA"], T["sB"]]
                    ni = 0
                    while 2 * d <= w:
                        dst, dstb = nxt[ni % 2]
                        ni += 1
                        TT("dve", dst[:, 0:ln - d], cur[:, d:ln], cur[:, 0:ln - d], ALU.add, (curb,), (dstb,))
                        cur, curb, ln, e0 = dst, dstb, ln - d, e0 + d
                        d *= 2
                    o = 15 - e0
                    STT("dve", pl[:, :N], cur[:, o:o + N], 1.0 / w, pbuf[:, 15:15 + N], ALU.mult, ALU.subtract,
                        (curb, pbb_), (plb,))
                    if ti == 0:
                        tf, tfb = tmp.next()
                        TT("dve", tf[:, 0:16], cur[:, o:o + 16], invc[:, g, :], ALU.mult, (curb,), (tfb,))
                        TT("dve", pl[:, 0:16], tf[:, 0:16], pbuf[:, 15:31], ALU.subtract, (tfb, pbb_), (plb,))
                    CP("pool", halo_p[l][:, j, :], pbuf[:, N:N + 15], (pbb_,), (halo_pb[l][j],))
                else:
                    spin, spinb, _ = T["sp_in"][jj]
                    spo, spob, cho = T["sp_out"][jj]
                    pn, pnb = tmp.next()
                    ACT(pn[:, :N], psp[:, :N], AF.Copy, (ppb,), (pnb,))
                    rd, rdb = tmp.next()
                    RED(rd[:, :N], spin[:, :, 16 - w:15], (spinb,), (rdb,))
                    TT("dve", rd[:, :N], rd[:, :N], pn[:, :N], ALU.add, (rdb, pnb), (rdb,))
                    STT("dve", pl[:, :N], rd[:, :N], 1.0 / w, pn[:, :N], ALU.mult, ALU.subtract, (rdb, pnb), (plb,))
                    CP("pool", spo[:, :, 0:14], spin[:, :, 1:15], (spinb,), (spob,))
                    CP("pool", spo[:, :, 14], pn[:, :N], (pnb,), (spob,))
                    DMA("sp", dr["pps"][l][:, j], spo, (spob,), (), cho)
            for jj in range(2):
                j = 2 * g + jj
                psg, pgb = proj(T, wg2, wg2b, jj, xn, xnb)
                t2, t2b = T["c1"][jj]
                ACT(t2[:, :N], psg[:, :N], AF.Tanh, (pgb,), (t2b,),
                    bias=hgb[:, l * 32 + 16 + j:l * 32 + 16 + j + 1], scale=0.5)
            for jj in range(2):
                j = 2 * g + jj
                t2, t2b = T["c1"][jj]
                psc2, pc2b = T["auxrot"].next()
                for kc in range(2):
                    MM(psc2[:, :N], pw[:, g * 2 + kc, jj * 128:(jj + 1) * 128], T["plbf"][kc][0][:, :N],
                       kc == 0, kc == 1, (T["plbf"][kc][1], pwb), (pc2b,))
                yc, ycb = tmp.next()
                STT("dve", yc[:, :N], t2[:, :N], 1.0, psc2[:, :N], ALU.add, ALU.mult, (t2b, pc2b), (ycb,))
                if sample:
                    STT("dve", m2[:, j, :N], yc[:, :N], P("psc", l * 8 + j), m2[:, j, :N], ALU.mult, ALU.add,
                        (ycb, m2b[j]), (m2b[j],))
                else:
                    STT("dve", mb[:, j, :N], yc[:, :N], P("psc", l * 8 + j), m2[:, j, :N], ALU.mult, ALU.add,
                        (ycb, m2b[j]), (mbb[j],))
        if sample:
            yield ("join",)
            yms, ymsb = T["yms"]
            for j in range(8):
                TT("dve", mb[:, j, :N], m2[:, j, :N], yms[:, j, :], ALU.add, (m2b[j], ymsb), (mbb[j],))
        stop("C")

        for mi in range(4):
            wo, wob = yield ("w", "o%d_%d" % (l, mi), dr["w_o"][l][:, :, 256 * mi:256 * mi + 256])
            for jj in range(2):
                j = 2 * mi + jj
                ps, pb = proj(T, wo, wob, jj, mb, mbb)
                STT("dve", x[:, j, :N], ps[:, :N], 0.5, x[:, j, :N], ALU.mult, ALU.add, (pb, xb[j]), (xb[j],))
        stop("WO")

        rmsnorm(x, xb, N, "nffn", l * 8, xn, xnb, T["sq"], T["sd"], T["sdb"], T["rstd"], T["rstdb"], rot=T["auxrot"])
        for half in range(2):
            for s8 in range(8):
                s = half * 8 + s8
                w1, w1b = yield ("w", "f1_%d_%d" % (l, s), dr["w_ff1"][l][:, :, 256 * s:256 * s + 256])
                for jj in range(2):
                    ps, pb = proj(T, w1, w1b, jj, xn, xnb)
                    r, rb = T["bft"].next()
                    ACT(r[:, :N], ps[:, :N], AF.Relu, (pb,), (rb,))
                    hi = s8 * 2 + jj
                    TT("dve", h[:, hi, :N], r[:, :N], r[:, :N], ALU.mult, (rb,), (hb[hi],))
            for mi in range(4):
                if sample:
                    for kgl in range(2):
                        r0 = half * 16 + kgl * 8
                        w2_, w2b_ = yield ("w", "f2_%d_%d_%d" % (l, r0, mi),
                                           dr["w_ff2"][l][:, r0:r0 + 8, 256 * mi:256 * mi + 256])
                        for jj in range(2):
                            j = 2 * mi + jj
                            ps, pb = T["mmrot"].next()
                            for k in range(8):
                                MM(ps[:, :N], w2_[:, k, jj * 128:(jj + 1) * 128], h[:, kgl * 8 + k, :N],
                                   k == 0, k == 7, (w2b_, hb[kgl * 8 + k]), (pb,))
                            TT("dve", x[:, j, :N], x[:, j, :N], ps[:, :N], ALU.add, (xb[j], pb), (xb[j],))
                    continue
                pss2 = [T["mmrot"].next(), T["mmrot"].next()]
                for kgl in range(2):
                    r0 = half * 16 + kgl * 8
                    w2_, w2b_ = yield ("w", "f2_%d_%d_%d" % (l, r0, mi),
                                       dr["w_ff2"][l][:, r0:r0 + 8, 256 * mi:256 * mi + 256])
                    for jj in range(2):
                        ps, pb = pss2[jj]
                        for k in range(8):
                            MM(ps[:, :N], w2_[:, k, jj * 128:(jj + 1) * 128], h[:, kgl * 8 + k, :N],
                               kgl == 0 and k == 0, kgl == 1 and k == 7, (w2b_, hb[kgl * 8 + k]), (pb,))
                for jj in range(2):
                    j = 2 * mi + jj
                    ps, pb = pss2[jj]
                    TT("dve", x[:, j, :N], x[:, j, :N], ps[:, :N], ALU.add, (xb[j], pb), (xb[j],))

    def run_layers(ctxs):
        l_ = ctxs[0][1]
        DMA("pool", pw, dr["pw"][:, l_ * 8:(l_ + 1) * 8, :], (), (pwb,), "pw")
        gens = [layer(*c) for c in ctxs]
        reqs = [None] * len(gens)
        done = [False] * len(gens)
        bg = []

        def advance(i, val):
            try:
                reqs[i] = gens[i].send(val) if val is not None or reqs[i] is not None else next(gens[i])
            except StopIteration:
                done[i] = True
                reqs[i] = None

        def bg_step(n):
            for _ in range(n):
                if not bg:
                    return
                b = bg[0]
                if b[1] is None:
                    try:
                        b[1] = next(b[0])
                    except StopIteration:
                        bg.pop(0)
                        continue
                W.la = 2
                slot = W.next(b[1][2])
                try:
                    b[1] = b[0].send(slot)
                except StopIteration:
                    bg.pop(0)

        for i in range(len(gens)):
            try:
                reqs[i] = next(gens[i])
            except StopIteration:
                done[i] = True
        while not all(done):
            progressed = False
            for i in range(len(gens)):
                if done[i]:
                    continue
                r = reqs[i]
                if r[0] == "spawn":
                    bg.append([r[1], None])
                    advance(i, 0)
                    progressed = True
                elif r[0] == "join":
                    while bg:
                        bg_step(1)
                    advance(i, 0)
                    progressed = True
            if progressed:
                continue
            active = [i for i in range(len(gens)) if not done[i]]
            keys = set(reqs[i][1] for i in active)
            assert len(keys) == 1, keys
            W.la = 2 if bg else 1
            slot = W.next(reqs[active[0]][2])
            for i in active:
                advance(i, slot)
            bg_step(1)
        while bg:
            bg_step(1)

    A.off = PHASE_BASE
    Tp = make_ctx(NT, False)
    Ts = make_ctx(NS, True)
    for ti in range(4):
        Tp["mmrot"], Tp["auxrot"] = (mm_rot, aux_rot) if ti == 3 else (mm_rot4, aux_rot4)
        W.la = 1
        DMA("sp", Tp["x"], dr["xT"][:, :, ti * NT:(ti + 1) * NT], (), Tp["xb"], "x")
        if ti == 3:
            DMA("sp", Ts["x"], dr["xsT"], (), Ts["xb"], "xs")
        for l in range(2):
            if ti == 3:
                run_layers([(Tp, l, ti), (Ts, l, 4)])
            else:
                run_layers([(Tp, l, ti)])
            stop("L")
        rmsnorm(Tp["x"], Tp["xb"], NT, "nfin", 0, Tp["m2"], Tp["m2b"], Tp["sq"], Tp["sd"], Tp["sdb"], Tp["rstd"],
                Tp["rstdb"])
        DMA("sp", dr["yT"][:, :, ti * NT:(ti + 1) * NT], Tp["m2"], Tp["m2b"], (), "y")
    for l in range(2):
        TS1("dve", cbp_r[l], cbp_r[l], 0.5, ALU.mult, (cbp_b[l],), (cbp_b[l],))
        DMA("sp", dr["cbp"][l], cbp_r[l], (cbp_b[l],), (), "o_cbp%d" % l)
        DMA("sp", dr["cap"][l], halo_a[l], halo_ab[l], (), "o_cap%d" % l)
        DMA("sp", dr["pp"][l], halo_p[l], halo_pb[l], (), "o_pp%d" % l)
    rmsnorm(Ts["x"], Ts["xb"], NS, "nfin", 0, Ts["m2"], Ts["m2b"], Ts["sq"], Ts["sd"], Ts["sdb"], Ts["rstd"],
            Ts["rstdb"], rot=s_rot)
    DMA("sp", dr["ysT"], Ts["m2"], Ts["m2b"], (), "ys")
    S.barrier()


def _fm(a):
    lead = a.shape[:-1]
    r = a.reshape(lead + (8, 128))
    return np.moveaxis(r, -1, 0)


def _pack_params(norm_mix, norm_mem, norm_ffn, norm_final, conv_a_w, conv_b_w, conv_b_bias, ln_b_gain, ln_b_bias,
                 pool_scale, gate_bias):
    prm = np.zeros((128, NPRM), np.float32)

    def put(name, arr):
        flat = np.ascontiguousarray(arr).reshape(128, -1)
        prm[:, PRM[name]:PRM[name] + flat.shape[1]] = flat

    put("nmix", _fm(norm_mix))
    put("nmem", _fm(norm_mem))
    put("nffn", _fm(norm_ffn))
    put("nfin", _fm(norm_final))
    put("caw", np.transpose(_fm(conv_a_w), (0, 1, 3, 2)))
    put("cbw", np.transpose(_fm(conv_b_w), (0, 1, 3, 2)))
    put("cbb", _fm(conv_b_bias))
    put("lng", _fm(ln_b_gain))
    put("lnb", _fm(ln_b_bias))
    put("psc", _fm(pool_scale))
    put("gb", np.transpose(gate_bias.reshape(2, 32, 128), (2, 0, 1)))
    return prm


def _wr(w, kc):
    L, K, E = w.shape
    return np.ascontiguousarray(np.transpose(w.reshape(L, kc, 128, E), (0, 2, 1, 3)))


def _make_in_maps(A_):
    (x_prompt, x_sample, mem_prompt, cache_mem_k, cache_mem_v, state_conv_a, state_conv_b, state_pool, norm_mix,
     norm_mem, w_kv, w_in, conv_a_w, conv_b_w, conv_b_bias, ln_b_gain, ln_b_bias, pool_w, pool_scale, gate_bias,
     w_o, norm_ffn, w_ff1, w_ff2, norm_final) = A_
    n = 8
    prm = _pack_params(norm_mix, norm_mem, norm_ffn, norm_final, conv_a_w, conv_b_w, conv_b_bias, ln_b_gain,
                       ln_b_bias, pool_scale, gate_bias)
    shared = dict(
        w_in=_wr(w_in, 8), w_kv=_wr(w_kv, 8), w_o=_wr(w_o, 8), w_ff1=_wr(w_ff1, 8), w_ff2=_wr(w_ff2, 32),
        pw=np.ascontiguousarray(np.transpose(pool_w.reshape(2, 4, 2, 128, 256), (3, 0, 1, 2, 4)).reshape(128, 16, 256)),
        prm=prm, ident=np.eye(128, dtype=np.float32),
        invc=np.ascontiguousarray(np.broadcast_to(
            np.array([[1.0 / min(t + 1, w) for t in range(16)] for w in WINS], np.float32)[None], (128, 4, 16))),
    )
    in_maps = []
    for i in range(n):
        sl = slice(NS * i, NS * (i + 1))
        m = dict(shared)
        m["xT"] = np.ascontiguousarray(_fm(x_prompt[i]))
        m["xT"] = np.ascontiguousarray(np.transpose(m["xT"], (0, 2, 1)))
        m["xsT"] = np.ascontiguousarray(np.transpose(_fm(x_sample[sl, 0, :]), (0, 2, 1)))
        m["memT"] = np.ascontiguousarray(np.transpose(_fm(mem_prompt[i]), (0, 2, 1)))
        k = cache_mem_k[:, sl].reshape(2, NS, 256, 8, 128)
        m["kT"] = np.ascontiguousarray(np.transpose(k, (0, 1, 4, 3, 2)))
        v = cache_mem_v[:, sl].reshape(2, NS, 2, 128, 1024)
        m["vv"] = np.ascontiguousarray(np.transpose(v, (0, 1, 3, 2, 4)))
        for nm, stt in (("sa", state_conv_a), ("sb", state_conv_b), ("sp", state_pool)):
            s_ = stt[:, sl]
            r = s_.reshape(2, NS, s_.shape[2], 8, 128)
            m[nm] = np.ascontiguousarray(np.transpose(r, (0, 4, 3, 1, 2)))
        in_maps.append(m)

    return in_maps


_NC_CACHE = {}


def kernel(x_prompt, x_sample, mem_prompt, cache_mem_k, cache_mem_v, state_conv_a, state_conv_b, state_pool,
           norm_mix, norm_mem, w_kv, w_in, conv_a_w, conv_b_w, conv_b_bias, ln_b_gain, ln_b_bias, pool_w,
           pool_scale, gate_bias, w_o, norm_ffn, w_ff1, w_ff2, norm_final):
    f = lambda a: np.asarray(a, dtype=np.float32)
    (x_prompt, x_sample, mem_prompt, cache_mem_k, cache_mem_v, state_conv_a, state_conv_b, state_pool, norm_mix,
     norm_mem, w_kv, w_in, conv_a_w, conv_b_w, conv_b_bias, ln_b_gain, ln_b_bias, pool_w, pool_scale, gate_bias,
     w_o, norm_ffn, w_ff1, w_ff2, norm_final) = map(f, (
        x_prompt, x_sample, mem_prompt, cache_mem_k, cache_mem_v, state_conv_a, state_conv_b, state_pool, norm_mix,
        norm_mem, w_kv, w_in, conv_a_w, conv_b_w, conv_b_bias, ln_b_gain, ln_b_bias, pool_w, pool_scale, gate_bias,
        w_o, norm_ffn, w_ff1, w_ff2, norm_final))
    n = 8
    in_maps = _make_in_maps((x_prompt, x_sample, mem_prompt, cache_mem_k, cache_mem_v, state_conv_a, state_conv_b,
                             state_pool, norm_mix, norm_mem, w_kv, w_in, conv_a_w, conv_b_w, conv_b_bias, ln_b_gain,
                             ln_b_bias, pool_w, pool_scale, gate_bias, w_o, norm_ffn, w_ff1, w_ff2, norm_final))
    if "nc" not in _NC_CACHE:
        _NC_CACHE["nc"] = build_program()
    nc = _NC_CACHE["nc"]
    res = run_bass_kernel_spmd(nc, in_maps, core_ids=list(range(n)))
    return _assemble(res.results)


def _assemble(R):
    n = 8

    def unfm(a):
        return np.transpose(a, (2, 1, 0)).reshape(a.shape[2], 1024)

    y_prompt = np.stack([unfm(R[i]["yT"]) for i in range(n)])
    y_sample = np.concatenate([unfm(R[i]["ysT"]) for i in range(n)])[:, None, :]
    mem_k = np.stack([np.stack([unfm(R[i]["mk"][l]).reshape(256, 4, 256) for i in range(n)]) for l in range(2)])
    mem_v = np.stack([np.stack([R[i]["mv"][l].reshape(256, 4, 256) for i in range(n)]) for l in range(2)])

    def pst(name):
        return np.stack([np.stack([unfm(R[i][name][l]) for i in range(n)]) for l in range(2)])

    def sst(name):
        outs = []
        for l in range(2):
            per = []
            for i in range(n):
                a = R[i][name][l]
                per.append(np.transpose(a, (2, 3, 1, 0)).reshape(NS, a.shape[3], 1024))
            outs.append(np.concatenate(per))
        return np.stack(outs)

    outs = (y_prompt, y_sample, mem_k, mem_v, pst("cap"), pst("cbp"), pst("pp"), sst("cas"), sst("cbs"), sst("pps"))
    return tuple(np.ascontiguousarray(o, dtype=np.float32) for o in outs)
```
